# Optimizing a Trainium2 kernel written in Bass

```python
import math
import jax, jax.numpy as jnp
from jax import lax
import numpy as np

D_MODEL = 1024
BATCH = 4
SEQ = 8192
DEPTH = 1

MEM_LEN = 256
HEAD_DIM = 64
MIX_W = D_MODEL
ATTN_W = MIX_W // 2
N_Q_HEADS = ATTN_W // HEAD_DIM
N_KV_HEADS = N_Q_HEADS // 4
GQA_GROUP = N_Q_HEADS // N_KV_HEADS
KV_W = N_KV_HEADS * HEAD_DIM
GM_W = MIX_W - ATTN_W
GM_HEADS = GM_W // HEAD_DIM
GM_DH = GM_W // GM_HEADS
IN_COLS = ATTN_W + 2 * KV_W + 2 * GM_W
WINDOW = 128
BLK = 128
CHUNK = 128
ROPE_THETA = 10000.0
XA_HEADS = 4
XA_DH = D_MODEL // XA_HEADS
D_FF = ((8 * D_MODEL // 3 + 127) // 128) * 128
CONV_W = 3
MAX_POS_OFFSET = 1024
EPS = 1e-6

kernel_name = "hybrid_swa_gmlp_xattn_convffn"


def rms_norm(x, g):
    xf = x.astype(jnp.float32)
    y = xf * lax.rsqrt(jnp.mean(xf * xf, axis=-1, keepdims=True) + EPS)
    return (y * g.astype(jnp.float32)).astype(x.dtype)


def rope(x, positions):
    dh = x.shape[-1]
    half = dh // 2
    inv_freq = 1.0 / (ROPE_THETA ** (jnp.arange(half, dtype=jnp.float32) * (2.0 / dh)))
    ang = positions.astype(jnp.float32)[..., None] * inv_freq
    cos = jnp.cos(ang)[:, :, None, :]
    sin = jnp.sin(ang)[:, :, None, :]
    xf = x.astype(jnp.float32)
    x1, x2 = xf[..., :half], xf[..., half:]
    out = jnp.concatenate([x1 * cos - x2 * sin, x2 * cos + x1 * sin], axis=-1)
    return out.astype(x.dtype)


def sliding_window_attn(q, k, v, sinks):
    B, S = q.shape[0], q.shape[1]
    nb = S // BLK
    qb = q.reshape(B, nb, BLK, N_KV_HEADS, GQA_GROUP, HEAD_DIM)
    kb = k.reshape(B, nb, BLK, N_KV_HEADS, HEAD_DIM)
    vb = v.reshape(B, nb, BLK, N_KV_HEADS, HEAD_DIM)
    pad = ((0, 0), (1, 0), (0, 0), (0, 0), (0, 0))
    kk = jnp.concatenate([jnp.pad(kb[:, :-1], pad), kb], axis=2)
    vv = jnp.concatenate([jnp.pad(vb[:, :-1], pad), vb], axis=2)
    scores = jnp.einsum('bnqhgd,bnkhd->bnhgqk', qb, kk).astype(jnp.float32)
    scores = scores * (1.0 / math.sqrt(HEAD_DIM))
    qi = jnp.arange(BLK)[:, None]
    kj = jnp.arange(2 * BLK)[None, :]
    diff = qi + BLK - kj
    band = (diff >= 0) & (diff < WINDOW)
    valid = (jnp.arange(nb)[:, None, None] > 0) | (kj >= BLK)[None]
    mask = (band[None] & valid)[None, :, None, None]
    scores = jnp.where(mask, scores, jnp.finfo(jnp.float32).min)
    sink = sinks.astype(jnp.float32).reshape(N_KV_HEADS, GQA_GROUP)[None, None, :, :, None, None]
    sink = jnp.broadcast_to(sink, scores.shape[:-1] + (1,))
    probs = jax.nn.softmax(jnp.concatenate([scores, sink], axis=-1), axis=-1)[..., :-1]
    out = jnp.einsum('bnhgqk,bnkhd->bnqhgd', probs.astype(v.dtype), vv)
    return out.reshape(B, S, ATTN_W)


def chunked_spatial_gating(u, v, ws, bs):
    B, S = v.shape[0], v.shape[1]
    nc = S // CHUNK
    vb = v.reshape(B, nc, CHUNK, GM_HEADS, GM_DH)
    causal = jnp.tril(jnp.ones((CHUNK, CHUNK), dtype=ws.dtype))
    mixed = jnp.einsum('hts,bnshd->bnthd', ws * causal[None], vb)
    mixed = mixed + bs.T[None, None, :, :, None]
    return u * mixed.reshape(B, S, GM_W)


def parallel_mixer(x, positions, mix_norm, w_in, q_norm, k_norm, attn_sinks,
                   gmlp_v_norm, gmlp_ws, gmlp_bs, attn_out_norm, gmlp_out_norm, w_out):
    B, S, _ = x.shape
    h = rms_norm(x, mix_norm)
    proj = h @ w_in
    q, k, v, gz = jnp.split(proj, [ATTN_W, ATTN_W + KV_W, ATTN_W + 2 * KV_W], axis=-1)
    q = rope(rms_norm(q.reshape(B, S, N_Q_HEADS, HEAD_DIM), q_norm), positions)
    k = rope(rms_norm(k.reshape(B, S, N_KV_HEADS, HEAD_DIM), k_norm), positions)
    v = v.reshape(B, S, N_KV_HEADS, HEAD_DIM)
    attn = sliding_window_attn(q, k, v, attn_sinks)
    gz = jax.nn.gelu(gz)
    gu, gv = jnp.split(gz, 2, axis=-1)
    gm = chunked_spatial_gating(gu, rms_norm(gv, gmlp_v_norm), gmlp_ws, gmlp_bs)
    y = jnp.concatenate([rms_norm(attn, attn_out_norm), rms_norm(gm, gmlp_out_norm)], axis=-1)
    return y @ w_out


def memory_cross_attn(x, mem, xa_norm, mem_norm, xa_wq, xa_wkv, xa_q_norm, xa_k_norm, xa_wo):
    B, S, _ = x.shape
    M = mem.shape[1]
    h = rms_norm(x, xa_norm)
    m = rms_norm(mem, mem_norm)
    q = rms_norm((h @ xa_wq).reshape(B, S, XA_HEADS, XA_DH), xa_q_norm)
    k, v = jnp.split(m @ xa_wkv, 2, axis=-1)
    k = rms_norm(k.reshape(B, M, XA_HEADS, XA_DH), xa_k_norm)
    v = v.reshape(B, M, XA_HEADS, XA_DH)
    scores = jnp.einsum('bshd,bmhd->bhsm', q, k).astype(jnp.float32) * (1.0 / math.sqrt(XA_DH))
    probs = jax.nn.softmax(scores, axis=-1).astype(v.dtype)
    out = jnp.einsum('bhsm,bmhd->bshd', probs, v).reshape(B, S, XA_HEADS * XA_DH)
    return out @ xa_wo


def conv_gated_ffn(x, ffn_norm, ffn_up, ffn_conv, ffn_conv_b, ffn_down):
    h = rms_norm(x, ffn_norm)
    a = h @ ffn_up
    c = lax.conv_general_dilated(
        a, ffn_conv.reshape(CONV_W, 1, 2 * D_FF).astype(a.dtype),
        window_strides=(1,), padding=[(CONV_W - 1, 0)],
        dimension_numbers=('NWC', 'WIO', 'NWC'),
        feature_group_count=2 * D_FF) + ffn_conv_b
    gate, up = jnp.split(c, 2, axis=-1)
    return (jax.nn.gelu(gate) * up) @ ffn_down


def setup_inputs(seed: int = 0) -> dict:
    key = jax.random.key(seed)
    ks = iter(jax.random.split(key, 40))
    L = DEPTH

    def nrm(shape, scale):
        return jax.random.normal(next(ks), shape, jnp.float32) * scale

    def gain(shape):
        return 1.0 + 0.02 * jax.random.normal(next(ks), shape, jnp.float32)

    x = nrm((BATCH, SEQ, D_MODEL), 1.0)
    mem = nrm((BATCH, MEM_LEN, D_MODEL), 1.0)
    offs = jax.random.randint(next(ks), (BATCH, 1), 0, MAX_POS_OFFSET, dtype=jnp.int32)
    positions = (offs + jnp.arange(SEQ, dtype=jnp.int32)[None, :]).astype(jnp.int32)
    return {
        "x": x,
        "mem": mem,
        "positions": positions,
        "mix_norm": gain((L, D_MODEL)),
        "w_in": nrm((L, D_MODEL, IN_COLS), D_MODEL ** -0.5),
        "q_norm": gain((L, HEAD_DIM)),
        "k_norm": gain((L, HEAD_DIM)),
        "attn_sinks": nrm((L, N_Q_HEADS), 0.5),
        "gmlp_v_norm": gain((L, GM_W)),
        "gmlp_ws": nrm((L, GM_HEADS, CHUNK, CHUNK), 0.5 * CHUNK ** -0.5),
        "gmlp_bs": 1.0 + nrm((L, GM_HEADS, CHUNK), 0.02),
        "attn_out_norm": gain((L, ATTN_W)),
        "gmlp_out_norm": gain((L, GM_W)),
        "w_out": nrm((L, MIX_W, D_MODEL), MIX_W ** -0.5),
        "xa_norm": gain((L, D_MODEL)),
        "mem_norm": gain((L, D_MODEL)),
        "xa_wq": nrm((L, D_MODEL, XA_HEADS * XA_DH), D_MODEL ** -0.5),
        "xa_wkv": nrm((L, D_MODEL, 2 * XA_HEADS * XA_DH), D_MODEL ** -0.5),
        "xa_q_norm": gain((L, XA_DH)),
        "xa_k_norm": gain((L, XA_DH)),
        "xa_wo": nrm((L, XA_HEADS * XA_DH, D_MODEL), (XA_HEADS * XA_DH) ** -0.5),
        "ffn_norm": gain((L, D_MODEL)),
        "ffn_up": nrm((L, D_MODEL, 2 * D_FF), D_MODEL ** -0.5),
        "ffn_conv": nrm((L, CONV_W, 2 * D_FF), CONV_W ** -0.5),
        "ffn_conv_b": nrm((L, 2 * D_FF), 0.02),
        "ffn_down": nrm((L, D_FF, D_MODEL), D_FF ** -0.5),
    }


def reference(x, mem, positions, mix_norm, w_in, q_norm, k_norm, attn_sinks,
              gmlp_v_norm, gmlp_ws, gmlp_bs, attn_out_norm, gmlp_out_norm, w_out,
              xa_norm, mem_norm, xa_wq, xa_wkv, xa_q_norm, xa_k_norm, xa_wo,
              ffn_norm, ffn_up, ffn_conv, ffn_conv_b, ffn_down):
    for l in range(DEPTH):
        x = x + parallel_mixer(x, positions, mix_norm[l], w_in[l], q_norm[l], k_norm[l],
                               attn_sinks[l], gmlp_v_norm[l], gmlp_ws[l], gmlp_bs[l],
                               attn_out_norm[l], gmlp_out_norm[l], w_out[l])
        x = x + memory_cross_attn(x, mem, xa_norm[l], mem_norm[l], xa_wq[l], xa_wkv[l],
                                  xa_q_norm[l], xa_k_norm[l], xa_wo[l])
        x = x + conv_gated_ffn(x, ffn_norm[l], ffn_up[l], ffn_conv[l], ffn_conv_b[l],
                               ffn_down[l])
    return x
```

```python
import math
import contextlib
import numpy as np
import concourse.bass as bass
import concourse.mybir as mybir
from concourse.bass_utils import run_bass_kernel_spmd

F32 = mybir.dt.float32
BF16 = mybir.dt.bfloat16
I32 = mybir.dt.int32
AF = mybir.ActivationFunctionType
ALU = mybir.AluOpType

D = 1024
KC = 8
SEQ = 8192
BATCH = 4
NCORES = 8
TOK_CORE = 4096
HALO = 256
NBM = 2
NTM = NBM * 128
NSUB_MAIN = TOK_CORE // NTM
DFF = 2816
NJ = 22
EPS = 1e-6
NS = 8
POOL_TT = "dve"
QK_LAG = 1
ATT_LAG = 2
MASK_ENG = "dve"
GM_ENG = "pool"
N_ACC = 5
A_FIRST = True
FFN_POOL = "pool"
PW = 2048

P_WIN, P_WOUT, P_WK, P_WV, P_WQ, P_WO, P_UP, P_DOWN = 0, 7, 11, 15, 19, 23, 27, 49
NPIECE = 65
CONV_LA = 4


def _cst_layout():
    off = {}
    n = 0
    for name, w in [("g1", 8), ("g2", 8), ("g3", 8), ("gmem", 8), ("gq", 1), ("gk", 1), ("gqp", 1), ("gkp", 1), ("invf", 1),
                    ("flag", 1), ("sel0", 1), ("sel1", 1), ("ga", 4), ("gmo", 4), ("gxq", 2), ("gxk", 2),
                    ("cw", 44 * 3), ("cb", 44), ("halfpi", 1), ("eps", 1), ("gvn", 512)]:
        off[name] = (n, n + w)
        n += w
    return off, n


CA, NCA = _cst_layout()


def _cstb_layout():
    off = {}
    n = 0
    for name, w in [("sinkr", 8), ("tril", 128), ("rm", 128), ("bones", 128)]:
        off[name] = (n, n + w)
        n += w
    return off, n


CB, NCB = _cstb_layout()


class Tk:
    __slots__ = ("name", "w", "r", "excl")

    def __init__(self, name, excl=False):
        self.name = name
        self.w = None
        self.r = {}
        self.excl = excl


class Eng:
    def __init__(self, name):
        self.name = name
        self.ops = []
        self.cnt = 0
        self.seen = {}


class Prog:
    def __init__(self):
        self.eng = {n: Eng(n) for n in ("pe", "act", "dve", "pool", "sp")}
        self.dmacnt = {}
        self.tag = ""
        self.tags = {n: [] for n in self.eng}

    def op(self, en, fn, reads=(), writes=(), signal=True, dma=None):
        e = self.eng[en]
        need = {}

        def req(ev):
            if ev is None:
                return
            k, v = ev
            if need.get(k, 0) < v:
                need[k] = v

        for b in reads:
            req(b.w)
            if b.excl:
                for k, v in b.r.items():
                    if k != en:
                        req((k, v))
        for b in writes:
            req(b.w)
            for k, v in b.r.items():
                req((k, v))
        waits = []
        for k, v in need.items():
            if k == "pe" and en == "pe":
                continue
            if e.seen.get(k, 0) < v:
                e.seen[k] = v
                waits.append((k, v))
        if dma is not None:
            self.dmacnt[dma] = self.dmacnt.get(dma, 0) + 16
            ev = (dma, self.dmacnt[dma])
            inc = (dma, 16)
        elif signal:
            e.cnt += 1
            ev = (en, e.cnt)
            inc = (en, 1)
        else:
            ev = (en, e.cnt + 1)
            inc = None
        e.ops.append((waits, fn, inc))
        self.tags[en].append(self.tag)
        for b in reads:
            if b.r.get(ev[0], 0) < ev[1]:
                b.r[ev[0]] = ev[1]
        for b in writes:
            b.w = ev
            b.r = {}
        return ev


class Sub:
    def __init__(self, si, nb, halo, first_real, tok0, pos0, slot0, idx):
        self.si = si
        self.nb = nb
        self.nt = nb * 128
        self.halo = halo
        self.first_real = first_real
        self.tok0 = tok0
        self.pos0 = pos0
        self.slot0 = slot0
        self.idx = idx


_CACHE = {}


class _Stop(Exception):
    pass


def build_program(n_main_sub=NSUB_MAIN, stop=None):
    nc = bass.Bass("TRN2", target_bir_lowering=False)
    P = Prog()
    dr = {}
    dr["xh"] = nc.dram_tensor("xh", [128, KC * HALO], F32, kind="ExternalInput").ap()
    dr["xm"] = nc.dram_tensor("xm", [128, KC * TOK_CORE], F32, kind="ExternalInput").ap()
    dr["posr"] = nc.dram_tensor("posr", [128, HALO + TOK_CORE], I32, kind="ExternalInput").ap()
    dr["memT"] = nc.dram_tensor("memT", [128, KC * 256], F32, kind="ExternalInput").ap()
    dr["wall"] = nc.dram_tensor("wall", [NPIECE, 128, PW], F32, kind="ExternalInput").ap()
    dr["csta"] = nc.dram_tensor("csta", [128, NCA], F32, kind="ExternalInput").ap()
    dr["cstb"] = nc.dram_tensor("cstb", [128, NCB], F32, kind="ExternalInput").ap()
    dr["c_mask"] = nc.dram_tensor("c_mask", [128, 1024], F32, kind="ExternalInput").ap()
    dr["c_wtf"] = nc.dram_tensor("c_wtf", [128, 1024], F32, kind="ExternalInput").ap()
    dr["c_bsr"] = nc.dram_tensor("c_bsr", [128, 512], F32, kind="ExternalInput").ap()
    dr["om"] = nc.dram_tensor("om", [128, KC * TOK_CORE], F32, kind="ExternalOutput").ap()
    wsc = nc.dram_tensor("wsc", [NPIECE, 128, PW], BF16).ap()

    es = contextlib.ExitStack()
    with es:
        def sb(name, shape, dt):
            return es.enter_context(nc.sbuf_tensor(name, shape, dt))

        ring = sb("ring", [128, NS, PW], BF16)
        csta = sb("csta_s", [128, NCA], F32)
        ps = es.enter_context(nc.psum_tensor("ps", [128, 8, 512], F32))
        ones = sb("ones", [128, 128], BF16)
        bones = sb("bones", [128, 128], BF16)
        rmat = sb("rmat", [128, 128], BF16)
        maskN = sb("maskN", [128, 1024], BF16)
        osb = [sb(f"osb{i}", [128, 256], F32) for i in range(2)]
        wtb = sb("wtb", [128, 8, 128], BF16)
        es_bc = sb("es_bc", [128, 2, 2, 128], F32)
        bias_bc = sb("bias_bc", [128, 4, 128], F32)
        e128 = sb("e128", [128, 8], F32)
        mxt = sb("mxt", [128, 4, 128], F32)
        hist = sb("hist", [128, KC, 2], BF16)
        kmT = sb("kmT", [128, 8, 256], BF16)
        vm = sb("vm", [128, 2, 1024], BF16)
        NSLOT = 1 + 2 * NBM
        KD = sb("KD", [128, 2, NSLOT * 128], BF16)
        VV = sb("VV", [128, NSLOT, 128], BF16)
        xT = [sb(f"xT{i}", [128, KC + 1, NTM], F32) for i in range(2)]
        hT = [sb(f"hT{i}", [128, KC, 2 + NTM], BF16) for i in range(2)]
        yT = [sb(f"yT{i}", [128, KC, NTM], BF16) for i in range(2)]
        sq_one = sb("sq", [128, KC, NTM], BF16)
        sq = [sq_one, sq_one]
        cs_t = [sb(f"cs{i}", [128, 2, NTM], F32) for i in range(2)]
        lnt = [sb(f"lnt{i}", [128, NTM], F32) for i in range(2)]
        rstd = [sb(f"rstd{i}", [128, NTM], F32) for i in range(2)]
        rstd2 = [sb(f"rstdb{i}", [128, NTM], F32) for i in range(2)]
        qn = [sb(f"qn{i}", [128, 4, NTM], BF16) for i in range(2)]
        guT = [sb(f"guT{i}", [128, 4, NTM], F32) for i in range(2)]
        gvraw = [sb(f"gvraw{i}", [128, NBM, 512], F32) for i in range(2)]
        gvn = [sb(f"gvn{i}", [128, NBM, 512], BF16) for i in range(2)]
        gvst = [sb(f"gvst{i}", [128, 8], F32) for i in range(2)]
        attnT = [sb(f"attnT{i}", [128, 4, NTM], F32) for i in range(2)]
        gmT = [sb(f"gmT{i}", [128, 4, NTM], F32) for i in range(2)]
        rtmp = attnT
        posi = [lnt[i][:].bitcast(I32) for i in range(2)]
        gT_all = sb("gT", [128, 2, NJ, NTM], BF16)
        NR = 3
        sqc = [sb(f"sqc{i}", [128, NTM], BF16) for i in range(NR)]
        rawb = [sb(f"rawb{i}", [128, NTM], BF16) for i in range(NR)]
        lnc = [sb(f"lnc{i}", [128, NTM], F32) for i in range(NR)]
        rsc = [sb(f"rsc{i}", [128, NTM], F32) for i in range(NR)]
        t1 = [sb(f"t1_{i}", [128, NTM], F32) for i in range(NR)]
        t2 = [sb(f"t2_{i}", [128, NTM], F32) for i in range(NR)]
        PT = [sb(f"PT{i}", [128, 2, 512], BF16) for i in range(3)]
        rec = [sb(f"rec{i}", [128, 256], F32) for i in range(2)]
        PTx = [sb(f"PTx{i}", [128, 2, NTM], BF16) for i in range(3)]
        recx = [sb(f"recx{i}", [128, NTM], F32) for i in range(3)]
        cvb = [sb(f"cvb{i}", [128, 2, 2, NTM], F32) for i in range(2)]
        junk = sb("junk", [128, 512], BF16)

        tk = {}

        def T(name, excl=False):
            if name not in tk:
                tk[name] = Tk(name, excl)
            return tk[name]

        pb = [T(f"ps{b}", True) for b in range(8)]

        def ca(name, a=None, b=None):
            lo, hi = CA[name]
            if a is None:
                return csta[:, lo:hi]
            return csta[:, lo + a:lo + (b if b is not None else a + 1)]

        DEF = []

        def defer(n, fn):
            if n <= 0:
                fn()
            else:
                DEF.append([n, fn])

        def flush_def():
            while DEF:
                n, fn = DEF.pop(0)
                fn()

        def tick():
            due = []
            for it in DEF:
                it[0] -= 1
            while DEF and DEF[0][0] <= 0:
                due.append(DEF.pop(0)[1])
            for fn in due:
                fn()

        def mm(out, lhsT, rhs, start, stop, reads, writes, signal=None, notick=False):
            if signal is None:
                signal = stop
            P.op("pe", lambda e: e.matmul(out, lhsT, rhs, start=start, stop=stop), reads, writes, signal)
            if stop and signal and not notick:
                tick()

        def act(out, in_, func, reads, writes, scale=None, bias=None, accum=None):
            kw = {}
            if scale is not None:
                kw["scale"] = scale
            if bias is not None:
                kw["bias"] = bias
            if accum is not None:
                kw["accum_out"] = accum
            P.op("act", lambda e: e.activation(out=out, in_=in_, func=func, **kw), reads, writes)

        def tt(en, out, in0, in1, op, reads, writes):
            P.op(en, lambda e: e.tensor_tensor(out=out, in0=in0, in1=in1, op=op), reads, writes)

        def ts(en, out, in0, s1, s2, op0, op1, reads, writes):
            if op1 is None:
                P.op(en, lambda e: e.tensor_scalar(out=out, in0=in0, scalar1=s1, scalar2=None, op0=op0), reads, writes)
            else:
                P.op(en, lambda e: e.tensor_scalar(out=out, in0=in0, scalar1=s1, scalar2=s2, op0=op0, op1=op1),
                     reads, writes)

        def stt(en, out, in0, scalar, in1, op0, op1, reads, writes):
            P.op(en, lambda e: e.scalar_tensor_tensor(out=out, in0=in0, scalar=scalar, in1=in1, op0=op0, op1=op1),
                 reads, writes)

        def cp(en, out, in_, reads, writes):
            P.op(en, lambda e: e.tensor_copy(out=out, in_=in_), reads, writes)

        def recip(out, in_, reads, writes):
            P.op("dve", lambda e: e.reciprocal(out=out, in_=in_), reads, writes)

        def memset(en, ap, val, writes):
            P.op(en, lambda e: e.memset(ap, val), (), writes)

        def dma(en, out, in_, reads, writes, sem):
            P.op(en, lambda e: e.dma_start(out=out, in_=in_), reads, writes, dma=sem)

        rot = {"acc": 0, "aux": 0, "r2": 0, "pt": 0, "ptx": 0, "cv": 0, "rec": 0, "fp": 0, "xq": 0, "xqs": 0}

        def nxt(key, n):
            v = rot[key]
            rot[key] = (v + 1) % n
            return v

        def acc_bank():
            return nxt("acc", N_ACC)

        def aux_bank():
            return N_ACC + nxt("aux", 8 - N_ACC)

        scr_tk = [T(f"scr{i}") for i in range(NPIECE)]
        slot_tk = [T(f"slot{i}") for i in range(NS)]

        seq = list(range(P_WIN, P_UP))
        ntile = (n_main_sub + 1) // 2
        for _ in range(ntile):
            seq += list(range(P_WIN, P_WK)) + list(range(P_WQ, NPIECE))
        W = {"next_load": 0, "next_acq": 0, "released": set(), "conv_next": 0, "nrel": 0}

        def w_pump():
            while W["next_load"] < len(seq):
                j = W["next_load"]
                if j >= NS and (j - NS) not in W["released"]:
                    break
                s = j % NS
                if scr_tk[seq[j]].w is None:
                    break
                dma("sp", ring[:, s, :], wsc[seq[j]], [scr_tk[seq[j]]], [slot_tk[s]], f"ring{s}")
                W["next_load"] += 1

        def w_acquire(expect):
            j = W["next_acq"]
            assert seq[j] == expect, (j, seq[j], expect)
            while scr_tk[expect].w is None:
                conv_issue_pair(W["conv_next"], [])
            w_pump()
            assert W["next_load"] > j, "weight ring too small for schedule"
            W["next_acq"] += 1
            s = j % NS
            return j, ring[:, s, :], slot_tk[s]

        def conv_issue_pair(i, reads):
            grp = [k for k in (i, i + 1) if k < NPIECE]
            for k in grp:
                dma("pool", wsc[k], dr["wall"][k], reads, [scr_tk[k]], f"conv{i // 2}")
            ev = scr_tk[grp[-1]].w
            for k in grp:
                scr_tk[k].w = ev
            W["conv_next"] = i + 2

        def w_release(j):
            W["released"].add(j)
            W["nrel"] += 1
            if W["conv_next"] < NPIECE and W["nrel"] % 2 == 0:
                conv_issue_pair(W["conv_next"], [slot_tk[j % NS]])
            w_pump()

        def emit_all():
            for i in range(0, min(CONV_LA, NPIECE), 2):
                conv_issue_pair(i, [])

            def ckpt(name):
                if stop == name:
                    raise _Stop()

            XR = [0, 0]
            t_xp = [[T(f"x{i}_{c}") for c in range(KC + 1)] for i in range(2)]

            class _XTk:
                def __init__(self, si):
                    self.si = si

                def __getitem__(self, k):
                    if isinstance(k, slice):
                        return [self[i] for i in range(*k.indices(KC))]
                    return t_xp[self.si][(k + XR[self.si]) % (KC + 1)]

                def __iter__(self):
                    return iter([self[i] for i in range(KC)])

                def __len__(self):
                    return KC

                def __add__(self, other):
                    return list(self) + list(other)

                def __radd__(self, other):
                    return list(other) + list(self)

            t_x = [_XTk(0), _XTk(1)]

            def xc(si, oc):
                return xT[si][:, (oc + XR[si]) % (KC + 1), :]

            def x_pieces(si):
                r = XR[si]
                if r + KC <= KC + 1:
                    return [(0, KC, r)]
                n1 = KC + 1 - r
                return [(0, n1, r), (n1, KC - n1, 0)]

            t_y = [[T(f"y{i}_{c}") for c in range(KC)] for i in range(2)]
            t_sq1 = [T(f"sq_{c}") for c in range(KC)]
            t_sq = [t_sq1, t_sq1]
            t_ln = [T("ln0"), T("ln1")]
            t_rs = [T("rs0"), T("rs1")]
            t_rs2 = [T("rsb0"), T("rsb1")]
            t_gu = [[T(f"gu{i}_{c}") for c in range(4)] for i in range(2)]
            t_at = [[T(f"at{i}_{b}") for b in range(NBM)] for i in range(2)]
            t_gm = [[T(f"gm{i}_{b}") for b in range(NBM)] for i in range(2)]

            t_csta = T("csta")
            dma("sp", csta[:], dr["csta"], [], [t_csta], "cstA")
            st_mask = xT[0][:, 0:4, :].rearrange("p k t -> p (k t)")
            st_wtf = xT[0][:, 4:8, :].rearrange("p k t -> p (k t)")
            st_small = gmT[1][:].rearrange("p k t -> p (k t)")[:, 0:NCB]
            dma("sp", st_mask, dr["c_mask"], [], t_x[0][0:4], "cstB")
            dma("sp", st_wtf, dr["c_wtf"], [], t_x[0][4:8], "cstC")
            dma("sp", bias_bc[:].rearrange("p c t -> p (c t)"), dr["c_bsr"], [], [T("bias_bc")], "cstD")
            dma("sp", st_small, dr["cstb"], [], t_gm[1], "cstE")
            w_pump()

            def smallc(name):
                lo, hi = CB[name]
                return st_small[:, lo:hi]

            t_const = T("const")
            memset("dve", ones[:], 1.0, [t_const])
            cp("dve", bones[:], smallc("bones"), t_gm[1], [t_const])
            cp("dve", rmat[:], smallc("rm"), t_gm[1], [t_const])
            cp("dve", maskN[:], st_mask, t_x[0][0:4], [t_const])
            wtf = st_wtf.rearrange("p (h t) -> p h t", h=8)
            tril_b = smallc("tril").unsqueeze(1).broadcast_to([128, 8, 128])
            tt("dve", wtb[:], wtf, tril_b, ALU.mult, t_x[0][4:8] + t_gm[1], [t_const])
            lo_s, hi_s = CB["sinkr"]
            t_e128 = T("e128")
            act(e128[:], st_small[:, lo_s:hi_s], AF.Exp, t_gm[1], [t_e128])
            for par in range(2):
                rows = slice(par * 64, (par + 1) * 64)
                for g in range(2):
                    for cc in range(2):
                        hd = 4 * g + 2 * cc + par
                        cp("dve", es_bc[rows, g, cc, :], e128[rows, hd:hd + 1].broadcast_to([64, 128]), [t_e128],
                           [t_const])
            ckpt("conv")
            t_kd = [T(f"kd{j}") for j in range(NSLOT)]
            t_vv = [T(f"vv{j}") for j in range(NSLOT)]
            memset("pool", KD[:, :, 0:128], 0.0, [t_kd[0]])
            memset("pool", VV[:, 0, :], 0.0, [t_vv[0]])
            t_h = [[T(f"h{i}_{k}") for k in range(KC)] for i in range(2)]
            memset("pool", hT[0][:, :, 0:2], 0.0, t_h[0])
            memset("pool", hT[1][:, :, 0:2], 0.0, t_h[1])

            def norm_tail_ops(si, nt, gname, b, hist_off=2):
                act(lnt[si][:, 0:nt], ps[:, b, 0:nt], AF.Ln, [pb[b], t_csta], [t_ln[si]], scale=1.0 / D,
                    bias=ca("eps"))
                act(rstd[si][:, 0:nt], lnt[si][:, 0:nt], AF.Exp, [t_ln[si]], [t_rs[si]], scale=-0.5)
                for kc in range(KC):
                    stt("dve", hT[si][:, kc, hist_off:hist_off + nt], xc(si, kc)[:, 0:nt], ca(gname, kc),
                        rstd[si][:, 0:nt], ALU.mult, ALU.mult, [t_x[si][kc], t_rs[si], t_csta], [t_h[si][kc]])

            def norm_full(si, nt, gname, hist_off=2, lag=0):
                for (c0, n, p0) in x_pieces(si):
                    act(sq[si][:, c0:c0 + n, 0:nt], xT[si][:, p0:p0 + n, 0:nt], AF.Square, t_x[si][c0:c0 + n],
                        t_sq[si][c0:c0 + n])

                def tail():
                    b = aux_bank()
                    for kc in range(KC):
                        mm(ps[:, b, 0:nt], ones[:], sq[si][:, kc, 0:nt], kc == 0, kc == KC - 1,
                           [t_const, t_sq[si][kc]], [pb[b]], notick=True)
                    norm_tail_ops(si, nt, gname, b, hist_off)

                defer(lag, tail)

            def norm1_chunk(ns_, oc, nb_, slot):
                si, nt = ns_.si, ns_.nt
                act(sq[si][:, oc, 0:nt], xT[si][:, slot, 0:nt], AF.Square, [t_xp[si][slot]], [t_sq[si][oc]])
                mm(ps[:, nb_, 0:nt], ones[:], sq[si][:, oc, 0:nt], oc == 0, oc == KC - 1, [t_const, t_sq[si][oc]],
                   [pb[nb_]], signal=True, notick=True)

            t_km = T("kmT")
            t_vm = T("vm")

            def mem_kv():
                assert XR[1] == 0
                dma("sp", xT[1][:, 0:KC, :].rearrange("p k t -> p (k t)"), dr["memT"], [], t_x[1], "xl1")
                norm_full(1, 256, "gmem")
                t_km = T("kmT")
                t_vm = T("vm")
                for hx in range(4):
                    j, wp, wt = w_acquire(P_WK + hx)
                    w3 = wp.rearrange("p (k c) -> p k c", k=KC)
                    bks = []
                    for dc in range(2):
                        b = acc_bank()
                        bks.append(b)
                        for kc in range(KC):
                            mm(ps[:, b, 0:256], w3[:, kc, dc * 128:(dc + 1) * 128], hT[1][:, kc, 2:258], kc == 0, kc == KC - 1,
                               [wt, t_h[1][kc]], [pb[b]])
                        act(sq[1][:, dc, 0:256], ps[:, b, 0:256], AF.Square, [pb[b]], [t_sq[1][dc]])
                    w_release(j)
                    bs_ = aux_bank()
                    for dc in range(2):
                        mm(ps[:, bs_, 0:256], ones[:], sq[1][:, dc, 0:256], dc == 0, dc == 1, [t_const, t_sq[1][dc]], [pb[bs_]])
                    r = nxt("r2", NR)
                    act(lnc[r][:, 0:256], ps[:, bs_, 0:256], AF.Ln, [pb[bs_], t_csta], [T(f"lnc{r}")], scale=1.0 / 256,
                        bias=ca("eps"))
                    act(rsc[r][:, 0:256], lnc[r][:, 0:256], AF.Exp, [T(f"lnc{r}")], [T(f"rsc{r}")], scale=-0.5)
                    for dc in range(2):
                        stt("dve", kmT[:, 2 * hx + dc, :], ps[:, bks[dc], 0:256], ca("gxk", dc), rsc[r][:, 0:256], ALU.mult,
                            ALU.mult, [pb[bks[dc]], T(f"rsc{r}"), t_csta], [t_km])
                for pv in range(4):
                    j, wp, wt = w_acquire(P_WV + pv)
                    w3 = wp.rearrange("p (k c) -> p k c", k=KC)
                    for mb in range(2):
                        b = acc_bank()
                        for kc in range(KC):
                            mm(ps[:, b, 0:256], hT[1][:, kc, 2 + mb * 128:2 + (mb + 1) * 128], w3[:, kc, :], kc == 0,
                               kc == KC - 1, [wt, t_h[1][kc]], [pb[b]])
                        cp("dve", vm[:, mb, pv * 256:(pv + 1) * 256], ps[:, b, 0:256], [pb[b]], [t_vm])
                    w_release(j)


            ckpt("mem")
            t_pos = t_ln
            t_cs = [T("cs0"), T("cs1")]
            t_qn = [[T(f"qn{i}_{c}") for c in range(4)] for i in range(2)]
            t_gvr = [[T(f"gvr{i}_{b}") for b in range(NBM)] for i in range(2)]
            t_gvn = [[T(f"gvn{i}_{b}") for b in range(NBM)] for i in range(2)]
            t_gvs = [T("gvs0"), T("gvs1")]
            t_gT = [[T(f"gT{i}_{j}") for j in range(NJ)] for i in range(2)]
            MAGIC = 12582912.0
            C1 = 6.28125
            C2 = 2 * math.pi - 6.28125

            def pos_load(s):
                dma("sp", posi[s.si][:, 0:s.nt], dr["posr"][:, s.pos0:s.pos0 + s.nt], [], [t_pos[s.si]], f"pl{s.si}")

            def load_sub(s):
                si, nt = s.si, s.nt
                assert nt == NTM
                src = dr["xh"] if s.halo else dr["xm"][:, KC * s.tok0:KC * (s.tok0 + nt)]
                assert XR[si] == 0
                dma("sp", xT[si][:, 0:KC, :].rearrange("p k t -> p (k t)"), src, [], t_x[si], f"xl{si}")
                pos_load(s)

            def rope_tables(s):
                si, nt = s.si, s.nt
                ang = rtmp[si][:, 0, 0:nt]
                kk = rtmp[si][:, 1, 0:nt]
                ab = rtmp[si][:, 2, 0:nt]
                cp("dve", kk, posi[si][:, 0:nt], [t_pos[si]], t_at[si])
                ts("dve", ang, kk, ca("invf"), None, ALU.mult, None, t_at[si] + [t_csta], t_at[si])
                ts("dve", kk, ang, 1.0 / (2 * math.pi), MAGIC, ALU.mult, ALU.add, t_at[si], t_at[si])
                ts("dve", kk, kk, -MAGIC, None, ALU.add, None, t_at[si], t_at[si])
                stt("dve", ang, kk, -C1, ang, ALU.mult, ALU.add, t_at[si], t_at[si])
                stt("dve", ang, kk, -C2, ang, ALU.mult, ALU.add, t_at[si], t_at[si])
                act(ab, ang, AF.Abs, t_at[si], t_at[si])
                act(cs_t[si][:, 0, 0:nt], ab, AF.Sin, t_at[si] + [t_csta], [t_cs[si]], scale=-1.0, bias=ca("halfpi"))
                act(cs_t[si][:, 1, 0:nt], ang, AF.Sin, t_at[si], [t_cs[si]])

            def post_qk(s, ci, b):
                si, nt = s.si, s.nt
                r = nxt("r2", NR)
                tq, tr, tl, trs, tt1, tt2 = (T(f"sqc{r}"), T(f"rawb{r}"), T(f"lnc{r}"), T(f"rsc{r}"), T(f"t1{r}"),
                                             T(f"t2{r}"))
                raw = ps[:, b, 0:nt]
                act(sqc[r][:, 0:nt], raw, AF.Square, [pb[b]], [tq])
                act(rawb[r][:, 0:nt], raw, AF.Copy, [pb[b]], [tr])
                defer(QK_LAG, lambda: post_qk_tail(s, ci, b, r))

            def post_qk_tail(s, ci, b, r):
                si, nt = s.si, s.nt
                tq, tr, tl, trs, tt1, tt2 = (T(f"sqc{r}"), T(f"rawb{r}"), T(f"lnc{r}"), T(f"rsc{r}"), T(f"t1{r}"),
                                             T(f"t2{r}"))
                raw = ps[:, b, 0:nt]
                b1 = aux_bank()
                mm(ps[:, b1, 0:nt], bones[:], sqc[r][:, 0:nt], True, True, [t_const, tq], [pb[b1]], notick=True)
                b2 = aux_bank()
                mm(ps[:, b2, 0:nt], rmat[:], rawb[r][:, 0:nt], True, True, [t_const, tr], [pb[b2]], notick=True)
                act(lnc[r][:, 0:nt], ps[:, b1, 0:nt], AF.Ln, [pb[b1], t_csta], [tl], scale=1.0 / 64, bias=ca("eps"))
                act(rsc[r][:, 0:nt], lnc[r][:, 0:nt], AF.Exp, [tl], [trs], scale=-0.5)
                gn, gpn = ("gq", "gqp") if ci < 4 else ("gk", "gkp")
                stt("dve", t1[r][:, 0:nt], raw, ca(gn), cs_t[si][:, 0, 0:nt], ALU.mult, ALU.mult,
                    [pb[b], t_cs[si], t_csta], [tt1])
                stt("dve", t2[r][:, 0:nt], ps[:, b2, 0:nt], ca(gpn), cs_t[si][:, 1, 0:nt], ALU.mult, ALU.mult,
                    [pb[b2], t_cs[si], t_csta], [tt2])
                tt(POOL_TT, t1[r][:, 0:nt], t1[r][:, 0:nt], t2[r][:, 0:nt], ALU.add, [tt1, tt2], [tt1])
                if ci < 4:
                    tt(POOL_TT, qn[si][:, ci, 0:nt], t1[r][:, 0:nt], rsc[r][:, 0:nt], ALU.mult, [tt1, trs],
                       [t_qn[si][ci]])
                else:
                    c0 = s.slot0 * 128
                    slots = [t_kd[s.slot0 + bb] for bb in range(s.nb)]
                    for (g, rows, orows) in ((0, slice(0, 64), slice(0, 64)), (0, slice(0, 64), slice(64, 128)),
                                             (1, slice(64, 128), slice(0, 64)), (1, slice(64, 128), slice(64, 128))):
                        tt("dve", KD[orows, g, c0:c0 + nt], t1[r][rows, 0:nt], rsc[r][rows, 0:nt], ALU.mult,
                           [tt1, trs], slots)

            def mixer_proj(subs):
                def do_piece(pi, wp, wt, s):
                    w3 = wp.rearrange("p (k c) -> p k c", k=KC)
                    if True:
                        si, nt = s.si, s.nt
                        h = hT[si]
                        if pi <= 4:
                            for jj in range(2):
                                ch = 2 * pi + jj
                                if ch <= 8:
                                    b = acc_bank()
                                    for kc in range(KC):
                                        mm(ps[:, b, 0:nt], w3[:, kc, jj * 128:(jj + 1) * 128], h[:, kc, 2:2 + nt], kc == 0,
                                           kc == KC - 1, [wt, t_h[si][kc]], [pb[b]])
                                    if ch <= 4:
                                        post_qk(s, ch, b)
                                    else:
                                        c = ch - 5
                                        act(guT[si][:, c, 0:nt], ps[:, b, 0:nt], AF.Gelu_apprx_tanh, [pb[b]],
                                            [t_gu[si][c]])
                                else:
                                    for bb in range(s.nb):
                                        b = acc_bank()
                                        for kc in range(KC):
                                            mm(ps[:, b, 0:128], h[:, kc, 2 + bb * 128:2 + (bb + 1) * 128],
                                               w3[:, kc, 128:256], kc == 0, kc == KC - 1, [wt, t_h[si][kc]], [pb[b]])
                                        cp("dve", VV[:, s.slot0 + bb, :], ps[:, b, 0:128], [pb[b]], [t_vv[s.slot0 + bb]])
                        else:
                            half = pi - 5
                            for bb in range(s.nb):
                                b = acc_bank()
                                for kc in range(KC):
                                    mm(ps[:, b, 0:256], h[:, kc, 2 + bb * 128:2 + (bb + 1) * 128], w3[:, kc, :], kc == 0,
                                       kc == KC - 1, [wt, t_h[si][kc]], [pb[b]])
                                act(gvraw[si][:, bb, half * 256:(half + 1) * 256], ps[:, b, 0:256], AF.Gelu_apprx_tanh,
                                    [pb[b]], [t_gvr[si][bb]])

                first = 0
                if len(subs) == 2 and A_FIRST:
                    held = [w_acquire(P_WIN + pi) for pi in (0, 1)]
                    for s in subs:
                        for pi in (0, 1):
                            do_piece(pi, held[pi][1], held[pi][2], s)
                    for pi in (0, 1):
                        w_release(held[pi][0])
                    first = 2
                for pi in range(first, 7):
                    j, wp, wt = w_acquire(P_WIN + pi)
                    for s in subs:
                        do_piece(pi, wp, wt, s)
                    w_release(j)
                    ckpt(f"p{pi}")
                flush_def()
                for s in subs:
                    si = s.si
                    for bb in range(s.nb):
                        act(junk[:, 0:512], gvraw[si][:, bb, :], AF.Square, [t_gvr[si][bb]], [T("junk"), t_gvs[si]],
                            accum=gvst[si][:, bb:bb + 1])
                    act(gvst[si][:, 4:4 + s.nb], gvst[si][:, 0:s.nb], AF.Ln, [t_gvs[si], t_csta], [t_gvs[si]],
                        scale=1.0 / 512, bias=ca("eps"))
                    act(gvst[si][:, 0:s.nb], gvst[si][:, 4:4 + s.nb], AF.Exp, [t_gvs[si]], [t_gvs[si]], scale=-0.5)
                    for bb in range(s.nb):
                        stt("dve", gvn[si][:, bb, :], gvraw[si][:, bb, :], gvst[si][:, bb:bb + 1], ca("gvn"), ALU.mult,
                            ALU.mult, [t_gvr[si][bb], t_gvs[si], t_csta], [t_gvn[si][bb]])

            def swa_S(u):
                s, bb, g = u["s"], u["bb"], u["g"]
                si = s.si
                sc = s.slot0 + bb
                bk = (4 * si, 4 * si + 1)
                for par in range(2):
                    rows = slice(par * 64, (par + 1) * 64)
                    for kbi, slot in enumerate((sc - 1, sc)):
                        mm(ps[:, bk[par], kbi * 256:(kbi + 1) * 256].rearrange("p (c q) -> p c q", c=2),
                           KD[rows, g, slot * 128:(slot + 1) * 128],
                           qn[si][rows, 2 * g:2 * g + 2, bb * 128:(bb + 1) * 128], True, True,
                           [t_kd[slot], t_qn[si][2 * g], t_qn[si][2 * g + 1]], [pb[bk[par]]], signal=(kbi == 1),
                           notick=True)
                u["p"] = nxt("pt", 3)

            def swa_E(u):
                si = u["s"].si
                bk = (4 * si, 4 * si + 1)
                p = u["p"]
                act(PT[p][:], ps[:, bk[0]:bk[0] + 2, :], AF.Exp, [pb[bk[0]], pb[bk[1]]], [T(f"PT{p}")], scale=0.125)

            def swa_M(u):
                s, bb = u["s"], u["bb"]
                p = u["p"]
                tp = T(f"PT{p}")
                tt(MASK_ENG, PT[p][:].rearrange("p a c -> p (a c)"), PT[p][:].rearrange("p a c -> p (a c)"), maskN[:],
                   ALU.mult, [tp, t_const], [tp])
                if s.first_real and bb == 0:
                    for par in range(2):
                        ts("dve", PT[p][:, par, 0:256], PT[p][:, par, 0:256], ca("flag"), None, ALU.mult, None,
                           [tp, t_csta], [tp])

            def swa_PV(u):
                s, bb, g, p = u["s"], u["bb"], u["g"], u["p"]
                si = s.si
                sc = s.slot0 + bb
                bo = 4 * si + 2
                tp = T(f"PT{p}")
                od = ps[:, bo, :].rearrange("p (a c q) -> p a c q", a=2, c=2)
                for par in range(2):
                    rows = slice(par * 64, (par + 1) * 64)
                    for kbi, slot in enumerate((sc - 1, sc)):
                        mm(od[rows, 0, :, :], VV[:, slot, g * 64:(g + 1) * 64],
                           PT[p][:, par, kbi * 256:(kbi + 1) * 256].rearrange("p (c q) -> p c q", c=2),
                           kbi == 0, kbi == 1, [t_vv[slot], tp], [pb[bo]], signal=False, notick=True)
                    for kbi, slot in enumerate((sc - 1, sc)):
                        mm(od[rows, 1, :, :], ones[:, 0:64],
                           PT[p][:, par, kbi * 256:(kbi + 1) * 256].rearrange("p (c q) -> p c q", c=2),
                           kbi == 0, kbi == 1, [t_const, tp], [pb[bo]], signal=(par == 1 and kbi == 1), notick=True)
                u["r"] = nxt("rec", 2)

            def swa_C(u):
                if u["two"]:
                    return
                s, g, r = u["s"], u["g"], u["r"]
                bo = 4 * s.si + 2
                od = ps[:, bo, :].rearrange("p (a c q) -> p a c q", a=2, c=2)
                os3 = osb[r][:].rearrange("p (c q) -> p c q", c=2)
                act(os3, od[:, 0, :, :], AF.Copy, [pb[bo]], [T(f"osb{r}")])

            def swa_A(u):
                s, g, r = u["s"], u["g"], u["r"]
                bo = 4 * s.si + 2
                od = ps[:, bo, :].rearrange("p (a c q) -> p a c q", a=2, c=2)
                rc3 = rec[r][:].rearrange("p (c q) -> p c q", c=2)
                tt("dve", rc3, od[:, 1, :, :], es_bc[:, g, :, :], ALU.add, [pb[bo], t_const], [T(f"rec{r}")])

            def swa_L(u):
                r = u["r"]
                trc = T(f"rec{r}")
                rc3 = rec[r][:].rearrange("p (c q) -> p c q", c=2)
                act(rc3, rc3, AF.Ln, [trc], [trc])
                act(rc3, rc3, AF.Exp, [trc], [trc], scale=-1.0)

            def swa_N(u):
                s, bb, g, r = u["s"], u["bb"], u["g"], u["r"]
                si = s.si
                rc3 = rec[r][:].rearrange("p (c q) -> p c q", c=2)
                if u["two"]:
                    bo = 4 * si + 2
                    od = ps[:, bo, :].rearrange("p (a c q) -> p a c q", a=2, c=2)
                    tt("dve", attnT[si][:, 2 * g:2 * g + 2, bb * 128:(bb + 1) * 128], od[:, 0, :, :], rc3, ALU.mult,
                       [pb[bo], T(f"rec{r}")], [t_at[si][bb]])
                    return
                os3 = osb[r][:].rearrange("p (c q) -> p c q", c=2)
                tt("dve", attnT[si][:, 2 * g:2 * g + 2, bb * 128:(bb + 1) * 128], os3, rc3, ALU.mult,
                   [T(f"osb{r}"), T(f"rec{r}")], [t_at[si][bb]])

            def gmlp(s, bb):
                si = s.si
                bm = 4 * si + 3
                mx = ps[:, bm, :].rearrange("p (c t) -> p c t", c=4)
                for c in range(4):
                    for par in range(2):
                        rows = slice(par * 64, (par + 1) * 64)
                        hd = 2 * c + par
                        mm(mx[rows, c, :], gvn[si][:, bb, hd * 64:(hd + 1) * 64], wtb[:, hd, :], True, True,
                           [t_gvn[si][bb], t_const], [pb[bm]], signal=(c == 3 and par == 1), notick=True)
                t_mxt = T("mxt")
                tt("dve", mxt[:], mx, bias_bc[:], ALU.add, [pb[bm], T("bias_bc")], [t_mxt])
                tt(GM_ENG, gmT[si][:, :, bb * 128:(bb + 1) * 128], mxt[:], guT[si][:, :, bb * 128:(bb + 1) * 128],
                   ALU.mult, [t_mxt] + t_gu[si], [t_gm[si][bb]])

            def mixer_core(subs):
                units = []
                maxnb = max(s.nb for s in subs)
                for bb in range(maxnb):
                    for g in range(2):
                        for s in subs:
                            if bb < s.nb:
                                units.append((s, bb, g))
                U = [{"s": s, "bb": bb, "g": g, "two": len(subs) == 2} for (s, bb, g) in units]
                n = len(U)

                def at(k):
                    return U[k] if 0 <= k < n else None

                for it in range(n + 3):
                    uS, uE, uP, uL = at(it), at(it - 1), at(it - 2), at(it - 3)
                    if uP is not None:
                        swa_PV(uP)
                    if uE is not None:
                        swa_E(uE)
                    if uS is not None:
                        swa_S(uS)
                    if uL is not None:
                        swa_L(uL)
                    if uP is not None:
                        swa_C(uP)
                    if uE is not None:
                        swa_M(uE)
                    if uP is not None:
                        swa_A(uP)
                    if uL is not None:
                        swa_N(uL)
                    if uL is not None and uL["g"] == 1:
                        gmlp(uL["s"], uL["bb"])
                last = subs[-1]
                ls = last.slot0 + last.nb - 1
                cp("pool", KD[:, :, 0:128], KD[:, :, ls * 128:(ls + 1) * 128], [t_kd[ls]], [t_kd[0]])
                cp("pool", VV[:, 0, :], VV[:, ls, :], [t_vv[ls]], [t_vv[0]])
                for s in subs:
                    si, nt = s.si, s.nt
                    act(sq[si][:, 0:4, 0:nt], attnT[si][:, :, 0:nt], AF.Square, t_at[si][:s.nb], t_sq[si][0:4])
                    act(sq[si][:, 4:8, 0:nt], gmT[si][:, :, 0:nt], AF.Square, t_gm[si][:s.nb], t_sq[si][4:8])
                    ba = aux_bank()
                    for c in range(4):
                        mm(ps[:, ba, 0:nt], ones[:], sq[si][:, c, 0:nt], c == 0, c == 3, [t_const, t_sq[si][c]], [pb[ba]])
                    bg = aux_bank()
                    for c in range(4):
                        mm(ps[:, bg, 0:nt], ones[:], sq[si][:, 4 + c, 0:nt], c == 0, c == 3, [t_const, t_sq[si][4 + c]],
                           [pb[bg]])
                    act(lnt[si][:, 0:nt], ps[:, ba, 0:nt], AF.Ln, [pb[ba], t_csta], [t_ln[si]], scale=1.0 / 512,
                        bias=ca("eps"))
                    act(rstd[si][:, 0:nt], lnt[si][:, 0:nt], AF.Exp, [t_ln[si]], [t_rs[si]], scale=-0.5)
                    act(lnt[si][:, 0:nt], ps[:, bg, 0:nt], AF.Ln, [pb[bg], t_csta, t_rs[si]], [t_ln[si]], scale=1.0 / 512,
                        bias=ca("eps"))
                    act(rstd2[si][:, 0:nt], lnt[si][:, 0:nt], AF.Exp, [t_ln[si]], [t_rs2[si]], scale=-0.5)
                    for c in range(4):
                        stt("dve", yT[si][:, c, 0:nt], attnT[si][:, c, 0:nt], ca("ga", c), rstd[si][:, 0:nt], ALU.mult,
                            ALU.mult, t_at[si][:s.nb] + [t_rs[si], t_csta], [t_y[si][c]])
                        stt("dve", yT[si][:, 4 + c, 0:nt], gmT[si][:, c, 0:nt], ca("gmo", c), rstd2[si][:, 0:nt],
                            ALU.mult, ALU.mult, t_gm[si][:s.nb] + [t_rs2[si], t_csta], [t_y[si][4 + c]])

            def proj_resid(subs, pbase, post_norm):
                held = []
                for pi in range(4):
                    held.append(w_acquire(pbase + pi))
                for idx, s in enumerate(subs):
                    si, nt = s.si, s.nt
                    for pi in range(4):
                        j, wp, wt = held[pi]
                        w3 = wp.rearrange("p (k c) -> p k c", k=KC)
                        for jj in range(2):
                            oc = 2 * pi + jj
                            b = acc_bank()
                            for kc in range(KC):
                                mm(ps[:, b, 0:nt], w3[:, kc, jj * 128:(jj + 1) * 128], yT[si][:, kc, 0:nt], kc == 0,
                                   kc == KC - 1, [wt, t_y[si][kc]], [pb[b]])
                            tt("dve", xc(si, oc)[:, 0:nt], ps[:, b, 0:nt], xc(si, oc)[:, 0:nt], ALU.add,
                               [pb[b], t_x[si][oc]], [t_x[si][oc]])
                        if idx == len(subs) - 1:
                            w_release(j)
                    last = idx == len(subs) - 1
                    norm_full(si, nt, post_norm, lag=(4 if not last else (3 if (post_norm == "g2" and len(subs) == 2) else 0)))
                if not (post_norm == "g2" and len(subs) == 2):
                    flush_def()

            def xattn_q(subs):
                held = [w_acquire(P_WQ + pi) for pi in range(4)]
                for idx, s in enumerate(subs):
                    si, nt = s.si, s.nt
                    for hx in range(4):
                        j, wp, wt = held[hx]
                        w3 = wp.rearrange("p (k c) -> p k c", k=KC)
                        bks = []
                        for dc in range(2):
                            b = nxt("xq", 6)
                            bks.append(b)
                            for kc in range(KC):
                                mm(ps[:, b, 0:nt], w3[:, kc, dc * 128:(dc + 1) * 128], hT[si][:, kc, 2:2 + nt], kc == 0,
                                   kc == KC - 1, [wt, t_h[si][kc]], [pb[b]])
                            act(sq[si][:, 2 * hx + dc, 0:nt], ps[:, b, 0:nt], AF.Square, [pb[b]], [t_sq[si][2 * hx + dc]])
                        if idx == len(subs) - 1:
                            w_release(j)
                        defer(2, (lambda s=s, hx=hx, bks=bks: xq_tail(s, hx, bks)))
                flush_def()

            def xq_tail(s, hx, bks):
                si, nt = s.si, s.nt
                bs_ = 6 + nxt("xqs", 2)
                for dc in range(2):
                    mm(ps[:, bs_, 0:nt], ones[:], sq[si][:, 2 * hx + dc, 0:nt], dc == 0, dc == 1,
                       [t_const, t_sq[si][2 * hx + dc]], [pb[bs_]], notick=True)
                r = nxt("r2", NR)
                act(lnc[r][:, 0:nt], ps[:, bs_, 0:nt], AF.Ln, [pb[bs_], t_csta], [T(f"lnc{r}")], scale=1.0 / 256,
                    bias=ca("eps"))
                act(rsc[r][:, 0:nt], lnc[r][:, 0:nt], AF.Exp, [T(f"lnc{r}")], [T(f"rsc{r}")], scale=-0.5)
                for dc in range(2):
                    stt("dve", yT[si][:, 2 * hx + dc, 0:nt], ps[:, bks[dc], 0:nt], ca("gxq", dc), rsc[r][:, 0:nt],
                        ALU.mult, ALU.mult, [pb[bks[dc]], T(f"rsc{r}"), t_csta], [t_y[si][2 * hx + dc]])

            def xattn_scores(s, hx, k):
                si, nt = s.si, s.nt
                bk = (2 * k, 2 * k + 1)
                for mb in range(2):
                    for dc in range(2):
                        mm(ps[:, bk[mb], 0:nt], kmT[:, 2 * hx + dc, mb * 128:(mb + 1) * 128], yT[si][:, 2 * hx + dc, 0:nt],
                           dc == 0, dc == 1, [t_km, t_y[si][2 * hx + dc]], [pb[bk[mb]]])
                p = nxt("ptx", 3)
                tp = T(f"PTx{p}")
                act(PTx[p][:, :, 0:nt], ps[:, bk[0]:bk[0] + 2, 0:nt], AF.Exp, [pb[bk[0]], pb[bk[1]]], [tp], scale=1.0 / 16)
                return p

            def xattn_pv(s, hx, p):
                si, nt = s.si, s.nt
                tp = T(f"PTx{p}")
                for mb in range(2):
                    mm(ps[:, 6, 0:nt], ones[:], PTx[p][:, mb, 0:nt], mb == 0, mb == 1, [t_const, tp], [pb[6]])
                trx = T(f"recx{p}")
                act(recx[p][:, 0:nt], ps[:, 6, 0:nt], AF.Ln, [pb[6]], [trx])
                act(recx[p][:, 0:nt], recx[p][:, 0:nt], AF.Exp, [trx], [trx], scale=-1.0)
                for dc in range(2):
                    bo = 4 + dc
                    for mb in range(2):
                        mm(ps[:, bo, 0:nt], vm[:, mb, hx * 256 + dc * 128:hx * 256 + (dc + 1) * 128], PTx[p][:, mb, 0:nt],
                           mb == 0, mb == 1, [t_vm, tp], [pb[bo]])
                    tt("dve", yT[si][:, 2 * hx + dc, 0:nt], ps[:, bo, 0:nt], recx[p][:, 0:nt], ALU.mult, [pb[bo], trx],
                       [t_y[si][2 * hx + dc]])

            def xattn_core(subs):
                units = []
                for hx in range(4):
                    for s in subs:
                        units.append((s, hx))
                pend = []
                for k, (s, hx) in enumerate(units):
                    p = xattn_scores(s, hx, k % 2)
                    pend.append((s, hx, p))
                    if len(pend) > ATT_LAG:
                        xattn_pv(*pend.pop(0))
                for u in pend:
                    xattn_pv(*u)

            def ffn(subs, next_subs):
                cwl, _ = CA["cw"]
                cbl, _ = CA["cb"]
                assert len(subs) == 2 and subs[0].nt == subs[1].nt
                nt = subs[0].nt
                for jp in range(NJ):
                    P.tag = f"ffn_up{jp:02d}"
                    j, wp, wt = w_acquire(P_UP + jp)
                    w3 = wp.rearrange("p (k c) -> p k c", k=KC)
                    v = nxt("cv", 2)
                    tcvs = [T(f"cvb{v}_0"), T(f"cvb{v}_1")]
                    prs = []
                    for wh in range(2):
                        pr = nxt("fp", 4)
                        prs.append(pr)
                        for pos, s in enumerate(subs):
                            b = 2 * pr + pos
                            for kc in range(KC):
                                mm(ps[:, b, 0:nt + 2], w3[:, kc, wh * 128:(wh + 1) * 128], hT[s.si][:, kc, 0:nt + 2],
                                   kc == 0, kc == KC - 1, [wt, t_h[s.si][kc]], [pb[b]])
                    w_release(j)
                    for wh in range(2):
                        m = 2 * jp + wh
                        pr = prs[wh]
                        pbs = [pb[2 * pr], pb[2 * pr + 1]]
                        w0 = csta[:, cwl + 3 * m:cwl + 3 * m + 1]
                        w1 = csta[:, cwl + 3 * m + 1:cwl + 3 * m + 2]
                        w2 = csta[:, cwl + 3 * m + 2:cwl + 3 * m + 3]
                        bb_ = csta[:, cbl + m:cbl + m + 1]
                        cv = cvb[v][:, wh, :, 0:nt]
                        tcv = tcvs[wh]
                        act(cv, ps[:, 2 * pr:2 * pr + 2, 2:nt + 2], AF.Identity, pbs + [t_csta], [tcv], scale=w2, bias=bb_)
                        stt("dve", cv, ps[:, 2 * pr:2 * pr + 2, 1:nt + 1], w1, cv, ALU.mult, ALU.add,
                            pbs + [tcv, t_csta], [tcv])
                        stt("dve", cv, ps[:, 2 * pr:2 * pr + 2, 0:nt], w0, cv, ALU.mult, ALU.add, pbs + [tcv, t_csta],
                            [tcv])
                    act(cvb[v][:, 0, :, 0:nt], cvb[v][:, 0, :, 0:nt], AF.Gelu_apprx_tanh, [tcvs[0]], [tcvs[0]])
                    tt("dve", gT_all[:, :, jp, 0:nt], cvb[v][:, 0, :, 0:nt], cvb[v][:, 1, :, 0:nt], ALU.mult, tcvs,
                       [t_gT[0][jp], t_gT[1][jp]])
                def dn_mm(oc, pos, s, b, helds, kcs):
                    for kc in kcs:
                        j, wp, wt = helds[kc // 11]
                        w3 = wp[:, 0:11 * 128].rearrange("p (k c) -> p k c", k=11)
                        mm(ps[:, b, 0:s.nt], w3[:, kc % 11, :], gT_all[:, pos, kc, 0:s.nt], kc == 0, kc == NJ - 1,
                           [wt, t_gT[pos][kc]], [pb[b]])

                def dn_evac(oc, pos, s, b):
                    si, nt = s.si, s.nt
                    tt("dve", xc(si, oc)[:, 0:nt], ps[:, b, 0:nt], xc(si, oc)[:, 0:nt], ALU.add, [pb[b], t_x[si][oc]],
                       [t_x[si][oc]])
                    c0 = KC * s.tok0 + oc * nt
                    dma("sp", dr["om"][:, c0:c0 + nt], xc(si, oc)[:, 0:nt], [t_x[si][oc]], [T(f"om{s.idx}_{oc}")],
                        f"xs{si}_{oc}")
                    if next_subs is not None:
                        ns_ = next_subs[pos]
                        assert ns_.si == si
                        n0 = KC * ns_.tok0 + oc * nt
                        slot = (oc + XR[si] - 1) % (KC + 1)
                        dma("pool", xT[si][:, slot, 0:nt], dr["xm"][:, n0:n0 + nt], [], [t_xp[si][slot]],
                            f"xp{si}_{oc}")
                        defer(2, (lambda ns_=ns_, oc=oc, pos=pos, slot=slot: norm1_chunk(ns_, oc, 6 + pos, slot)))

                P.tag = "ffn_dn0"
                h01 = {oc: [w_acquire(P_DOWN + 2 * oc), w_acquire(P_DOWN + 2 * oc + 1)] for oc in (0, 1)}
                bk01 = {}
                for oc in (0, 1):
                    for pos, s in enumerate(subs):
                        bk01[(oc, pos)] = acc_bank()
                        dn_mm(oc, pos, s, bk01[(oc, pos)], h01[oc], range(0, NJ - 1))
                for oc in (0, 1):
                    for pos, s in enumerate(subs):
                        dn_mm(oc, pos, s, bk01[(oc, pos)], h01[oc], [NJ - 1])
                        dn_evac(oc, pos, s, bk01[(oc, pos)])
                    w_release(h01[oc][0][0])
                    w_release(h01[oc][1][0])
                for oc in range(2, KC):
                    P.tag = f"ffn_dn{oc}"
                    helds = [w_acquire(P_DOWN + 2 * oc), w_acquire(P_DOWN + 2 * oc + 1)]
                    for pos, s in enumerate(subs):
                        b = acc_bank()
                        dn_mm(oc, pos, s, b, helds, range(NJ))
                        dn_evac(oc, pos, s, b)
                    w_release(helds[0][0])
                    w_release(helds[1][0])
                flush_def()

            t_hist = T("hist")

            def hist_save(last, use_flag):
                src = hT[last.si][:, :, last.nt:last.nt + 2]
                if use_flag:
                    ts("pool", hist[:], src, ca("flag"), None, ALU.mult, None, t_h[last.si] + [t_csta], [t_hist])
                else:
                    cp("pool", hist[:], src, t_h[last.si], [t_hist])

            def hist_load(cur, prev):
                dst = hT[cur.si][:, :, 0:2]
                if prev is None:
                    cp("pool", dst, hist[:], [t_hist], t_h[cur.si])
                else:
                    cp("pool", dst, hT[prev.si][:, :, prev.nt:prev.nt + 2], t_h[prev.si], t_h[cur.si])

            halo = Sub(0, 2, True, False, 0, 0, 1, 0)
            subs_all = []
            for k in range(n_main_sub):
                si = (k + 1) % 2
                subs_all.append(Sub(si, NBM, False, k == 0, k * NTM, HALO + k * NTM, 0, k + 1))
            tiles = [[halo]]
            for k in range(0, n_main_sub, 2):
                tiles.append(subs_all[k:k + 2])

            for ti, subs in enumerate(tiles):
                slot = 1
                for s in subs:
                    s.slot0 = slot
                    slot += s.nb
                prefetched = ti >= 2
                P.tag = "rope_norm1"
                if not prefetched:
                    for s in subs:
                        load_sub(s)
                    ckpt(f"t{ti}_load")
                    for s in subs:
                        rope_tables(s)
                for pos, s in enumerate(subs):
                    if prefetched:
                        norm_tail_ops(s.si, s.nt, "g1", 6 + pos)
                    else:
                        norm_full(s.si, s.nt, "g1")
                ckpt(f"t{ti}_norm")
                P.tag = "mixer_proj"
                mixer_proj(subs)
                ckpt(f"t{ti}_proj")
                P.tag = "mixer_core"
                mixer_core(subs)
                ckpt(f"t{ti}_core")
                P.tag = "proj_resid"
                proj_resid(subs, P_WOUT, "g2")
                ckpt(f"t{ti}_wout")
                if ti == 0:
                    P.tag = "mem_kv"
                    mem_kv()
                P.tag = "xattn_q"
                xattn_q(subs)
                ckpt(f"t{ti}_xq")
                P.tag = "xattn_core"
                xattn_core(subs)
                ckpt(f"t{ti}_xc")
                P.tag = "proj_resid_wo"
                proj_resid(subs, P_WO, "g3")
                ckpt(f"t{ti}_wo")
                if subs[0].halo:
                    hist_save(subs[0], True)
                    continue
                prev = None
                for s in subs:
                    hist_load(s, prev)
                    prev = s
                hist_save(subs[-1], False)
                next_subs = tiles[ti + 1] if (ti + 1 < len(tiles) and ti >= 1) else None
                if next_subs is not None:
                    P.tag = "rope_norm1"
                    for s in next_subs:
                        pos_load(s)
                        rope_tables(s)
                P.tag = "ffn"
                ffn(subs, next_subs)
                if next_subs is not None:
                    for s in subs:
                        XR[s.si] = (XR[s.si] - 1) % (KC + 1)
                ckpt(f"t{ti}_ffn")

        stopped = False
        try:
            emit_all()
        except _Stop:
            stopped = True
        if not stopped:
            assert W["next_acq"] == len(seq), (W["next_acq"], len(seq))
        final_waits = sorted(P.dmacnt.items())

        sems = {}
        for k in ["pe", "act", "dve", "pool"] + sorted(P.dmacnt.keys()):
            sems[k] = es.enter_context(nc.semaphore(k))
        _CACHE["sbuf_left"] = nc.sbuf_bytes_remaining
        block = es.enter_context(nc.Block())

        def replay(en, h, final=False):
            for waits, fn, inc in P.eng[en].ops:
                for k, v in waits:
                    h.wait_ge(sems[k], v)
                ins = fn(h)
                if inc is not None:
                    ins.then_inc(sems[inc[0]], inc[1])
            if final:
                for k, v in final_waits:
                    h.wait_ge(sems[k], v)

        @block.tensor
        def _(h):
            replay("pe", h)

        @block.scalar
        def _(h):
            replay("act", h)

        @block.vector
        def _(h):
            replay("dve", h)

        @block.gpsimd
        def _(h):
            replay("pool", h)

        @block.sync
        def _(h):
            replay("sp", h, final=True)

    counts = {k: len(v.ops) for k, v in P.eng.items()}
    _CACHE["tags"] = P.tags
    return nc, counts


def _piece_k1024(Wm, cols):
    K, N = Wm.shape
    w = Wm.reshape(KC, 128, N)[:, :, cols]
    return np.ascontiguousarray(w.transpose(1, 0, 2)).reshape(128, -1)


def _build_wall(inp):
    wall = np.zeros((NPIECE, 128, PW), np.float32)
    wkv = inp["xa_wkv"][0]
    for hx in range(4):
        wall[P_WK + hx] = _piece_k1024(wkv, np.arange(hx * 256, (hx + 1) * 256))
        wall[P_WV + hx] = _piece_k1024(wkv, 1024 + np.arange(hx * 256, (hx + 1) * 256))
    w_in = inp["w_in"][0]
    order = np.concatenate([np.arange(0, 512), np.arange(512, 640), np.arange(768, 1280), np.arange(640, 768),
                            np.arange(1280, 1792)])
    for pi in range(7):
        wall[P_WIN + pi] = _piece_k1024(w_in, order[pi * 256:(pi + 1) * 256])
    for pi in range(4):
        cols = np.arange(pi * 256, (pi + 1) * 256)
        wall[P_WOUT + pi] = _piece_k1024(inp["w_out"][0], cols)
        wall[P_WQ + pi] = _piece_k1024(inp["xa_wq"][0], cols)
        wall[P_WO + pi] = _piece_k1024(inp["xa_wo"][0], cols)
    up = inp["ffn_up"][0]
    for j in range(NJ):
        cols = np.concatenate([np.arange(j * 128, (j + 1) * 128), DFF + np.arange(j * 128, (j + 1) * 128)])
        wall[P_UP + j] = _piece_k1024(up, cols)
    dn = inp["ffn_down"][0].reshape(NJ, 128, D)
    for oc in range(KC):
        for half in range(2):
            w = dn[half * 11:(half + 1) * 11, :, oc * 128:(oc + 1) * 128]
            wall[P_DOWN + 2 * oc + half, :, 0:11 * 128] = w.transpose(1, 0, 2).reshape(128, -1)
    return wall


def _cols(v, n):
    return np.ascontiguousarray(np.asarray(v, np.float32).reshape(n, 128).T)


def _build_csta(inp, flag):
    c = np.zeros((128, NCA), np.float32)

    def put(name, arr):
        lo, hi = CA[name]
        c[:, lo:hi] = arr

    put("g1", _cols(inp["mix_norm"][0], 8))
    put("g2", _cols(inp["xa_norm"][0], 8))
    put("g3", _cols(inp["ffn_norm"][0], 8))
    put("gmem", _cols(inp["mem_norm"][0], 8))
    p = np.arange(128)
    put("gq", inp["q_norm"][0][p % 64][:, None])
    put("gk", inp["k_norm"][0][p % 64][:, None])
    put("gqp", inp["q_norm"][0][(p % 64 + 32) % 64][:, None])
    put("gkp", inp["k_norm"][0][(p % 64 + 32) % 64][:, None])
    inv_freq = (1.0 / (10000.0 ** (np.arange(32, dtype=np.float32) * np.float32(2.0 / 64)))).astype(np.float32)
    put("invf", inv_freq[(p % 64) % 32][:, None])
    put("flag", np.full((128, 1), flag, np.float32))
    put("sel0", (p == 0).astype(np.float32)[:, None])
    put("sel1", (p == 1).astype(np.float32)[:, None])
    put("ga", _cols(inp["attn_out_norm"][0], 4))
    put("gmo", _cols(inp["gmlp_out_norm"][0], 4))
    put("gxq", _cols(inp["xa_q_norm"][0], 2))
    put("gxk", _cols(inp["xa_k_norm"][0], 2))
    conv = inp["ffn_conv"][0]
    cbv = inp["ffn_conv_b"][0]
    cw = np.zeros((128, 44, 3), np.float32)
    cbm = np.zeros((128, 44), np.float32)
    for j in range(NJ):
        for wh in range(2):
            ch = wh * DFF + j * 128 + p
            cw[:, 2 * j + wh, :] = conv[:, ch].T
            cbm[:, 2 * j + wh] = cbv[ch]
    put("cw", cw.reshape(128, -1))
    put("cb", cbm)
    put("halfpi", np.full((128, 1), math.pi / 2, np.float32))
    put("eps", np.full((128, 1), EPS, np.float32))
    put("gvn", np.broadcast_to(inp["gmlp_v_norm"][0][None, :], (128, 512)))
    return c


def _build_cstb(inp):
    c = np.zeros((128, NCB), np.float32)

    def put(name, arr):
        lo, hi = CB[name]
        c[:, lo:hi] = arr

    put("sinkr", np.broadcast_to(inp["attn_sinks"][0].reshape(1, -1), (128, 8)))
    s = np.arange(128)[:, None]
    t = np.arange(128)[None, :]
    put("tril", (s <= t).astype(np.float32))
    rm = np.zeros((128, 128), np.float32)
    for hh in range(2):
        for mm_ in range(64):
            if mm_ < 32:
                rm[hh * 64 + mm_ + 32, hh * 64 + mm_] = -1.0
            else:
                rm[hh * 64 + mm_ - 32, hh * 64 + mm_] = 1.0
    put("rm", rm)
    bo = np.zeros((128, 128), np.float32)
    bo[0:64, 0:64] = 1.0
    bo[64:128, 64:128] = 1.0
    put("bones", bo)
    mprev = (s > t).astype(np.float32)
    mcur = (s <= t).astype(np.float32)
    m = np.zeros((128, 2, 2, 2, 128), np.float32)
    m[:, :, 0, :, :] = mprev[:, None, None, :]
    m[:, :, 1, :, :] = mcur[:, None, None, :]
    ws = inp["gmlp_ws"][0]
    wtf = np.ascontiguousarray(ws.transpose(2, 0, 1)).reshape(128, -1)
    bs = inp["gmlp_bs"][0]
    bsr = np.zeros((128, 4, 128), np.float32)
    for c_ in range(4):
        bsr[0:64, c_, :] = bs[2 * c_][None, :]
        bsr[64:128, c_, :] = bs[2 * c_ + 1][None, :]
    bsr = bsr.reshape(128, 512)
    return c, m.reshape(128, -1), wtf, bsr


def _xT_blocks(xrows, nt):
    Tn = xrows.shape[0]
    a = xrows.reshape(Tn // nt, nt, KC, 128)
    a = a.transpose(3, 0, 2, 1)
    return np.ascontiguousarray(a).reshape(128, -1)


def _prepare_inputs(inp, n_main_sub=NSUB_MAIN):
    inp = {k: np.asarray(v) for k, v in inp.items()}
    wall = _build_wall(inp)
    cstb, c_mask, c_wtf, c_bsr = _build_cstb(inp)
    x = inp["x"]
    pos = inp["positions"]
    maps = []
    for c in range(NCORES):
        b, half = c // 2, c % 2
        s0 = half * TOK_CORE
        if half == 0:
            xh_rows = np.zeros((HALO, D), np.float32)
            ph = np.zeros((HALO,), np.int32)
        else:
            xh_rows = x[b, s0 - HALO:s0]
            ph = pos[b, s0 - HALO:s0]
        xm_rows = x[b, s0:s0 + TOK_CORE]
        posr = np.concatenate([ph, pos[b, s0:s0 + TOK_CORE]]).astype(np.int32)
        maps.append({
            "xh": _xT_blocks(xh_rows, HALO),
            "xm": _xT_blocks(xm_rows, NTM),
            "posr": np.ascontiguousarray(np.broadcast_to(posr[None, :], (128, HALO + TOK_CORE))),
            "memT": _xT_blocks(inp["mem"][b], 256),
            "wall": wall,
            "csta": _build_csta(inp, 1.0 if half == 1 else 0.0),
            "cstb": cstb,
            "c_mask": c_mask,
            "c_wtf": c_wtf,
            "c_bsr": c_bsr,
        })
    return maps


def _assemble(results):
    out = np.zeros((BATCH, SEQ, D), np.float32)
    for c in range(NCORES):
        b, half = c // 2, c % 2
        om = np.asarray(results[c]["om"]).reshape(128, NSUB_MAIN, KC, NTM)
        rows = om.transpose(1, 3, 2, 0).reshape(TOK_CORE, D)
        out[b, half * TOK_CORE:(half + 1) * TOK_CORE] = rows
    return out


def kernel(**inputs):
    if "nc" not in _CACHE:
        _CACHE["nc"] = build_program()[0]
    nc = _CACHE["nc"]
    maps = _prepare_inputs(inputs)
    res = run_bass_kernel_spmd(nc, maps, core_ids=list(range(NCORES)))
    return _assemble(res.results)
```

```python
import math
import contextlib
import numpy as np
import concourse.bass as bass
import concourse.mybir as mybir
from concourse.bass_utils import run_bass_kernel_spmd

F32 = mybir.dt.float32
BF16 = mybir.dt.bfloat16
I32 = mybir.dt.int32
AF = mybir.ActivationFunctionType
ALU = mybir.AluOpType

D = 1024
KC = 8
SEQ = 8192
BATCH = 4
NCORES = 8
TOK_CORE = 4096
HALO = 256
NBM = 2
NTM = NBM * 128
NSUB_MAIN = TOK_CORE // NTM
DFF = 2816
NJ = 22
EPS = 1e-6
NS = 8
POOL_TT = "dve"
QK_LAG = 1
ATT_LAG = 2
MASK_ENG = "dve"
GM_ENG = "dve"
N_ACC = 5
A_FIRST = True
FFN_POOL = "pool"
PW = 2048

P_WIN, P_WOUT, P_WK, P_WV, P_WQ, P_WO, P_UP, P_DOWN = 0, 7, 11, 15, 19, 23, 27, 49
NPIECE = 65
CONV_LA = 4


def _cst_layout():
    off = {}
    n = 0
    for name, w in [("g1", 8), ("g2", 8), ("g3", 8), ("gmem", 8), ("gq", 1), ("gk", 1), ("gqp", 1), ("gkp", 1), ("invf", 1),
                    ("flag", 1), ("sel0", 1), ("sel1", 1), ("ga", 4), ("gmo", 4), ("gxq", 2), ("gxk", 2),
                    ("cw", 44 * 3), ("cb", 44), ("halfpi", 1), ("eps", 1), ("gvn", 512)]:
        off[name] = (n, n + w)
        n += w
    return off, n


CA, NCA = _cst_layout()


def _cstb_layout():
    off = {}
    n = 0
    for name, w in [("sinkr", 8), ("tril", 128), ("rm", 128), ("bones", 128)]:
        off[name] = (n, n + w)
        n += w
    return off, n


CB, NCB = _cstb_layout()


class Tk:
    __slots__ = ("name", "w", "r", "excl")

    def __init__(self, name, excl=False):
        self.name = name
        self.w = None
        self.r = {}
        self.excl = excl


class Eng:
    def __init__(self, name):
        self.name = name
        self.ops = []
        self.cnt = 0
        self.seen = {}


class Prog:
    def __init__(self):
        self.eng = {n: Eng(n) for n in ("pe", "act", "dve", "pool", "sp")}
        self.dmacnt = {}
        self.tag = ""
        self.tags = {n: [] for n in self.eng}

    def op(self, en, fn, reads=(), writes=(), signal=True, dma=None):
        e = self.eng[en]
        need = {}

        def req(ev):
            if ev is None:
                return
            k, v = ev
            if need.get(k, 0) < v:
                need[k] = v

        for b in reads:
            req(b.w)
            if b.excl:
                for k, v in b.r.items():
                    if k != en:
                        req((k, v))
        for b in writes:
            req(b.w)
            for k, v in b.r.items():
                req((k, v))
        waits = []
        for k, v in need.items():
            if k == "pe" and en == "pe":
                continue
            if e.seen.get(k, 0) < v:
                e.seen[k] = v
                waits.append((k, v))
        if dma is not None:
            self.dmacnt[dma] = self.dmacnt.get(dma, 0) + 16
            ev = (dma, self.dmacnt[dma])
            inc = (dma, 16)
        elif signal:
            e.cnt += 1
            ev = (en, e.cnt)
            inc = (en, 1)
        else:
            ev = (en, e.cnt + 1)
            inc = None
        e.ops.append((waits, fn, inc))
        self.tags[en].append(self.tag)
        for b in reads:
            if b.r.get(ev[0], 0) < ev[1]:
                b.r[ev[0]] = ev[1]
        for b in writes:
            b.w = ev
            b.r = {}
        return ev


class Sub:
    def __init__(self, si, nb, halo, first_real, tok0, pos0, slot0, idx):
        self.si = si
        self.nb = nb
        self.nt = nb * 128
        self.halo = halo
        self.first_real = first_real
        self.tok0 = tok0
        self.pos0 = pos0
        self.slot0 = slot0
        self.idx = idx


_CACHE = {}


class _Stop(Exception):
    pass


def build_program(n_main_sub=NSUB_MAIN, stop=None):
    nc = bass.Bass("TRN2", target_bir_lowering=False)
    P = Prog()
    dr = {}
    dr["xh"] = nc.dram_tensor("xh", [128, KC * HALO], F32, kind="ExternalInput").ap()
    dr["xm"] = nc.dram_tensor("xm", [128, KC * TOK_CORE], F32, kind="ExternalInput").ap()
    dr["posr"] = nc.dram_tensor("posr", [128, HALO + TOK_CORE], I32, kind="ExternalInput").ap()
    dr["memT"] = nc.dram_tensor("memT", [128, KC * 256], F32, kind="ExternalInput").ap()
    dr["wall"] = nc.dram_tensor("wall", [NPIECE, 128, PW], F32, kind="ExternalInput").ap()
    dr["csta"] = nc.dram_tensor("csta", [128, NCA], F32, kind="ExternalInput").ap()
    dr["cstb"] = nc.dram_tensor("cstb", [128, NCB], F32, kind="ExternalInput").ap()
    dr["c_mask"] = nc.dram_tensor("c_mask", [128, 1024], F32, kind="ExternalInput").ap()
    dr["c_wtf"] = nc.dram_tensor("c_wtf", [128, 1024], F32, kind="ExternalInput").ap()
    dr["c_bsr"] = nc.dram_tensor("c_bsr", [128, 512], F32, kind="ExternalInput").ap()
    dr["om"] = nc.dram_tensor("om", [128, KC * TOK_CORE], F32, kind="ExternalOutput").ap()
    wsc = nc.dram_tensor("wsc", [NPIECE, 128, PW], BF16).ap()

    es = contextlib.ExitStack()
    with es:
        def sb(name, shape, dt):
            return es.enter_context(nc.sbuf_tensor(name, shape, dt))

        ring = sb("ring", [128, NS, PW], BF16)
        csta = sb("csta_s", [128, NCA], F32)
        ps = es.enter_context(nc.psum_tensor("ps", [128, 8, 512], F32))
        ones = sb("ones", [128, 128], BF16)
        bones = sb("bones", [128, 128], BF16)
        rmat = sb("rmat", [128, 128], BF16)
        maskN = sb("maskN", [128, 1024], BF16)
        osb = [sb(f"osb{i}", [128, 256], F32) for i in range(2)]
        wtb = sb("wtb", [128, 8, 128], BF16)
        es_bc = sb("es_bc", [128, 2, 2, 128], F32)
        bias_bc = sb("bias_bc", [128, 4, 128], F32)
        e128 = sb("e128", [128, 8], F32)
        mxt = sb("mxt", [128, 4, 128], F32)
        hist = sb("hist", [128, KC, 2], BF16)
        kmT = sb("kmT", [128, 8, 256], BF16)
        vm = sb("vm", [128, 2, 1024], BF16)
        NSLOT = 1 + 2 * NBM
        KD = sb("KD", [128, 2, NSLOT * 128], BF16)
        VV = sb("VV", [128, NSLOT, 128], BF16)
        xT = [sb(f"xT{i}", [128, KC + 1, NTM], F32) for i in range(2)]
        hT = [sb(f"hT{i}", [128, KC, 2 + NTM], BF16) for i in range(2)]
        yT = [sb(f"yT{i}", [128, KC, NTM], BF16) for i in range(2)]
        sq_one = sb("sq", [128, KC, NTM], BF16)
        sq = [sq_one, sq_one]
        cs_t = [sb(f"cs{i}", [128, 2, NTM], F32) for i in range(2)]
        lnt = [sb(f"lnt{i}", [128, NTM], F32) for i in range(2)]
        rstd = [sb(f"rstd{i}", [128, NTM], F32) for i in range(2)]
        rstd2 = [sb(f"rstdb{i}", [128, NTM], F32) for i in range(2)]
        qn = [sb(f"qn{i}", [128, 4, NTM], BF16) for i in range(2)]
        guT = [sb(f"guT{i}", [128, 4, NTM], F32) for i in range(2)]
        gvraw = [sb(f"gvraw{i}", [128, NBM, 512], F32) for i in range(2)]
        gvn = [sb(f"gvn{i}", [128, NBM, 512], BF16) for i in range(2)]
        gvst = [sb(f"gvst{i}", [128, 8], F32) for i in range(2)]
        attnT = [sb(f"attnT{i}", [128, 4, NTM], F32) for i in range(2)]
        gmT = [sb(f"gmT{i}", [128, 4, NTM], F32) for i in range(2)]
        rtmp = attnT
        posi = [lnt[i][:].bitcast(I32) for i in range(2)]
        gT_all = sb("gT", [128, 2, NJ, NTM], BF16)
        NR = 3
        sqc = [sb(f"sqc{i}", [128, NTM], BF16) for i in range(NR)]
        rawb = [sb(f"rawb{i}", [128, NTM], BF16) for i in range(NR)]
        lnc = [sb(f"lnc{i}", [128, NTM], F32) for i in range(NR)]
        rsc = [sb(f"rsc{i}", [128, NTM], F32) for i in range(NR)]
        t1 = [sb(f"t1_{i}", [128, NTM], F32) for i in range(NR)]
        t2 = [sb(f"t2_{i}", [128, NTM], F32) for i in range(NR)]
        PT = [sb(f"PT{i}", [128, 2, 512], BF16) for i in range(3)]
        rec = [sb(f"rec{i}", [128, 256], F32) for i in range(2)]
        PTx = [sb(f"PTx{i}", [128, 2, NTM], BF16) for i in range(3)]
        recx = [sb(f"recx{i}", [128, NTM], F32) for i in range(3)]
        cvb = [sb(f"cvb{i}", [128, 2, 2, NTM], F32) for i in range(2)]
        junk = sb("junk", [128, 512], BF16)

        tk = {}

        def T(name, excl=False):
            if name not in tk:
                tk[name] = Tk(name, excl)
            return tk[name]

        pb = [T(f"ps{b}", True) for b in range(8)]

        def ca(name, a=None, b=None):
            lo, hi = CA[name]
            if a is None:
                return csta[:, lo:hi]
            return csta[:, lo + a:lo + (b if b is not None else a + 1)]

        DEF = []

        def defer(n, fn):
            if n <= 0:
                fn()
            else:
                DEF.append([n, fn])

        def flush_def():
            while DEF:
                n, fn = DEF.pop(0)
                fn()

        def tick():
            due = []
            for it in DEF:
                it[0] -= 1
            while DEF and DEF[0][0] <= 0:
                due.append(DEF.pop(0)[1])
            for fn in due:
                fn()

        def mm(out, lhsT, rhs, start, stop, reads, writes, signal=None, notick=False):
            if signal is None:
                signal = stop
            P.op("pe", lambda e: e.matmul(out, lhsT, rhs, start=start, stop=stop), reads, writes, signal)
            if stop and signal and not notick:
                tick()

        def act(out, in_, func, reads, writes, scale=None, bias=None, accum=None):
            kw = {}
            if scale is not None:
                kw["scale"] = scale
            if bias is not None:
                kw["bias"] = bias
            if accum is not None:
                kw["accum_out"] = accum
            P.op("act", lambda e: e.activation(out=out, in_=in_, func=func, **kw), reads, writes)

        def tt(en, out, in0, in1, op, reads, writes):
            P.op(en, lambda e: e.tensor_tensor(out=out, in0=in0, in1=in1, op=op), reads, writes)

        def ts(en, out, in0, s1, s2, op0, op1, reads, writes):
            if op1 is None:
                P.op(en, lambda e: e.tensor_scalar(out=out, in0=in0, scalar1=s1, scalar2=None, op0=op0), reads, writes)
            else:
                P.op(en, lambda e: e.tensor_scalar(out=out, in0=in0, scalar1=s1, scalar2=s2, op0=op0, op1=op1),
                     reads, writes)

        def stt(en, out, in0, scalar, in1, op0, op1, reads, writes):
            P.op(en, lambda e: e.scalar_tensor_tensor(out=out, in0=in0, scalar=scalar, in1=in1, op0=op0, op1=op1),
                 reads, writes)

        def cp(en, out, in_, reads, writes):
            P.op(en, lambda e: e.tensor_copy(out=out, in_=in_), reads, writes)

        def recip(out, in_, reads, writes):
            P.op("dve", lambda e: e.reciprocal(out=out, in_=in_), reads, writes)

        def memset(en, ap, val, writes):
            P.op(en, lambda e: e.memset(ap, val), (), writes)

        def dma(en, out, in_, reads, writes, sem):
            P.op(en, lambda e: e.dma_start(out=out, in_=in_), reads, writes, dma=sem)

        rot = {"acc": 0, "aux": 0, "r2": 0, "pt": 0, "ptx": 0, "cv": 0, "rec": 0, "fp": 0, "xq": 0, "xqs": 0}

        def nxt(key, n):
            v = rot[key]
            rot[key] = (v + 1) % n
            return v

        def acc_bank():
            return nxt("acc", N_ACC)

        def aux_bank():
            return N_ACC + nxt("aux", 8 - N_ACC)

        scr_tk = [T(f"scr{i}") for i in range(NPIECE)]
        slot_tk = [T(f"slot{i}") for i in range(NS)]

        seq = list(range(P_WIN, P_UP))
        ntile = (n_main_sub + 1) // 2
        for _ in range(ntile):
            seq += list(range(P_WIN, P_WK)) + list(range(P_WQ, NPIECE))
        W = {"next_load": 0, "next_acq": 0, "released": set(), "conv_next": 0, "nrel": 0}

        def w_pump():
            while W["next_load"] < len(seq):
                j = W["next_load"]
                if j >= NS and (j - NS) not in W["released"]:
                    break
                s = j % NS
                if scr_tk[seq[j]].w is None:
                    break
                dma("sp", ring[:, s, :], wsc[seq[j]], [scr_tk[seq[j]]], [slot_tk[s]], f"ring{s}")
                W["next_load"] += 1

        def w_acquire(expect):
            j = W["next_acq"]
            assert seq[j] == expect, (j, seq[j], expect)
            while scr_tk[expect].w is None:
                conv_issue_pair(W["conv_next"], [])
            w_pump()
            assert W["next_load"] > j, "weight ring too small for schedule"
            W["next_acq"] += 1
            s = j % NS
            return j, ring[:, s, :], slot_tk[s]

        def conv_issue_pair(i, reads):
            grp = [k for k in (i, i + 1) if k < NPIECE]
            for k in grp:
                dma("pool", wsc[k], dr["wall"][k], reads, [scr_tk[k]], f"conv{i // 2}")
            ev = scr_tk[grp[-1]].w
            for k in grp:
                scr_tk[k].w = ev
            W["conv_next"] = i + 2

        def w_release(j):
            W["released"].add(j)
            W["nrel"] += 1
            if W["conv_next"] < NPIECE and W["nrel"] % 2 == 0:
                conv_issue_pair(W["conv_next"], [slot_tk[j % NS]])
            w_pump()

        def emit_all():
            for i in range(0, min(CONV_LA, NPIECE), 2):
                conv_issue_pair(i, [])

            def ckpt(name):
                if stop == name:
                    raise _Stop()

            XR = [0, 0]
            t_xp = [[T(f"x{i}_{c}") for c in range(KC + 1)] for i in range(2)]

            class _XTk:
                def __init__(self, si):
                    self.si = si

                def __getitem__(self, k):
                    if isinstance(k, slice):
                        return [self[i] for i in range(*k.indices(KC))]
                    return t_xp[self.si][(k + XR[self.si]) % (KC + 1)]

                def __iter__(self):
                    return iter([self[i] for i in range(KC)])

                def __len__(self):
                    return KC

                def __add__(self, other):
                    return list(self) + list(other)

                def __radd__(self, other):
                    return list(other) + list(self)

            t_x = [_XTk(0), _XTk(1)]

            def xc(si, oc):
                return xT[si][:, (oc + XR[si]) % (KC + 1), :]

            def x_pieces(si):
                r = XR[si]
                if r + KC <= KC + 1:
                    return [(0, KC, r)]
                n1 = KC + 1 - r
                return [(0, n1, r), (n1, KC - n1, 0)]

            t_y = [[T(f"y{i}_{c}") for c in range(KC)] for i in range(2)]
            t_sq1 = [T(f"sq_{c}") for c in range(KC)]
            t_sq = [t_sq1, t_sq1]
            t_ln = [T("ln0"), T("ln1")]
            t_rs = [T("rs0"), T("rs1")]
            t_rs2 = [T("rsb0"), T("rsb1")]
            t_gu = [[T(f"gu{i}_{c}") for c in range(4)] for i in range(2)]
            t_at = [[T(f"at{i}_{b}") for b in range(NBM)] for i in range(2)]
            t_gm = [[T(f"gm{i}_{b}") for b in range(NBM)] for i in range(2)]

            t_csta = T("csta")
            dma("sp", csta[:], dr["csta"], [], [t_csta], "cstA")
            st_mask = xT[0][:, 0:4, :].rearrange("p k t -> p (k t)")
            st_wtf = xT[0][:, 4:8, :].rearrange("p k t -> p (k t)")
            st_small = gmT[1][:].rearrange("p k t -> p (k t)")[:, 0:NCB]
            dma("sp", st_mask, dr["c_mask"], [], t_x[0][0:4], "cstB")
            dma("sp", st_wtf, dr["c_wtf"], [], t_x[0][4:8], "cstC")
            dma("sp", bias_bc[:].rearrange("p c t -> p (c t)"), dr["c_bsr"], [], [T("bias_bc")], "cstD")
            dma("sp", st_small, dr["cstb"], [], t_gm[1], "cstE")
            w_pump()

            def smallc(name):
                lo, hi = CB[name]
                return st_small[:, lo:hi]

            t_const = T("const")
            memset("dve", ones[:], 1.0, [t_const])
            cp("dve", bones[:], smallc("bones"), t_gm[1], [t_const])
            cp("dve", rmat[:], smallc("rm"), t_gm[1], [t_const])
            cp("dve", maskN[:], st_mask, t_x[0][0:4], [t_const])
            wtf = st_wtf.rearrange("p (h t) -> p h t", h=8)
            tril_b = smallc("tril").unsqueeze(1).broadcast_to([128, 8, 128])
            tt("dve", wtb[:], wtf, tril_b, ALU.mult, t_x[0][4:8] + t_gm[1], [t_const])
            lo_s, hi_s = CB["sinkr"]
            t_e128 = T("e128")
            act(e128[:], st_small[:, lo_s:hi_s], AF.Exp, t_gm[1], [t_e128])
            for par in range(2):
                rows = slice(par * 64, (par + 1) * 64)
                for g in range(2):
                    for cc in range(2):
                        hd = 4 * g + 2 * cc + par
                        cp("dve", es_bc[rows, g, cc, :], e128[rows, hd:hd + 1].broadcast_to([64, 128]), [t_e128],
                           [t_const])
            ckpt("conv")
            t_kd = [T(f"kd{j}") for j in range(NSLOT)]
            t_vv = [T(f"vv{j}") for j in range(NSLOT)]
            memset("pool", KD[:, :, 0:128], 0.0, [t_kd[0]])
            memset("pool", VV[:, 0, :], 0.0, [t_vv[0]])
            t_h = [[T(f"h{i}_{k}") for k in range(KC)] for i in range(2)]
            memset("pool", hT[0][:, :, 0:2], 0.0, t_h[0])
            memset("pool", hT[1][:, :, 0:2], 0.0, t_h[1])

            def norm_tail_ops(si, nt, gname, b, hist_off=2):
                act(lnt[si][:, 0:nt], ps[:, b, 0:nt], AF.Ln, [pb[b], t_csta], [t_ln[si]], scale=1.0 / D,
                    bias=ca("eps"))
                act(rstd[si][:, 0:nt], lnt[si][:, 0:nt], AF.Exp, [t_ln[si]], [t_rs[si]], scale=-0.5)
                for kc in range(KC):
                    stt("dve", hT[si][:, kc, hist_off:hist_off + nt], xc(si, kc)[:, 0:nt], ca(gname, kc),
                        rstd[si][:, 0:nt], ALU.mult, ALU.mult, [t_x[si][kc], t_rs[si], t_csta], [t_h[si][kc]])

            def norm_full(si, nt, gname, hist_off=2, lag=0):
                for (c0, n, p0) in x_pieces(si):
                    act(sq[si][:, c0:c0 + n, 0:nt], xT[si][:, p0:p0 + n, 0:nt], AF.Square, t_x[si][c0:c0 + n],
                        t_sq[si][c0:c0 + n])

                def tail():
                    b = aux_bank()
                    for kc in range(KC):
                        mm(ps[:, b, 0:nt], ones[:], sq[si][:, kc, 0:nt], kc == 0, kc == KC - 1,
                           [t_const, t_sq[si][kc]], [pb[b]], notick=True)
                    norm_tail_ops(si, nt, gname, b, hist_off)

                defer(lag, tail)

            def norm1_chunk(ns_, oc, nb_, slot):
                si, nt = ns_.si, ns_.nt
                act(sq[si][:, oc, 0:nt], xT[si][:, slot, 0:nt], AF.Square, [t_xp[si][slot]], [t_sq[si][oc]])
                mm(ps[:, nb_, 0:nt], ones[:], sq[si][:, oc, 0:nt], oc == 0, oc == KC - 1, [t_const, t_sq[si][oc]],
                   [pb[nb_]], signal=True, notick=True)

            t_km = T("kmT")
            t_vm = T("vm")

            def mem_kv():
                assert XR[1] == 0
                dma("sp", xT[1][:, 0:KC, :].rearrange("p k t -> p (k t)"), dr["memT"], [], t_x[1], "xl1")
                norm_full(1, 256, "gmem")
                t_km = T("kmT")
                t_vm = T("vm")
                for hx in range(4):
                    j, wp, wt = w_acquire(P_WK + hx)
                    w3 = wp.rearrange("p (k c) -> p k c", k=KC)
                    bks = []
                    for dc in range(2):
                        b = acc_bank()
                        bks.append(b)
                        for kc in range(KC):
                            mm(ps[:, b, 0:256], w3[:, kc, dc * 128:(dc + 1) * 128], hT[1][:, kc, 2:258], kc == 0, kc == KC - 1,
                               [wt, t_h[1][kc]], [pb[b]])
                        act(sq[1][:, dc, 0:256], ps[:, b, 0:256], AF.Square, [pb[b]], [t_sq[1][dc]])
                    w_release(j)
                    bs_ = aux_bank()
                    for dc in range(2):
                        mm(ps[:, bs_, 0:256], ones[:], sq[1][:, dc, 0:256], dc == 0, dc == 1, [t_const, t_sq[1][dc]], [pb[bs_]])
                    r = nxt("r2", NR)
                    act(lnc[r][:, 0:256], ps[:, bs_, 0:256], AF.Ln, [pb[bs_], t_csta], [T(f"lnc{r}")], scale=1.0 / 256,
                        bias=ca("eps"))
                    act(rsc[r][:, 0:256], lnc[r][:, 0:256], AF.Exp, [T(f"lnc{r}")], [T(f"rsc{r}")], scale=-0.5)
                    for dc in range(2):
                        stt("dve", kmT[:, 2 * hx + dc, :], ps[:, bks[dc], 0:256], ca("gxk", dc), rsc[r][:, 0:256], ALU.mult,
                            ALU.mult, [pb[bks[dc]], T(f"rsc{r}"), t_csta], [t_km])
                for pv in range(4):
                    j, wp, wt = w_acquire(P_WV + pv)
                    w3 = wp.rearrange("p (k c) -> p k c", k=KC)
                    for mb in range(2):
                        b = acc_bank()
                        for kc in range(KC):
                            mm(ps[:, b, 0:256], hT[1][:, kc, 2 + mb * 128:2 + (mb + 1) * 128], w3[:, kc, :], kc == 0,
                               kc == KC - 1, [wt, t_h[1][kc]], [pb[b]])
                        cp("dve", vm[:, mb, pv * 256:(pv + 1) * 256], ps[:, b, 0:256], [pb[b]], [t_vm])
                    w_release(j)


            ckpt("mem")
            t_pos = t_ln
            t_cs = [T("cs0"), T("cs1")]
            t_qn = [[T(f"qn{i}_{c}") for c in range(4)] for i in range(2)]
            t_gvr = [[T(f"gvr{i}_{b}") for b in range(NBM)] for i in range(2)]
            t_gvn = [[T(f"gvn{i}_{b}") for b in range(NBM)] for i in range(2)]
            t_gvs = [T("gvs0"), T("gvs1")]
            t_gT = [[T(f"gT{i}_{j}") for j in range(NJ)] for i in range(2)]
            MAGIC = 12582912.0
            C1 = 6.28125
            C2 = 2 * math.pi - 6.28125

            def pos_load(s, en="sp"):
                dma(en, posi[s.si][:, 0:s.nt], dr["posr"][:, s.pos0:s.pos0 + s.nt], [], [t_pos[s.si]], f"pl{s.si}")

            def load_sub(s):
                si, nt = s.si, s.nt
                assert nt == NTM
                src = dr["xh"] if s.halo else dr["xm"][:, KC * s.tok0:KC * (s.tok0 + nt)]
                assert XR[si] == 0
                dma("act", xT[si][:, 0:KC, :].rearrange("p k t -> p (k t)"), src, [], t_x[si], f"xl{si}")
                pos_load(s, "act")

            def rope_tables(s):
                si, nt = s.si, s.nt
                ang = rtmp[si][:, 0, 0:nt]
                kk = rtmp[si][:, 1, 0:nt]
                ab = rtmp[si][:, 2, 0:nt]
                cp("dve", kk, posi[si][:, 0:nt], [t_pos[si]], t_at[si])
                ts("dve", ang, kk, ca("invf"), None, ALU.mult, None, t_at[si] + [t_csta], t_at[si])
                ts("dve", kk, ang, 1.0 / (2 * math.pi), MAGIC, ALU.mult, ALU.add, t_at[si], t_at[si])
                ts("dve", kk, kk, -MAGIC, None, ALU.add, None, t_at[si], t_at[si])
                stt("dve", ang, kk, -C1, ang, ALU.mult, ALU.add, t_at[si], t_at[si])
                stt("dve", ang, kk, -C2, ang, ALU.mult, ALU.add, t_at[si], t_at[si])
                act(ab, ang, AF.Abs, t_at[si], t_at[si])
                act(cs_t[si][:, 0, 0:nt], ab, AF.Sin, t_at[si] + [t_csta], [t_cs[si]], scale=-1.0, bias=ca("halfpi"))
                act(cs_t[si][:, 1, 0:nt], ang, AF.Sin, t_at[si], [t_cs[si]])

            def post_qk(s, ci, b):
                si, nt = s.si, s.nt
                r = nxt("r2", NR)
                tq, tr, tl, trs, tt1, tt2 = (T(f"sqc{r}"), T(f"rawb{r}"), T(f"lnc{r}"), T(f"rsc{r}"), T(f"t1{r}"),
                                             T(f"t2{r}"))
                raw = ps[:, b, 0:nt]
                act(sqc[r][:, 0:nt], raw, AF.Square, [pb[b]], [tq])
                act(rawb[r][:, 0:nt], raw, AF.Copy, [pb[b]], [tr])
                defer(QK_LAG, lambda: post_qk_tail(s, ci, b, r))

            def post_qk_tail(s, ci, b, r):
                si, nt = s.si, s.nt
                tq, tr, tl, trs, tt1, tt2 = (T(f"sqc{r}"), T(f"rawb{r}"), T(f"lnc{r}"), T(f"rsc{r}"), T(f"t1{r}"),
                                             T(f"t2{r}"))
                raw = ps[:, b, 0:nt]
                b1 = aux_bank()
                mm(ps[:, b1, 0:nt], bones[:], sqc[r][:, 0:nt], True, True, [t_const, tq], [pb[b1]], notick=True)
                b2 = aux_bank()
                mm(ps[:, b2, 0:nt], rmat[:], rawb[r][:, 0:nt], True, True, [t_const, tr], [pb[b2]], notick=True)
                act(lnc[r][:, 0:nt], ps[:, b1, 0:nt], AF.Ln, [pb[b1], t_csta], [tl], scale=1.0 / 64, bias=ca("eps"))
                act(rsc[r][:, 0:nt], lnc[r][:, 0:nt], AF.Exp, [tl], [trs], scale=-0.5)
                gn, gpn = ("gq", "gqp") if ci < 4 else ("gk", "gkp")
                stt("dve", t1[r][:, 0:nt], raw, ca(gn), cs_t[si][:, 0, 0:nt], ALU.mult, ALU.mult,
                    [pb[b], t_cs[si], t_csta], [tt1])
                stt("dve", t2[r][:, 0:nt], ps[:, b2, 0:nt], ca(gpn), cs_t[si][:, 1, 0:nt], ALU.mult, ALU.mult,
                    [pb[b2], t_cs[si], t_csta], [tt2])
                tt(POOL_TT, t1[r][:, 0:nt], t1[r][:, 0:nt], t2[r][:, 0:nt], ALU.add, [tt1, tt2], [tt1])
                if ci < 4:
                    tt(POOL_TT, qn[si][:, ci, 0:nt], t1[r][:, 0:nt], rsc[r][:, 0:nt], ALU.mult, [tt1, trs],
                       [t_qn[si][ci]])
                else:
                    c0 = s.slot0 * 128
                    slots = [t_kd[s.slot0 + bb] for bb in range(s.nb)]
                    for (g, rows, orows) in ((0, slice(0, 64), slice(0, 64)), (0, slice(0, 64), slice(64, 128)),
                                             (1, slice(64, 128), slice(0, 64)), (1, slice(64, 128), slice(64, 128))):
                        tt("dve", KD[orows, g, c0:c0 + nt], t1[r][rows, 0:nt], rsc[r][rows, 0:nt], ALU.mult,
                           [tt1, trs], slots)

            def mixer_proj(subs):
                def do_piece(pi, wp, wt, s):
                    w3 = wp.rearrange("p (k c) -> p k c", k=KC)
                    if True:
                        si, nt = s.si, s.nt
                        h = hT[si]
                        if pi <= 4:
                            for jj in range(2):
                                ch = 2 * pi + jj
                                if ch <= 8:
                                    b = acc_bank()
                                    for kc in range(KC):
                                        mm(ps[:, b, 0:nt], w3[:, kc, jj * 128:(jj + 1) * 128], h[:, kc, 2:2 + nt], kc == 0,
                                           kc == KC - 1, [wt, t_h[si][kc]], [pb[b]])
                                    if ch <= 4:
                                        post_qk(s, ch, b)
                                    else:
                                        c = ch - 5
                                        act(guT[si][:, c, 0:nt], ps[:, b, 0:nt], AF.Gelu_apprx_tanh, [pb[b]],
                                            [t_gu[si][c]])
                                else:
                                    for bb in range(s.nb):
                                        b = acc_bank()
                                        for kc in range(KC):
                                            mm(ps[:, b, 0:128], h[:, kc, 2 + bb * 128:2 + (bb + 1) * 128],
                                               w3[:, kc, 128:256], kc == 0, kc == KC - 1, [wt, t_h[si][kc]], [pb[b]])
                                        cp("dve", VV[:, s.slot0 + bb, :], ps[:, b, 0:128], [pb[b]], [t_vv[s.slot0 + bb]])
                        else:
                            half = pi - 5
                            for bb in range(s.nb):
                                b = acc_bank()
                                for kc in range(KC):
                                    mm(ps[:, b, 0:256], h[:, kc, 2 + bb * 128:2 + (bb + 1) * 128], w3[:, kc, :], kc == 0,
                                       kc == KC - 1, [wt, t_h[si][kc]], [pb[b]])
                                act(gvraw[si][:, bb, half * 256:(half + 1) * 256], ps[:, b, 0:256], AF.Gelu_apprx_tanh,
                                    [pb[b]], [t_gvr[si][bb]])

                first = 0
                if len(subs) == 2 and A_FIRST:
                    held = [w_acquire(P_WIN + pi) for pi in (0, 1)]
                    for s in subs:
                        for pi in (0, 1):
                            do_piece(pi, held[pi][1], held[pi][2], s)
                    for pi in (0, 1):
                        w_release(held[pi][0])
                    first = 2
                for pi in range(first, 7):
                    j, wp, wt = w_acquire(P_WIN + pi)
                    for s in subs:
                        do_piece(pi, wp, wt, s)
                    w_release(j)
                    ckpt(f"p{pi}")
                flush_def()
                for s in subs:
                    si = s.si
                    for bb in range(s.nb):
                        act(junk[:, 0:512], gvraw[si][:, bb, :], AF.Square, [t_gvr[si][bb]], [T("junk"), t_gvs[si]],
                            accum=gvst[si][:, bb:bb + 1])
                    act(gvst[si][:, 4:4 + s.nb], gvst[si][:, 0:s.nb], AF.Ln, [t_gvs[si], t_csta], [t_gvs[si]],
                        scale=1.0 / 512, bias=ca("eps"))
                    act(gvst[si][:, 0:s.nb], gvst[si][:, 4:4 + s.nb], AF.Exp, [t_gvs[si]], [t_gvs[si]], scale=-0.5)
                    for bb in range(s.nb):
                        stt("dve", gvn[si][:, bb, :], gvraw[si][:, bb, :], gvst[si][:, bb:bb + 1], ca("gvn"), ALU.mult,
                            ALU.mult, [t_gvr[si][bb], t_gvs[si], t_csta], [t_gvn[si][bb]])

            def swa_S(u):
                s, bb, g = u["s"], u["bb"], u["g"]
                si = s.si
                sc = s.slot0 + bb
                bk = (4 * si, 4 * si + 1)
                for par in range(2):
                    rows = slice(par * 64, (par + 1) * 64)
                    for kbi, slot in enumerate((sc - 1, sc)):
                        mm(ps[:, bk[par], kbi * 256:(kbi + 1) * 256].rearrange("p (c q) -> p c q", c=2),
                           KD[rows, g, slot * 128:(slot + 1) * 128],
                           qn[si][rows, 2 * g:2 * g + 2, bb * 128:(bb + 1) * 128], True, True,
                           [t_kd[slot], t_qn[si][2 * g], t_qn[si][2 * g + 1]], [pb[bk[par]]], signal=(kbi == 1),
                           notick=True)
                u["p"] = nxt("pt", 3)

            def swa_E(u):
                si = u["s"].si
                bk = (4 * si, 4 * si + 1)
                p = u["p"]
                act(PT[p][:], ps[:, bk[0]:bk[0] + 2, :], AF.Exp, [pb[bk[0]], pb[bk[1]]], [T(f"PT{p}")], scale=0.125)

            def swa_M(u):
                s, bb = u["s"], u["bb"]
                p = u["p"]
                tp = T(f"PT{p}")
                tt(MASK_ENG, PT[p][:].rearrange("p a c -> p (a c)"), PT[p][:].rearrange("p a c -> p (a c)"), maskN[:],
                   ALU.mult, [tp, t_const], [tp])
                if s.first_real and bb == 0:
                    for par in range(2):
                        ts("dve", PT[p][:, par, 0:256], PT[p][:, par, 0:256], ca("flag"), None, ALU.mult, None,
                           [tp, t_csta], [tp])

            def swa_PV(u):
                s, bb, g, p = u["s"], u["bb"], u["g"], u["p"]
                si = s.si
                sc = s.slot0 + bb
                bo = 4 * si + 2
                tp = T(f"PT{p}")
                od = ps[:, bo, :].rearrange("p (a c q) -> p a c q", a=2, c=2)
                for par in range(2):
                    rows = slice(par * 64, (par + 1) * 64)
                    for kbi, slot in enumerate((sc - 1, sc)):
                        mm(od[rows, 0, :, :], VV[:, slot, g * 64:(g + 1) * 64],
                           PT[p][:, par, kbi * 256:(kbi + 1) * 256].rearrange("p (c q) -> p c q", c=2),
                           kbi == 0, kbi == 1, [t_vv[slot], tp], [pb[bo]], signal=False, notick=True)
                    for kbi, slot in enumerate((sc - 1, sc)):
                        mm(od[rows, 1, :, :], ones[:, 0:64],
                           PT[p][:, par, kbi * 256:(kbi + 1) * 256].rearrange("p (c q) -> p c q", c=2),
                           kbi == 0, kbi == 1, [t_const, tp], [pb[bo]], signal=(par == 1 and kbi == 1), notick=True)
                u["r"] = nxt("rec", 2)

            def swa_C(u):
                if u["two"]:
                    return
                s, g, r = u["s"], u["g"], u["r"]
                bo = 4 * s.si + 2
                od = ps[:, bo, :].rearrange("p (a c q) -> p a c q", a=2, c=2)
                os3 = osb[r][:].rearrange("p (c q) -> p c q", c=2)
                act(os3, od[:, 0, :, :], AF.Copy, [pb[bo]], [T(f"osb{r}")])

            def swa_A(u):
                s, g, r = u["s"], u["g"], u["r"]
                bo = 4 * s.si + 2
                od = ps[:, bo, :].rearrange("p (a c q) -> p a c q", a=2, c=2)
                rc3 = rec[r][:].rearrange("p (c q) -> p c q", c=2)
                tt("dve", rc3, od[:, 1, :, :], es_bc[:, g, :, :], ALU.add, [pb[bo], t_const], [T(f"rec{r}")])

            def swa_L(u):
                r = u["r"]
                trc = T(f"rec{r}")
                rc3 = rec[r][:].rearrange("p (c q) -> p c q", c=2)
                act(rc3, rc3, AF.Ln, [trc], [trc])
                act(rc3, rc3, AF.Exp, [trc], [trc], scale=-1.0)

            def swa_N(u):
                s, bb, g, r = u["s"], u["bb"], u["g"], u["r"]
                si = s.si
                rc3 = rec[r][:].rearrange("p (c q) -> p c q", c=2)
                if u["two"]:
                    bo = 4 * si + 2
                    od = ps[:, bo, :].rearrange("p (a c q) -> p a c q", a=2, c=2)
                    tt("dve", attnT[si][:, 2 * g:2 * g + 2, bb * 128:(bb + 1) * 128], od[:, 0, :, :], rc3, ALU.mult,
                       [pb[bo], T(f"rec{r}")], [t_at[si][bb]])
                    return
                os3 = osb[r][:].rearrange("p (c q) -> p c q", c=2)
                tt("dve", attnT[si][:, 2 * g:2 * g + 2, bb * 128:(bb + 1) * 128], os3, rc3, ALU.mult,
                   [T(f"osb{r}"), T(f"rec{r}")], [t_at[si][bb]])

            def gmlp(s, bb):
                si = s.si
                bm = 4 * si + 3
                mx = ps[:, bm, :].rearrange("p (c t) -> p c t", c=4)
                for c in range(4):
                    for par in range(2):
                        rows = slice(par * 64, (par + 1) * 64)
                        hd = 2 * c + par
                        mm(mx[rows, c, :], gvn[si][:, bb, hd * 64:(hd + 1) * 64], wtb[:, hd, :], True, True,
                           [t_gvn[si][bb], t_const], [pb[bm]], signal=(c == 3 and par == 1), notick=True)
                t_mxt = T("mxt")
                tt("dve", mxt[:], mx, bias_bc[:], ALU.add, [pb[bm], T("bias_bc")], [t_mxt])
                tt(GM_ENG, gmT[si][:, :, bb * 128:(bb + 1) * 128], mxt[:], guT[si][:, :, bb * 128:(bb + 1) * 128],
                   ALU.mult, [t_mxt] + t_gu[si], [t_gm[si][bb]])

            def mixer_core(subs):
                units = []
                maxnb = max(s.nb for s in subs)
                for bb in range(maxnb):
                    for g in range(2):
                        for s in subs:
                            if bb < s.nb:
                                units.append((s, bb, g))
                U = [{"s": s, "bb": bb, "g": g, "two": len(subs) == 2} for (s, bb, g) in units]
                n = len(U)

                def at(k):
                    return U[k] if 0 <= k < n else None

                for it in range(n + 3):
                    uS, uE, uP, uL = at(it), at(it - 1), at(it - 2), at(it - 3)
                    if uP is not None:
                        swa_PV(uP)
                    if uE is not None:
                        swa_E(uE)
                    if uS is not None:
                        swa_S(uS)
                    if uL is not None:
                        swa_L(uL)
                    if uP is not None:
                        swa_C(uP)
                    if uE is not None:
                        swa_M(uE)
                    if uP is not None:
                        swa_A(uP)
                    if uL is not None:
                        swa_N(uL)
                    if uL is not None and uL["g"] == 1:
                        gmlp(uL["s"], uL["bb"])
                last = subs[-1]
                ls = last.slot0 + last.nb - 1
                cp("pool", KD[:, :, 0:128], KD[:, :, ls * 128:(ls + 1) * 128], [t_kd[ls]], [t_kd[0]])
                cp("pool", VV[:, 0, :], VV[:, ls, :], [t_vv[ls]], [t_vv[0]])
                for s in subs:
                    si, nt = s.si, s.nt
                    act(sq[si][:, 0:4, 0:nt], attnT[si][:, :, 0:nt], AF.Square, t_at[si][:s.nb], t_sq[si][0:4])
                    act(sq[si][:, 4:8, 0:nt], gmT[si][:, :, 0:nt], AF.Square, t_gm[si][:s.nb], t_sq[si][4:8])
                    ba = aux_bank()
                    for c in range(4):
                        mm(ps[:, ba, 0:nt], ones[:], sq[si][:, c, 0:nt], c == 0, c == 3, [t_const, t_sq[si][c]], [pb[ba]])
                    bg = aux_bank()
                    for c in range(4):
                        mm(ps[:, bg, 0:nt], ones[:], sq[si][:, 4 + c, 0:nt], c == 0, c == 3, [t_const, t_sq[si][4 + c]],
                           [pb[bg]])
                    act(lnt[si][:, 0:nt], ps[:, ba, 0:nt], AF.Ln, [pb[ba], t_csta], [t_ln[si]], scale=1.0 / 512,
                        bias=ca("eps"))
                    act(rstd[si][:, 0:nt], lnt[si][:, 0:nt], AF.Exp, [t_ln[si]], [t_rs[si]], scale=-0.5)
                    act(lnt[si][:, 0:nt], ps[:, bg, 0:nt], AF.Ln, [pb[bg], t_csta, t_rs[si]], [t_ln[si]], scale=1.0 / 512,
                        bias=ca("eps"))
                    act(rstd2[si][:, 0:nt], lnt[si][:, 0:nt], AF.Exp, [t_ln[si]], [t_rs2[si]], scale=-0.5)
                    for c in range(4):
                        stt("dve", yT[si][:, c, 0:nt], attnT[si][:, c, 0:nt], ca("ga", c), rstd[si][:, 0:nt], ALU.mult,
                            ALU.mult, t_at[si][:s.nb] + [t_rs[si], t_csta], [t_y[si][c]])
                        stt("dve", yT[si][:, 4 + c, 0:nt], gmT[si][:, c, 0:nt], ca("gmo", c), rstd2[si][:, 0:nt],
                            ALU.mult, ALU.mult, t_gm[si][:s.nb] + [t_rs2[si], t_csta], [t_y[si][4 + c]])

            def proj_resid(subs, pbase, post_norm):
                held = []
                for pi in range(4):
                    held.append(w_acquire(pbase + pi))
                for idx, s in enumerate(subs):
                    si, nt = s.si, s.nt
                    for pi in range(4):
                        j, wp, wt = held[pi]
                        w3 = wp.rearrange("p (k c) -> p k c", k=KC)
                        for jj in range(2):
                            oc = 2 * pi + jj
                            b = acc_bank()
                            for kc in range(KC):
                                mm(ps[:, b, 0:nt], w3[:, kc, jj * 128:(jj + 1) * 128], yT[si][:, kc, 0:nt], kc == 0,
                                   kc == KC - 1, [wt, t_y[si][kc]], [pb[b]])
                            tt("dve", xc(si, oc)[:, 0:nt], ps[:, b, 0:nt], xc(si, oc)[:, 0:nt], ALU.add,
                               [pb[b], t_x[si][oc]], [t_x[si][oc]])
                        if idx == len(subs) - 1:
                            w_release(j)
                    last = idx == len(subs) - 1
                    norm_full(si, nt, post_norm, lag=(4 if not last else (3 if (post_norm == "g2" and len(subs) == 2) else 0)))
                if not (post_norm == "g2" and len(subs) == 2):
                    flush_def()

            def xattn_q(subs):
                held = [w_acquire(P_WQ + pi) for pi in range(4)]
                for idx, s in enumerate(subs):
                    si, nt = s.si, s.nt
                    for hx in range(4):
                        j, wp, wt = held[hx]
                        w3 = wp.rearrange("p (k c) -> p k c", k=KC)
                        bks = []
                        for dc in range(2):
                            b = nxt("xq", 6)
                            bks.append(b)
                            for kc in range(KC):
                                mm(ps[:, b, 0:nt], w3[:, kc, dc * 128:(dc + 1) * 128], hT[si][:, kc, 2:2 + nt], kc == 0,
                                   kc == KC - 1, [wt, t_h[si][kc]], [pb[b]])
                            act(sq[si][:, 2 * hx + dc, 0:nt], ps[:, b, 0:nt], AF.Square, [pb[b]], [t_sq[si][2 * hx + dc]])
                        if idx == len(subs) - 1:
                            w_release(j)
                        defer(2, (lambda s=s, hx=hx, bks=bks: xq_tail(s, hx, bks)))
                flush_def()

            def xq_tail(s, hx, bks):
                si, nt = s.si, s.nt
                bs_ = 6 + nxt("xqs", 2)
                for dc in range(2):
                    mm(ps[:, bs_, 0:nt], ones[:], sq[si][:, 2 * hx + dc, 0:nt], dc == 0, dc == 1,
                       [t_const, t_sq[si][2 * hx + dc]], [pb[bs_]], notick=True)
                r = nxt("r2", NR)
                act(lnc[r][:, 0:nt], ps[:, bs_, 0:nt], AF.Ln, [pb[bs_], t_csta], [T(f"lnc{r}")], scale=1.0 / 256,
                    bias=ca("eps"))
                act(rsc[r][:, 0:nt], lnc[r][:, 0:nt], AF.Exp, [T(f"lnc{r}")], [T(f"rsc{r}")], scale=-0.5)
                for dc in range(2):
                    stt("dve", yT[si][:, 2 * hx + dc, 0:nt], ps[:, bks[dc], 0:nt], ca("gxq", dc), rsc[r][:, 0:nt],
                        ALU.mult, ALU.mult, [pb[bks[dc]], T(f"rsc{r}"), t_csta], [t_y[si][2 * hx + dc]])

            def xattn_scores(s, hx, k):
                si, nt = s.si, s.nt
                bk = (2 * k, 2 * k + 1)
                for mb in range(2):
                    for dc in range(2):
                        mm(ps[:, bk[mb], 0:nt], kmT[:, 2 * hx + dc, mb * 128:(mb + 1) * 128], yT[si][:, 2 * hx + dc, 0:nt],
                           dc == 0, dc == 1, [t_km, t_y[si][2 * hx + dc]], [pb[bk[mb]]])
                p = nxt("ptx", 3)
                tp = T(f"PTx{p}")
                act(PTx[p][:, :, 0:nt], ps[:, bk[0]:bk[0] + 2, 0:nt], AF.Exp, [pb[bk[0]], pb[bk[1]]], [tp], scale=1.0 / 16)
                return p

            def xattn_pv(s, hx, p):
                si, nt = s.si, s.nt
                tp = T(f"PTx{p}")
                for mb in range(2):
                    mm(ps[:, 6, 0:nt], ones[:], PTx[p][:, mb, 0:nt], mb == 0, mb == 1, [t_const, tp], [pb[6]])
                trx = T(f"recx{p}")
                act(recx[p][:, 0:nt], ps[:, 6, 0:nt], AF.Ln, [pb[6]], [trx])
                act(recx[p][:, 0:nt], recx[p][:, 0:nt], AF.Exp, [trx], [trx], scale=-1.0)
                for dc in range(2):
                    bo = 4 + dc
                    for mb in range(2):
                        mm(ps[:, bo, 0:nt], vm[:, mb, hx * 256 + dc * 128:hx * 256 + (dc + 1) * 128], PTx[p][:, mb, 0:nt],
                           mb == 0, mb == 1, [t_vm, tp], [pb[bo]])
                    tt("dve", yT[si][:, 2 * hx + dc, 0:nt], ps[:, bo, 0:nt], recx[p][:, 0:nt], ALU.mult, [pb[bo], trx],
                       [t_y[si][2 * hx + dc]])

            def xattn_core(subs):
                units = []
                for hx in range(4):
                    for s in subs:
                        units.append((s, hx))
                pend = []
                for k, (s, hx) in enumerate(units):
                    p = xattn_scores(s, hx, k % 2)
                    pend.append((s, hx, p))
                    if len(pend) > ATT_LAG:
                        xattn_pv(*pend.pop(0))
                for u in pend:
                    xattn_pv(*u)

            def ffn(subs, next_subs):
                cwl, _ = CA["cw"]
                cbl, _ = CA["cb"]
                assert len(subs) == 2 and subs[0].nt == subs[1].nt
                nt = subs[0].nt
                for jp in range(NJ):
                    P.tag = f"ffn_up{jp:02d}"
                    j, wp, wt = w_acquire(P_UP + jp)
                    w3 = wp.rearrange("p (k c) -> p k c", k=KC)
                    v = nxt("cv", 2)
                    tcvs = [T(f"cvb{v}_0"), T(f"cvb{v}_1")]
                    prs = []
                    for wh in range(2):
                        pr = nxt("fp", 4)
                        prs.append(pr)
                        for pos, s in enumerate(subs):
                            b = 2 * pr + pos
                            for kc in range(KC):
                                mm(ps[:, b, 0:nt + 2], w3[:, kc, wh * 128:(wh + 1) * 128], hT[s.si][:, kc, 0:nt + 2],
                                   kc == 0, kc == KC - 1, [wt, t_h[s.si][kc]], [pb[b]])
                    w_release(j)
                    for wh in range(2):
                        m = 2 * jp + wh
                        pr = prs[wh]
                        pbs = [pb[2 * pr], pb[2 * pr + 1]]
                        w0 = csta[:, cwl + 3 * m:cwl + 3 * m + 1]
                        w1 = csta[:, cwl + 3 * m + 1:cwl + 3 * m + 2]
                        w2 = csta[:, cwl + 3 * m + 2:cwl + 3 * m + 3]
                        bb_ = csta[:, cbl + m:cbl + m + 1]
                        cv = cvb[v][:, wh, :, 0:nt]
                        tcv = tcvs[wh]
                        act(cv, ps[:, 2 * pr:2 * pr + 2, 2:nt + 2], AF.Identity, pbs + [t_csta], [tcv], scale=w2, bias=bb_)
                        stt("dve", cv, ps[:, 2 * pr:2 * pr + 2, 1:nt + 1], w1, cv, ALU.mult, ALU.add,
                            pbs + [tcv, t_csta], [tcv])
                        stt("dve", cv, ps[:, 2 * pr:2 * pr + 2, 0:nt], w0, cv, ALU.mult, ALU.add, pbs + [tcv, t_csta],
                            [tcv])
                    act(cvb[v][:, 0, :, 0:nt], cvb[v][:, 0, :, 0:nt], AF.Gelu_apprx_tanh, [tcvs[0]], [tcvs[0]])
                    tt("dve", gT_all[:, :, jp, 0:nt], cvb[v][:, 0, :, 0:nt], cvb[v][:, 1, :, 0:nt], ALU.mult, tcvs,
                       [t_gT[0][jp], t_gT[1][jp]])
                def dn_mm(oc, pos, s, b, helds, kcs):
                    for kc in kcs:
                        j, wp, wt = helds[kc // 11]
                        w3 = wp[:, 0:11 * 128].rearrange("p (k c) -> p k c", k=11)
                        mm(ps[:, b, 0:s.nt], w3[:, kc % 11, :], gT_all[:, pos, kc, 0:s.nt], kc == 0, kc == NJ - 1,
                           [wt, t_gT[pos][kc]], [pb[b]])

                def dn_evac(oc, pos, s, b):
                    si, nt = s.si, s.nt
                    tt("dve", xc(si, oc)[:, 0:nt], ps[:, b, 0:nt], xc(si, oc)[:, 0:nt], ALU.add, [pb[b], t_x[si][oc]],
                       [t_x[si][oc]])
                    c0 = KC * s.tok0 + oc * nt
                    dma("sp", dr["om"][:, c0:c0 + nt], xc(si, oc)[:, 0:nt], [t_x[si][oc]], [T(f"om{s.idx}_{oc}")],
                        f"xs{si}_{oc}")
                    if next_subs is not None:
                        ns_ = next_subs[pos]
                        assert ns_.si == si
                        n0 = KC * ns_.tok0 + oc * nt
                        slot = (oc + XR[si] - 1) % (KC + 1)
                        dma("pool", xT[si][:, slot, 0:nt], dr["xm"][:, n0:n0 + nt], [], [t_xp[si][slot]],
                            f"xp{si}_{oc}")
                        defer(2, (lambda ns_=ns_, oc=oc, pos=pos, slot=slot: norm1_chunk(ns_, oc, 6 + pos, slot)))

                P.tag = "ffn_dn0"
                h01 = {oc: [w_acquire(P_DOWN + 2 * oc), w_acquire(P_DOWN + 2 * oc + 1)] for oc in (0, 1)}
                bk01 = {}
                for oc in (0, 1):
                    for pos, s in enumerate(subs):
                        bk01[(oc, pos)] = acc_bank()
                        dn_mm(oc, pos, s, bk01[(oc, pos)], h01[oc], range(0, NJ - 1))
                for oc in (0, 1):
                    for pos, s in enumerate(subs):
                        dn_mm(oc, pos, s, bk01[(oc, pos)], h01[oc], [NJ - 1])
                        dn_evac(oc, pos, s, bk01[(oc, pos)])
                    w_release(h01[oc][0][0])
                    w_release(h01[oc][1][0])
                for oc in range(2, KC):
                    P.tag = f"ffn_dn{oc}"
                    helds = [w_acquire(P_DOWN + 2 * oc), w_acquire(P_DOWN + 2 * oc + 1)]
                    for pos, s in enumerate(subs):
                        b = acc_bank()
                        dn_mm(oc, pos, s, b, helds, range(NJ))
                        dn_evac(oc, pos, s, b)
                    w_release(helds[0][0])
                    w_release(helds[1][0])
                flush_def()

            t_hist = T("hist")

            def hist_save(last, use_flag):
                src = hT[last.si][:, :, last.nt:last.nt + 2]
                if use_flag:
                    ts("pool", hist[:], src, ca("flag"), None, ALU.mult, None, t_h[last.si] + [t_csta], [t_hist])
                else:
                    cp("pool", hist[:], src, t_h[last.si], [t_hist])

            def hist_load(cur, prev):
                dst = hT[cur.si][:, :, 0:2]
                if prev is None:
                    cp("pool", dst, hist[:], [t_hist], t_h[cur.si])
                else:
                    cp("pool", dst, hT[prev.si][:, :, prev.nt:prev.nt + 2], t_h[prev.si], t_h[cur.si])

            halo = Sub(0, 2, True, False, 0, 0, 1, 0)
            subs_all = []
            for k in range(n_main_sub):
                si = (k + 1) % 2
                subs_all.append(Sub(si, NBM, False, k == 0, k * NTM, HALO + k * NTM, 0, k + 1))
            tiles = [[halo]]
            for k in range(0, n_main_sub, 2):
                tiles.append(subs_all[k:k + 2])

            for ti, subs in enumerate(tiles):
                slot = 1
                for s in subs:
                    s.slot0 = slot
                    slot += s.nb
                prefetched = ti >= 2
                P.tag = "rope_norm1"
                if not prefetched:
                    for s in subs:
                        load_sub(s)
                    ckpt(f"t{ti}_load")
                    for s in subs:
                        rope_tables(s)
                for pos, s in enumerate(subs):
                    if prefetched:
                        norm_tail_ops(s.si, s.nt, "g1", 6 + pos)
                    else:
                        norm_full(s.si, s.nt, "g1")
                ckpt(f"t{ti}_norm")
                P.tag = "mixer_proj"
                mixer_proj(subs)
                ckpt(f"t{ti}_proj")
                P.tag = "mixer_core"
                mixer_core(subs)
                ckpt(f"t{ti}_core")
                P.tag = "proj_resid"
                proj_resid(subs, P_WOUT, "g2")
                ckpt(f"t{ti}_wout")
                if ti == 0:
                    P.tag = "mem_kv"
                    mem_kv()
                P.tag = "xattn_q"
                xattn_q(subs)
                ckpt(f"t{ti}_xq")
                P.tag = "xattn_core"
                xattn_core(subs)
                ckpt(f"t{ti}_xc")
                P.tag = "proj_resid_wo"
                proj_resid(subs, P_WO, "g3")
                ckpt(f"t{ti}_wo")
                if subs[0].halo:
                    hist_save(subs[0], True)
                    continue
                prev = None
                for s in subs:
                    hist_load(s, prev)
                    prev = s
                hist_save(subs[-1], False)
                next_subs = tiles[ti + 1] if (ti + 1 < len(tiles) and ti >= 1) else None
                if next_subs is not None:
                    P.tag = "rope_norm1"
                    for s in next_subs:
                        pos_load(s)
                        rope_tables(s)
                P.tag = "ffn"
                ffn(subs, next_subs)
                if next_subs is not None:
                    for s in subs:
                        XR[s.si] = (XR[s.si] - 1) % (KC + 1)
                ckpt(f"t{ti}_ffn")

        stopped = False
        try:
            emit_all()
        except _Stop:
            stopped = True
        if not stopped:
            assert W["next_acq"] == len(seq), (W["next_acq"], len(seq))
        final_waits = sorted(P.dmacnt.items())

        sems = {}
        for k in ["pe", "act", "dve", "pool"] + sorted(P.dmacnt.keys()):
            sems[k] = es.enter_context(nc.semaphore(k))
        _CACHE["sbuf_left"] = nc.sbuf_bytes_remaining
        block = es.enter_context(nc.Block())

        def replay(en, h, final=False):
            for waits, fn, inc in P.eng[en].ops:
                for k, v in waits:
                    h.wait_ge(sems[k], v)
                ins = fn(h)
                if inc is not None:
                    ins.then_inc(sems[inc[0]], inc[1])
            if final:
                for k, v in final_waits:
                    h.wait_ge(sems[k], v)

        @block.tensor
        def _(h):
            replay("pe", h)

        @block.scalar
        def _(h):
            replay("act", h)

        @block.vector
        def _(h):
            replay("dve", h)

        @block.gpsimd
        def _(h):
            replay("pool", h)

        @block.sync
        def _(h):
            replay("sp", h, final=True)

    counts = {k: len(v.ops) for k, v in P.eng.items()}
    _CACHE["tags"] = P.tags
    return nc, counts


def _piece_k1024(Wm, cols):
    K, N = Wm.shape
    w = Wm.reshape(KC, 128, N)[:, :, cols]
    return np.ascontiguousarray(w.transpose(1, 0, 2)).reshape(128, -1)


def _build_wall(inp):
    wall = np.zeros((NPIECE, 128, PW), np.float32)
    wkv = inp["xa_wkv"][0]
    for hx in range(4):
        wall[P_WK + hx] = _piece_k1024(wkv, np.arange(hx * 256, (hx + 1) * 256))
        wall[P_WV + hx] = _piece_k1024(wkv, 1024 + np.arange(hx * 256, (hx + 1) * 256))
    w_in = inp["w_in"][0]
    order = np.concatenate([np.arange(0, 512), np.arange(512, 640), np.arange(768, 1280), np.arange(640, 768),
                            np.arange(1280, 1792)])
    for pi in range(7):
        wall[P_WIN + pi] = _piece_k1024(w_in, order[pi * 256:(pi + 1) * 256])
    for pi in range(4):
        cols = np.arange(pi * 256, (pi + 1) * 256)
        wall[P_WOUT + pi] = _piece_k1024(inp["w_out"][0], cols)
        wall[P_WQ + pi] = _piece_k1024(inp["xa_wq"][0], cols)
        wall[P_WO + pi] = _piece_k1024(inp["xa_wo"][0], cols)
    up = inp["ffn_up"][0]
    for j in range(NJ):
        cols = np.concatenate([np.arange(j * 128, (j + 1) * 128), DFF + np.arange(j * 128, (j + 1) * 128)])
        wall[P_UP + j] = _piece_k1024(up, cols)
    dn = inp["ffn_down"][0].reshape(NJ, 128, D)
    for oc in range(KC):
        for half in range(2):
            w = dn[half * 11:(half + 1) * 11, :, oc * 128:(oc + 1) * 128]
            wall[P_DOWN + 2 * oc + half, :, 0:11 * 128] = w.transpose(1, 0, 2).reshape(128, -1)
    return wall


def _cols(v, n):
    return np.ascontiguousarray(np.asarray(v, np.float32).reshape(n, 128).T)


def _build_csta(inp, flag):
    c = np.zeros((128, NCA), np.float32)

    def put(name, arr):
        lo, hi = CA[name]
        c[:, lo:hi] = arr

    put("g1", _cols(inp["mix_norm"][0], 8))
    put("g2", _cols(inp["xa_norm"][0], 8))
    put("g3", _cols(inp["ffn_norm"][0], 8))
    put("gmem", _cols(inp["mem_norm"][0], 8))
    p = np.arange(128)
    put("gq", inp["q_norm"][0][p % 64][:, None])
    put("gk", inp["k_norm"][0][p % 64][:, None])
    put("gqp", inp["q_norm"][0][(p % 64 + 32) % 64][:, None])
    put("gkp", inp["k_norm"][0][(p % 64 + 32) % 64][:, None])
    inv_freq = (1.0 / (10000.0 ** (np.arange(32, dtype=np.float32) * np.float32(2.0 / 64)))).astype(np.float32)
    put("invf", inv_freq[(p % 64) % 32][:, None])
    put("flag", np.full((128, 1), flag, np.float32))
    put("sel0", (p == 0).astype(np.float32)[:, None])
    put("sel1", (p == 1).astype(np.float32)[:, None])
    put("ga", _cols(inp["attn_out_norm"][0], 4))
    put("gmo", _cols(inp["gmlp_out_norm"][0], 4))
    put("gxq", _cols(inp["xa_q_norm"][0], 2))
    put("gxk", _cols(inp["xa_k_norm"][0], 2))
    conv = inp["ffn_conv"][0]
    cbv = inp["ffn_conv_b"][0]
    cw = np.zeros((128, 44, 3), np.float32)
    cbm = np.zeros((128, 44), np.float32)
    for j in range(NJ):
        for wh in range(2):
            ch = wh * DFF + j * 128 + p
            cw[:, 2 * j + wh, :] = conv[:, ch].T
            cbm[:, 2 * j + wh] = cbv[ch]
    put("cw", cw.reshape(128, -1))
    put("cb", cbm)
    put("halfpi", np.full((128, 1), math.pi / 2, np.float32))
    put("eps", np.full((128, 1), EPS, np.float32))
    put("gvn", np.broadcast_to(inp["gmlp_v_norm"][0][None, :], (128, 512)))
    return c


def _build_cstb(inp):
    c = np.zeros((128, NCB), np.float32)

    def put(name, arr):
        lo, hi = CB[name]
        c[:, lo:hi] = arr

    put("sinkr", np.broadcast_to(inp["attn_sinks"][0].reshape(1, -1), (128, 8)))
    s = np.arange(128)[:, None]
    t = np.arange(128)[None, :]
    put("tril", (s <= t).astype(np.float32))
    rm = np.zeros((128, 128), np.float32)
    for hh in range(2):
        for mm_ in range(64):
            if mm_ < 32:
                rm[hh * 64 + mm_ + 32, hh * 64 + mm_] = -1.0
            else:
                rm[hh * 64 + mm_ - 32, hh * 64 + mm_] = 1.0
    put("rm", rm)
    bo = np.zeros((128, 128), np.float32)
    bo[0:64, 0:64] = 1.0
    bo[64:128, 64:128] = 1.0
    put("bones", bo)
    mprev = (s > t).astype(np.float32)
    mcur = (s <= t).astype(np.float32)
    m = np.zeros((128, 2, 2, 2, 128), np.float32)
    m[:, :, 0, :, :] = mprev[:, None, None, :]
    m[:, :, 1, :, :] = mcur[:, None, None, :]
    ws = inp["gmlp_ws"][0]
    wtf = np.ascontiguousarray(ws.transpose(2, 0, 1)).reshape(128, -1)
    bs = inp["gmlp_bs"][0]
    bsr = np.zeros((128, 4, 128), np.float32)
    for c_ in range(4):
        bsr[0:64, c_, :] = bs[2 * c_][None, :]
        bsr[64:128, c_, :] = bs[2 * c_ + 1][None, :]
    bsr = bsr.reshape(128, 512)
    return c, m.reshape(128, -1), wtf, bsr


def _xT_blocks(xrows, nt):
    Tn = xrows.shape[0]
    a = xrows.reshape(Tn // nt, nt, KC, 128)
    a = a.transpose(3, 0, 2, 1)
    return np.ascontiguousarray(a).reshape(128, -1)


def _prepare_inputs(inp, n_main_sub=NSUB_MAIN):
    inp = {k: np.asarray(v) for k, v in inp.items()}
    wall = _build_wall(inp)
    cstb, c_mask, c_wtf, c_bsr = _build_cstb(inp)
    x = inp["x"]
    pos = inp["positions"]
    maps = []
    for c in range(NCORES):
        b, half = c // 2, c % 2
        s0 = half * TOK_CORE
        if half == 0:
            xh_rows = np.zeros((HALO, D), np.float32)
            ph = np.zeros((HALO,), np.int32)
        else:
            xh_rows = x[b, s0 - HALO:s0]
            ph = pos[b, s0 - HALO:s0]
        xm_rows = x[b, s0:s0 + TOK_CORE]
        posr = np.concatenate([ph, pos[b, s0:s0 + TOK_CORE]]).astype(np.int32)
        maps.append({
            "xh": _xT_blocks(xh_rows, HALO),
            "xm": _xT_blocks(xm_rows, NTM),
            "posr": np.ascontiguousarray(np.broadcast_to(posr[None, :], (128, HALO + TOK_CORE))),
            "memT": _xT_blocks(inp["mem"][b], 256),
            "wall": wall,
            "csta": _build_csta(inp, 1.0 if half == 1 else 0.0),
            "cstb": cstb,
            "c_mask": c_mask,
            "c_wtf": c_wtf,
            "c_bsr": c_bsr,
        })
    return maps


def _assemble(results):
    out = np.zeros((BATCH, SEQ, D), np.float32)
    for c in range(NCORES):
        b, half = c // 2, c % 2
        om = np.asarray(results[c]["om"]).reshape(128, NSUB_MAIN, KC, NTM)
        rows = om.transpose(1, 3, 2, 0).reshape(TOK_CORE, D)
        out[b, half * TOK_CORE:(half + 1) * TOK_CORE] = rows
    return out


def kernel(**inputs):
    if "nc" not in _CACHE:
        _CACHE["nc"] = build_program()[0]
    nc = _CACHE["nc"]
    maps = _prepare_inputs(inputs)
    res = run_bass_kernel_spmd(nc, maps, core_ids=list(range(NCORES)))
    return _assemble(res.results)
```

```python
import math
import contextlib
import numpy as np
import concourse.bass as bass
import concourse.mybir as mybir
from concourse.bass_utils import run_bass_kernel_spmd

F32 = mybir.dt.float32
BF16 = mybir.dt.bfloat16
I32 = mybir.dt.int32
AF = mybir.ActivationFunctionType
ALU = mybir.AluOpType

D = 1024
KC = 8
SEQ = 8192
BATCH = 4
NCORES = 8
TOK_CORE = 4096
HALO = 256
NBM = 2
NTM = NBM * 128
NSUB_MAIN = TOK_CORE // NTM
DFF = 2816
NJ = 22
EPS = 1e-6
NS = 8
POOL_TT = "dve"
QK_LAG = 1
ATT_LAG = 2
MASK_ENG = "dve"
GM_ENG = "dve"
N_ACC = 5
A_FIRST = True
FFN_POOL = "pool"
PW = 2048

P_WIN, P_WOUT, P_WK, P_WV, P_WQ, P_WO, P_UP, P_DOWN = 0, 7, 11, 15, 19, 23, 27, 49
NPIECE = 65
CONV_LA = 4


def _cst_layout():
    off = {}
    n = 0
    for name, w in [("g1", 8), ("g2", 8), ("g3", 8), ("gmem", 8), ("gq", 1), ("gk", 1), ("gqp", 1), ("gkp", 1), ("invf", 1),
                    ("flag", 1), ("sel0", 1), ("sel1", 1), ("ga", 4), ("gmo", 4), ("gxq", 2), ("gxk", 2),
                    ("cw", 44 * 3), ("cb", 44), ("halfpi", 1), ("eps", 1), ("gvn", 512)]:
        off[name] = (n, n + w)
        n += w
    return off, n


CA, NCA = _cst_layout()


def _cstb_layout():
    off = {}
    n = 0
    for name, w in [("sinkr", 8), ("tril", 128), ("rm", 128), ("bones", 128)]:
        off[name] = (n, n + w)
        n += w
    return off, n


CB, NCB = _cstb_layout()


class Tk:
    __slots__ = ("name", "w", "r", "excl")

    def __init__(self, name, excl=False):
        self.name = name
        self.w = None
        self.r = {}
        self.excl = excl


class Eng:
    def __init__(self, name):
        self.name = name
        self.ops = []
        self.cnt = 0
        self.seen = {}


class Prog:
    def __init__(self):
        self.eng = {n: Eng(n) for n in ("pe", "act", "dve", "pool", "sp")}
        self.dmacnt = {}
        self.tag = ""
        self.tags = {n: [] for n in self.eng}

    def op(self, en, fn, reads=(), writes=(), signal=True, dma=None):
        e = self.eng[en]
        need = {}

        def req(ev):
            if ev is None:
                return
            k, v = ev
            if need.get(k, 0) < v:
                need[k] = v

        for b in reads:
            req(b.w)
            if b.excl:
                for k, v in b.r.items():
                    if k != en:
                        req((k, v))
        for b in writes:
            req(b.w)
            for k, v in b.r.items():
                req((k, v))
        waits = []
        for k, v in need.items():
            if k == "pe" and en == "pe":
                continue
            if e.seen.get(k, 0) < v:
                e.seen[k] = v
                waits.append((k, v))
        if dma is not None:
            self.dmacnt[dma] = self.dmacnt.get(dma, 0) + 16
            ev = (dma, self.dmacnt[dma])
            inc = (dma, 16)
        elif signal:
            e.cnt += 1
            ev = (en, e.cnt)
            inc = (en, 1)
        else:
            ev = (en, e.cnt + 1)
            inc = None
        e.ops.append((waits, fn, inc))
        self.tags[en].append(self.tag)
        for b in reads:
            if b.r.get(ev[0], 0) < ev[1]:
                b.r[ev[0]] = ev[1]
        for b in writes:
            b.w = ev
            b.r = {}
        return ev


class Sub:
    def __init__(self, si, nb, halo, first_real, tok0, pos0, slot0, idx):
        self.si = si
        self.nb = nb
        self.nt = nb * 128
        self.halo = halo
        self.first_real = first_real
        self.tok0 = tok0
        self.pos0 = pos0
        self.slot0 = slot0
        self.idx = idx


_CACHE = {}


class _Stop(Exception):
    pass


def build_program(n_main_sub=NSUB_MAIN, stop=None):
    nc = bass.Bass("TRN2", target_bir_lowering=False)
    P = Prog()
    dr = {}
    dr["xh"] = nc.dram_tensor("xh", [128, KC * HALO], F32, kind="ExternalInput").ap()
    dr["xm"] = nc.dram_tensor("xm", [128, KC * TOK_CORE], F32, kind="ExternalInput").ap()
    dr["posr"] = nc.dram_tensor("posr", [128, HALO + TOK_CORE], I32, kind="ExternalInput").ap()
    dr["memT"] = nc.dram_tensor("memT", [128, KC * 256], F32, kind="ExternalInput").ap()
    dr["wall"] = nc.dram_tensor("wall", [NPIECE, 128, PW], F32, kind="ExternalInput").ap()
    dr["csta"] = nc.dram_tensor("csta", [128, NCA], F32, kind="ExternalInput").ap()
    dr["cstb"] = nc.dram_tensor("cstb", [128, NCB], F32, kind="ExternalInput").ap()
    dr["c_mask"] = nc.dram_tensor("c_mask", [128, 1024], F32, kind="ExternalInput").ap()
    dr["c_wtf"] = nc.dram_tensor("c_wtf", [128, 1024], F32, kind="ExternalInput").ap()
    dr["c_bsr"] = nc.dram_tensor("c_bsr", [128, 512], F32, kind="ExternalInput").ap()
    dr["om"] = nc.dram_tensor("om", [128, KC * TOK_CORE], F32, kind="ExternalOutput").ap()
    wsc = nc.dram_tensor("wsc", [NPIECE, 128, PW], BF16).ap()

    es = contextlib.ExitStack()
    with es:
        def sb(name, shape, dt):
            return es.enter_context(nc.sbuf_tensor(name, shape, dt))

        ring = sb("ring", [128, NS, PW], BF16)
        csta = sb("csta_s", [128, NCA], F32)
        ps = es.enter_context(nc.psum_tensor("ps", [128, 8, 512], F32))
        ones = sb("ones", [128, 128], BF16)
        bones = sb("bones", [128, 128], BF16)
        rmat = sb("rmat", [128, 128], BF16)
        maskN = sb("maskN", [128, 1024], BF16)
        osb = [sb(f"osb{i}", [128, 256], F32) for i in range(2)]
        wtb = sb("wtb", [128, 8, 128], BF16)
        es_bc = sb("es_bc", [128, 2, 2, 128], F32)
        bias_bc = sb("bias_bc", [128, 4, 128], F32)
        e128 = sb("e128", [128, 8], F32)
        mxt = sb("mxt", [128, 4, 128], F32)
        hist = sb("hist", [128, KC, 2], BF16)
        kmT = sb("kmT", [128, 8, 256], BF16)
        vm = sb("vm", [128, 2, 1024], BF16)
        NSLOT = 1 + 2 * NBM
        KD = sb("KD", [128, 2, NSLOT * 128], BF16)
        VV = sb("VV", [128, NSLOT, 128], BF16)
        xT = [sb(f"xT{i}", [128, KC + 1, NTM], F32) for i in range(2)]
        hT = [sb(f"hT{i}", [128, KC, 2 + NTM], BF16) for i in range(2)]
        yT = [sb(f"yT{i}", [128, KC, NTM], BF16) for i in range(2)]
        sq_one = sb("sq", [128, KC, NTM], BF16)
        sq = [sq_one, sq_one]
        cs_t = [sb(f"cs{i}", [128, 2, NTM], F32) for i in range(2)]
        lnt = [sb(f"lnt{i}", [128, NTM], F32) for i in range(2)]
        rstd = [sb(f"rstd{i}", [128, NTM], F32) for i in range(2)]
        rstd2 = [sb(f"rstdb{i}", [128, NTM], F32) for i in range(2)]
        qn = [sb(f"qn{i}", [128, 4, NTM], BF16) for i in range(2)]
        guT = [sb(f"guT{i}", [128, 4, NTM], F32) for i in range(2)]
        gvraw = [sb(f"gvraw{i}", [128, NBM, 512], F32) for i in range(2)]
        gvn = [sb(f"gvn{i}", [128, NBM, 512], BF16) for i in range(2)]
        gvst = [sb(f"gvst{i}", [128, 8], F32) for i in range(2)]
        attnT = [sb(f"attnT{i}", [128, 4, NTM], F32) for i in range(2)]
        gmT = [sb(f"gmT{i}", [128, 4, NTM], F32) for i in range(2)]
        rtmp = attnT
        posi = [lnt[i][:].bitcast(I32) for i in range(2)]
        gT_all = sb("gT", [128, 2, NJ, NTM], BF16)
        NR = 3
        sqc = [sb(f"sqc{i}", [128, NTM], BF16) for i in range(NR)]
        rawb = [sb(f"rawb{i}", [128, NTM], BF16) for i in range(NR)]
        lnc = [sb(f"lnc{i}", [128, NTM], F32) for i in range(NR)]
        rsc = [sb(f"rsc{i}", [128, NTM], F32) for i in range(NR)]
        t1 = [sb(f"t1_{i}", [128, NTM], F32) for i in range(NR)]
        t2 = [sb(f"t2_{i}", [128, NTM], F32) for i in range(NR)]
        PT = [sb(f"PT{i}", [128, 2, 512], BF16) for i in range(3)]
        rec = [sb(f"rec{i}", [128, 256], F32) for i in range(2)]
        PTx = [sb(f"PTx{i}", [128, 2, NTM], BF16) for i in range(3)]
        recx = [sb(f"recx{i}", [128, NTM], F32) for i in range(3)]
        cvb = [sb(f"cvb{i}", [128, 2, 2, NTM], F32) for i in range(2)]
        junk = sb("junk", [128, 512], BF16)

        tk = {}

        def T(name, excl=False):
            if name not in tk:
                tk[name] = Tk(name, excl)
            return tk[name]

        pb = [T(f"ps{b}", True) for b in range(8)]

        def ca(name, a=None, b=None):
            lo, hi = CA[name]
            if a is None:
                return csta[:, lo:hi]
            return csta[:, lo + a:lo + (b if b is not None else a + 1)]

        DEF = []

        def defer(n, fn):
            if n <= 0:
                fn()
            else:
                DEF.append([n, fn])

        def flush_def():
            while DEF:
                n, fn = DEF.pop(0)
                fn()

        def tick():
            due = []
            for it in DEF:
                it[0] -= 1
            while DEF and DEF[0][0] <= 0:
                due.append(DEF.pop(0)[1])
            for fn in due:
                fn()

        def mm(out, lhsT, rhs, start, stop, reads, writes, signal=None, notick=False):
            if signal is None:
                signal = stop
            P.op("pe", lambda e: e.matmul(out, lhsT, rhs, start=start, stop=stop), reads, writes, signal)
            if stop and signal and not notick:
                tick()

        def act(out, in_, func, reads, writes, scale=None, bias=None, accum=None):
            kw = {}
            if scale is not None:
                kw["scale"] = scale
            if bias is not None:
                kw["bias"] = bias
            if accum is not None:
                kw["accum_out"] = accum
            P.op("act", lambda e: e.activation(out=out, in_=in_, func=func, **kw), reads, writes)

        def tt(en, out, in0, in1, op, reads, writes):
            P.op(en, lambda e: e.tensor_tensor(out=out, in0=in0, in1=in1, op=op), reads, writes)

        def ts(en, out, in0, s1, s2, op0, op1, reads, writes):
            if op1 is None:
                P.op(en, lambda e: e.tensor_scalar(out=out, in0=in0, scalar1=s1, scalar2=None, op0=op0), reads, writes)
            else:
                P.op(en, lambda e: e.tensor_scalar(out=out, in0=in0, scalar1=s1, scalar2=s2, op0=op0, op1=op1),
                     reads, writes)

        def stt(en, out, in0, scalar, in1, op0, op1, reads, writes):
            P.op(en, lambda e: e.scalar_tensor_tensor(out=out, in0=in0, scalar=scalar, in1=in1, op0=op0, op1=op1),
                 reads, writes)

        def cp(en, out, in_, reads, writes):
            P.op(en, lambda e: e.tensor_copy(out=out, in_=in_), reads, writes)

        def recip(out, in_, reads, writes):
            P.op("dve", lambda e: e.reciprocal(out=out, in_=in_), reads, writes)

        def memset(en, ap, val, writes):
            P.op(en, lambda e: e.memset(ap, val), (), writes)

        def dma(en, out, in_, reads, writes, sem):
            P.op(en, lambda e: e.dma_start(out=out, in_=in_), reads, writes, dma=sem)

        rot = {"acc": 0, "aux": 0, "r2": 0, "pt": 0, "ptx": 0, "cv": 0, "rec": 0, "fp": 0, "xq": 0, "xqs": 0}

        def nxt(key, n):
            v = rot[key]
            rot[key] = (v + 1) % n
            return v

        def acc_bank():
            return nxt("acc", N_ACC)

        def aux_bank():
            return N_ACC + nxt("aux", 8 - N_ACC)

        scr_tk = [T(f"scr{i}") for i in range(NPIECE)]
        slot_tk = [T(f"slot{i}") for i in range(NS)]

        seq = list(range(P_WIN, P_UP))
        ntile = (n_main_sub + 1) // 2
        for _ in range(ntile):
            seq += list(range(P_WIN, P_WK)) + list(range(P_WQ, NPIECE))
        W = {"next_load": 0, "next_acq": 0, "released": set(), "conv_next": 0, "nrel": 0}

        def w_pump():
            while W["next_load"] < len(seq):
                j = W["next_load"]
                if j >= NS and (j - NS) not in W["released"]:
                    break
                s = j % NS
                if scr_tk[seq[j]].w is None:
                    break
                dma("sp", ring[:, s, :], wsc[seq[j]], [scr_tk[seq[j]]], [slot_tk[s]], f"ring{s}")
                W["next_load"] += 1

        def w_acquire(expect):
            j = W["next_acq"]
            assert seq[j] == expect, (j, seq[j], expect)
            while scr_tk[expect].w is None:
                conv_issue_pair(W["conv_next"], [])
            w_pump()
            assert W["next_load"] > j, "weight ring too small for schedule"
            W["next_acq"] += 1
            s = j % NS
            return j, ring[:, s, :], slot_tk[s]

        def conv_issue_pair(i, reads):
            grp = [k for k in (i, i + 1) if k < NPIECE]
            for k in grp:
                dma("pool", wsc[k], dr["wall"][k], reads, [scr_tk[k]], f"conv{i // 2}")
            ev = scr_tk[grp[-1]].w
            for k in grp:
                scr_tk[k].w = ev
            W["conv_next"] = i + 2

        def w_release(j):
            W["released"].add(j)
            W["nrel"] += 1
            if W["conv_next"] < NPIECE and W["nrel"] % 2 == 0:
                conv_issue_pair(W["conv_next"], [slot_tk[j % NS]])
            w_pump()

        def emit_all():
            for i in range(0, min(CONV_LA, NPIECE), 2):
                conv_issue_pair(i, [])

            def ckpt(name):
                if stop == name:
                    raise _Stop()

            XR = [0, 0]
            t_xp = [[T(f"x{i}_{c}") for c in range(KC + 1)] for i in range(2)]

            class _XTk:
                def __init__(self, si):
                    self.si = si

                def __getitem__(self, k):
                    if isinstance(k, slice):
                        return [self[i] for i in range(*k.indices(KC))]
                    return t_xp[self.si][(k + XR[self.si]) % (KC + 1)]

                def __iter__(self):
                    return iter([self[i] for i in range(KC)])

                def __len__(self):
                    return KC

                def __add__(self, other):
                    return list(self) + list(other)

                def __radd__(self, other):
                    return list(other) + list(self)

            t_x = [_XTk(0), _XTk(1)]

            def xc(si, oc):
                return xT[si][:, (oc + XR[si]) % (KC + 1), :]

            def x_pieces(si):
                r = XR[si]
                if r + KC <= KC + 1:
                    return [(0, KC, r)]
                n1 = KC + 1 - r
                return [(0, n1, r), (n1, KC - n1, 0)]

            t_y = [[T(f"y{i}_{c}") for c in range(KC)] for i in range(2)]
            t_sq1 = [T(f"sq_{c}") for c in range(KC)]
            t_sq = [t_sq1, t_sq1]
            t_ln = [T("ln0"), T("ln1")]
            t_rs = [T("rs0"), T("rs1")]
            t_rs2 = [T("rsb0"), T("rsb1")]
            t_gu = [[T(f"gu{i}_{c}") for c in range(4)] for i in range(2)]
            t_at = [[T(f"at{i}_{b}") for b in range(NBM)] for i in range(2)]
            t_gm = [[T(f"gm{i}_{b}") for b in range(NBM)] for i in range(2)]

            t_csta = T("csta")
            dma("sp", csta[:], dr["csta"], [], [t_csta], "cstA")
            st_mask = xT[0][:, 0:4, :].rearrange("p k t -> p (k t)")
            st_wtf = xT[0][:, 4:8, :].rearrange("p k t -> p (k t)")
            st_small = gmT[1][:].rearrange("p k t -> p (k t)")[:, 0:NCB]
            dma("sp", st_mask, dr["c_mask"], [], t_x[0][0:4], "cstB")
            dma("sp", st_wtf, dr["c_wtf"], [], t_x[0][4:8], "cstC")
            dma("sp", bias_bc[:].rearrange("p c t -> p (c t)"), dr["c_bsr"], [], [T("bias_bc")], "cstD")
            dma("sp", st_small, dr["cstb"], [], t_gm[1], "cstE")
            w_pump()

            def smallc(name):
                lo, hi = CB[name]
                return st_small[:, lo:hi]

            t_const = T("const")
            memset("dve", ones[:], 1.0, [t_const])
            cp("dve", bones[:], smallc("bones"), t_gm[1], [t_const])
            cp("dve", rmat[:], smallc("rm"), t_gm[1], [t_const])
            cp("dve", maskN[:], st_mask, t_x[0][0:4], [t_const])
            wtf = st_wtf.rearrange("p (h t) -> p h t", h=8)
            tril_b = smallc("tril").unsqueeze(1).broadcast_to([128, 8, 128])
            tt("dve", wtb[:], wtf, tril_b, ALU.mult, t_x[0][4:8] + t_gm[1], [t_const])
            lo_s, hi_s = CB["sinkr"]
            t_e128 = T("e128")
            act(e128[:], st_small[:, lo_s:hi_s], AF.Exp, t_gm[1], [t_e128])
            for par in range(2):
                rows = slice(par * 64, (par + 1) * 64)
                for g in range(2):
                    for cc in range(2):
                        hd = 4 * g + 2 * cc + par
                        cp("dve", es_bc[rows, g, cc, :], e128[rows, hd:hd + 1].broadcast_to([64, 128]), [t_e128],
                           [t_const])
            ckpt("conv")
            t_kd = [T(f"kd{j}") for j in range(NSLOT)]
            t_vv = [T(f"vv{j}") for j in range(NSLOT)]
            memset("pool", KD[:, :, 0:128], 0.0, [t_kd[0]])
            memset("pool", VV[:, 0, :], 0.0, [t_vv[0]])
            t_h = [[T(f"h{i}_{k}") for k in range(KC)] for i in range(2)]
            memset("pool", hT[0][:, :, 0:2], 0.0, t_h[0])
            memset("pool", hT[1][:, :, 0:2], 0.0, t_h[1])

            def norm_tail_ops(si, nt, gname, b, hist_off=2):
                act(lnt[si][:, 0:nt], ps[:, b, 0:nt], AF.Ln, [pb[b], t_csta], [t_ln[si]], scale=1.0 / D,
                    bias=ca("eps"))
                act(rstd[si][:, 0:nt], lnt[si][:, 0:nt], AF.Exp, [t_ln[si]], [t_rs[si]], scale=-0.5)
                for kc in range(KC):
                    stt("dve", hT[si][:, kc, hist_off:hist_off + nt], xc(si, kc)[:, 0:nt], ca(gname, kc),
                        rstd[si][:, 0:nt], ALU.mult, ALU.mult, [t_x[si][kc], t_rs[si], t_csta], [t_h[si][kc]])

            def norm_full(si, nt, gname, hist_off=2, lag=0):
                for (c0, n, p0) in x_pieces(si):
                    act(sq[si][:, c0:c0 + n, 0:nt], xT[si][:, p0:p0 + n, 0:nt], AF.Square, t_x[si][c0:c0 + n],
                        t_sq[si][c0:c0 + n])

                def tail():
                    b = aux_bank()
                    for kc in range(KC):
                        mm(ps[:, b, 0:nt], ones[:], sq[si][:, kc, 0:nt], kc == 0, kc == KC - 1,
                           [t_const, t_sq[si][kc]], [pb[b]], notick=True)
                    norm_tail_ops(si, nt, gname, b, hist_off)

                defer(lag, tail)

            def norm1_chunk(ns_, oc, nb_, slot):
                si, nt = ns_.si, ns_.nt
                act(sq[si][:, oc, 0:nt], xT[si][:, slot, 0:nt], AF.Square, [t_xp[si][slot]], [t_sq[si][oc]])
                mm(ps[:, nb_, 0:nt], ones[:], sq[si][:, oc, 0:nt], oc == 0, oc == KC - 1, [t_const, t_sq[si][oc]],
                   [pb[nb_]], signal=True, notick=True)

            t_km = T("kmT")
            t_vm = T("vm")

            def mem_kv():
                assert XR[1] == 0
                dma("sp", xT[1][:, 0:KC, :].rearrange("p k t -> p (k t)"), dr["memT"], [], t_x[1], "xl1")
                norm_full(1, 256, "gmem")
                t_km = T("kmT")
                t_vm = T("vm")
                for hx in range(4):
                    j, wp, wt = w_acquire(P_WK + hx)
                    w3 = wp.rearrange("p (k c) -> p k c", k=KC)
                    bks = []
                    for dc in range(2):
                        b = acc_bank()
                        bks.append(b)
                        for kc in range(KC):
                            mm(ps[:, b, 0:256], w3[:, kc, dc * 128:(dc + 1) * 128], hT[1][:, kc, 2:258], kc == 0, kc == KC - 1,
                               [wt, t_h[1][kc]], [pb[b]])
                        act(sq[1][:, dc, 0:256], ps[:, b, 0:256], AF.Square, [pb[b]], [t_sq[1][dc]])
                    w_release(j)
                    bs_ = aux_bank()
                    for dc in range(2):
                        mm(ps[:, bs_, 0:256], ones[:], sq[1][:, dc, 0:256], dc == 0, dc == 1, [t_const, t_sq[1][dc]], [pb[bs_]])
                    r = nxt("r2", NR)
                    act(lnc[r][:, 0:256], ps[:, bs_, 0:256], AF.Ln, [pb[bs_], t_csta], [T(f"lnc{r}")], scale=1.0 / 256,
                        bias=ca("eps"))
                    act(rsc[r][:, 0:256], lnc[r][:, 0:256], AF.Exp, [T(f"lnc{r}")], [T(f"rsc{r}")], scale=-0.5)
                    for dc in range(2):
                        stt("dve", kmT[:, 2 * hx + dc, :], ps[:, bks[dc], 0:256], ca("gxk", dc), rsc[r][:, 0:256], ALU.mult,
                            ALU.mult, [pb[bks[dc]], T(f"rsc{r}"), t_csta], [t_km])
                for pv in range(4):
                    j, wp, wt = w_acquire(P_WV + pv)
                    w3 = wp.rearrange("p (k c) -> p k c", k=KC)
                    for mb in range(2):
                        b = acc_bank()
                        for kc in range(KC):
                            mm(ps[:, b, 0:256], hT[1][:, kc, 2 + mb * 128:2 + (mb + 1) * 128], w3[:, kc, :], kc == 0,
                               kc == KC - 1, [wt, t_h[1][kc]], [pb[b]])
                        cp("dve", vm[:, mb, pv * 256:(pv + 1) * 256], ps[:, b, 0:256], [pb[b]], [t_vm])
                    w_release(j)


            ckpt("mem")
            t_pos = t_ln
            t_cs = [T("cs0"), T("cs1")]
            t_qn = [[T(f"qn{i}_{c}") for c in range(4)] for i in range(2)]
            t_gvr = [[T(f"gvr{i}_{b}") for b in range(NBM)] for i in range(2)]
            t_gvn = [[T(f"gvn{i}_{b}") for b in range(NBM)] for i in range(2)]
            t_gvs = [T("gvs0"), T("gvs1")]
            t_gT = [[T(f"gT{i}_{j}") for j in range(NJ)] for i in range(2)]
            MAGIC = 12582912.0
            C1 = 6.28125
            C2 = 2 * math.pi - 6.28125

            def pos_load(s, en="sp"):
                dma(en, posi[s.si][:, 0:s.nt], dr["posr"][:, s.pos0:s.pos0 + s.nt], [], [t_pos[s.si]], f"pl{s.si}")

            def load_sub(s):
                si, nt = s.si, s.nt
                assert nt == NTM
                src = dr["xh"] if s.halo else dr["xm"][:, KC * s.tok0:KC * (s.tok0 + nt)]
                assert XR[si] == 0
                dma("act", xT[si][:, 0:KC, :].rearrange("p k t -> p (k t)"), src, [], t_x[si], f"xl{si}")
                pos_load(s, "act")

            def rope_tables(s):
                si, nt = s.si, s.nt
                ang = rtmp[si][:, 0, 0:nt]
                kk = rtmp[si][:, 1, 0:nt]
                ab = rtmp[si][:, 2, 0:nt]
                cp("dve", kk, posi[si][:, 0:nt], [t_pos[si]], t_at[si])
                ts("dve", ang, kk, ca("invf"), None, ALU.mult, None, t_at[si] + [t_csta], t_at[si])
                ts("dve", kk, ang, 1.0 / (2 * math.pi), MAGIC, ALU.mult, ALU.add, t_at[si], t_at[si])
                ts("dve", kk, kk, -MAGIC, None, ALU.add, None, t_at[si], t_at[si])
                stt("dve", ang, kk, -C1, ang, ALU.mult, ALU.add, t_at[si], t_at[si])
                stt("dve", ang, kk, -C2, ang, ALU.mult, ALU.add, t_at[si], t_at[si])
                act(ab, ang, AF.Abs, t_at[si], t_at[si])
                act(cs_t[si][:, 0, 0:nt], ab, AF.Sin, t_at[si] + [t_csta], [t_cs[si]], scale=-1.0, bias=ca("halfpi"))
                act(cs_t[si][:, 1, 0:nt], ang, AF.Sin, t_at[si], [t_cs[si]])

            def post_qk(s, ci, b):
                si, nt = s.si, s.nt
                r = nxt("r2", NR)
                tq, tr, tl, trs, tt1, tt2 = (T(f"sqc{r}"), T(f"rawb{r}"), T(f"lnc{r}"), T(f"rsc{r}"), T(f"t1{r}"),
                                             T(f"t2{r}"))
                raw = ps[:, b, 0:nt]
                act(sqc[r][:, 0:nt], raw, AF.Square, [pb[b]], [tq])
                act(rawb[r][:, 0:nt], raw, AF.Copy, [pb[b]], [tr])
                defer(QK_LAG, lambda: post_qk_tail(s, ci, b, r))

            def post_qk_tail(s, ci, b, r):
                si, nt = s.si, s.nt
                tq, tr, tl, trs, tt1, tt2 = (T(f"sqc{r}"), T(f"rawb{r}"), T(f"lnc{r}"), T(f"rsc{r}"), T(f"t1{r}"),
                                             T(f"t2{r}"))
                raw = ps[:, b, 0:nt]
                b1 = aux_bank()
                mm(ps[:, b1, 0:nt], bones[:], sqc[r][:, 0:nt], True, True, [t_const, tq], [pb[b1]], notick=True)
                b2 = aux_bank()
                mm(ps[:, b2, 0:nt], rmat[:], rawb[r][:, 0:nt], True, True, [t_const, tr], [pb[b2]], notick=True)
                act(lnc[r][:, 0:nt], ps[:, b1, 0:nt], AF.Ln, [pb[b1], t_csta], [tl], scale=1.0 / 64, bias=ca("eps"))
                act(rsc[r][:, 0:nt], lnc[r][:, 0:nt], AF.Exp, [tl], [trs], scale=-0.5)
                gn, gpn = ("gq", "gqp") if ci < 4 else ("gk", "gkp")
                stt("dve", t1[r][:, 0:nt], raw, ca(gn), cs_t[si][:, 0, 0:nt], ALU.mult, ALU.mult,
                    [pb[b], t_cs[si], t_csta], [tt1])
                stt("dve", t2[r][:, 0:nt], ps[:, b2, 0:nt], ca(gpn), cs_t[si][:, 1, 0:nt], ALU.mult, ALU.mult,
                    [pb[b2], t_cs[si], t_csta], [tt2])
                tt(POOL_TT, t1[r][:, 0:nt], t1[r][:, 0:nt], t2[r][:, 0:nt], ALU.add, [tt1, tt2], [tt1])
                if ci < 4:
                    tt(POOL_TT, qn[si][:, ci, 0:nt], t1[r][:, 0:nt], rsc[r][:, 0:nt], ALU.mult, [tt1, trs],
                       [t_qn[si][ci]])
                else:
                    c0 = s.slot0 * 128
                    slots = [t_kd[s.slot0 + bb] for bb in range(s.nb)]
                    for (g, rows, orows) in ((0, slice(0, 64), slice(0, 64)), (0, slice(0, 64), slice(64, 128)),
                                             (1, slice(64, 128), slice(0, 64)), (1, slice(64, 128), slice(64, 128))):
                        tt("dve", KD[orows, g, c0:c0 + nt], t1[r][rows, 0:nt], rsc[r][rows, 0:nt], ALU.mult,
                           [tt1, trs], slots)

            def mixer_proj(subs):
                def do_piece(pi, wp, wt, s):
                    w3 = wp.rearrange("p (k c) -> p k c", k=KC)
                    if True:
                        si, nt = s.si, s.nt
                        h = hT[si]
                        if pi <= 4:
                            for jj in range(2):
                                ch = 2 * pi + jj
                                if ch <= 8:
                                    b = acc_bank()
                                    for kc in range(KC):
                                        mm(ps[:, b, 0:nt], w3[:, kc, jj * 128:(jj + 1) * 128], h[:, kc, 2:2 + nt], kc == 0,
                                           kc == KC - 1, [wt, t_h[si][kc]], [pb[b]])
                                    if ch <= 4:
                                        post_qk(s, ch, b)
                                    else:
                                        c = ch - 5
                                        act(guT[si][:, c, 0:nt], ps[:, b, 0:nt], AF.Gelu_apprx_tanh, [pb[b]],
                                            [t_gu[si][c]])
                                else:
                                    for bb in range(s.nb):
                                        b = acc_bank()
                                        for kc in range(KC):
                                            mm(ps[:, b, 0:128], h[:, kc, 2 + bb * 128:2 + (bb + 1) * 128],
                                               w3[:, kc, 128:256], kc == 0, kc == KC - 1, [wt, t_h[si][kc]], [pb[b]])
                                        cp("dve", VV[:, s.slot0 + bb, :], ps[:, b, 0:128], [pb[b]], [t_vv[s.slot0 + bb]])
                        else:
                            half = pi - 5
                            for bb in range(s.nb):
                                b = acc_bank()
                                for kc in range(KC):
                                    mm(ps[:, b, 0:256], h[:, kc, 2 + bb * 128:2 + (bb + 1) * 128], w3[:, kc, :], kc == 0,
                                       kc == KC - 1, [wt, t_h[si][kc]], [pb[b]])
                                act(gvraw[si][:, bb, half * 256:(half + 1) * 256], ps[:, b, 0:256], AF.Gelu_apprx_tanh,
                                    [pb[b]], [t_gvr[si][bb]])

                first = 0
                if len(subs) == 2 and A_FIRST:
                    held = [w_acquire(P_WIN + pi) for pi in (0, 1)]
                    for s in subs:
                        for pi in (0, 1):
                            do_piece(pi, held[pi][1], held[pi][2], s)
                    for pi in (0, 1):
                        w_release(held[pi][0])
                    first = 2
                for pi in range(first, 7):
                    j, wp, wt = w_acquire(P_WIN + pi)
                    for s in subs:
                        do_piece(pi, wp, wt, s)
                    w_release(j)
                    ckpt(f"p{pi}")
                flush_def()
                for s in subs:
                    si = s.si
                    for bb in range(s.nb):
                        act(junk[:, 0:512], gvraw[si][:, bb, :], AF.Square, [t_gvr[si][bb]], [T("junk"), t_gvs[si]],
                            accum=gvst[si][:, bb:bb + 1])
                    act(gvst[si][:, 4:4 + s.nb], gvst[si][:, 0:s.nb], AF.Ln, [t_gvs[si], t_csta], [t_gvs[si]],
                        scale=1.0 / 512, bias=ca("eps"))
                    act(gvst[si][:, 0:s.nb], gvst[si][:, 4:4 + s.nb], AF.Exp, [t_gvs[si]], [t_gvs[si]], scale=-0.5)
                    for bb in range(s.nb):
                        stt("dve", gvn[si][:, bb, :], gvraw[si][:, bb, :], gvst[si][:, bb:bb + 1], ca("gvn"), ALU.mult,
                            ALU.mult, [t_gvr[si][bb], t_gvs[si], t_csta], [t_gvn[si][bb]])

            def swa_S(u):
                s, bb, g = u["s"], u["bb"], u["g"]
                si = s.si
                sc = s.slot0 + bb
                bk = (4 * si, 4 * si + 1)
                for par in range(2):
                    rows = slice(par * 64, (par + 1) * 64)
                    for kbi, slot in enumerate((sc - 1, sc)):
                        mm(ps[:, bk[par], kbi * 256:(kbi + 1) * 256].rearrange("p (c q) -> p c q", c=2),
                           KD[rows, g, slot * 128:(slot + 1) * 128],
                           qn[si][rows, 2 * g:2 * g + 2, bb * 128:(bb + 1) * 128], True, True,
                           [t_kd[slot], t_qn[si][2 * g], t_qn[si][2 * g + 1]], [pb[bk[par]]], signal=(kbi == 1),
                           notick=True)
                u["p"] = nxt("pt", 3)

            def swa_E(u):
                si = u["s"].si
                bk = (4 * si, 4 * si + 1)
                p = u["p"]
                act(PT[p][:], ps[:, bk[0]:bk[0] + 2, :], AF.Exp, [pb[bk[0]], pb[bk[1]]], [T(f"PT{p}")], scale=0.125)

            def swa_M(u):
                s, bb = u["s"], u["bb"]
                p = u["p"]
                tp = T(f"PT{p}")
                tt(MASK_ENG, PT[p][:].rearrange("p a c -> p (a c)"), PT[p][:].rearrange("p a c -> p (a c)"), maskN[:],
                   ALU.mult, [tp, t_const], [tp])
                if s.first_real and bb == 0:
                    for par in range(2):
                        ts("dve", PT[p][:, par, 0:256], PT[p][:, par, 0:256], ca("flag"), None, ALU.mult, None,
                           [tp, t_csta], [tp])

            def swa_PV(u):
                s, bb, g, p = u["s"], u["bb"], u["g"], u["p"]
                si = s.si
                sc = s.slot0 + bb
                bo = 4 * si + 2
                tp = T(f"PT{p}")
                od = ps[:, bo, :].rearrange("p (a c q) -> p a c q", a=2, c=2)
                for par in range(2):
                    rows = slice(par * 64, (par + 1) * 64)
                    for kbi, slot in enumerate((sc - 1, sc)):
                        mm(od[rows, 0, :, :], VV[:, slot, g * 64:(g + 1) * 64],
                           PT[p][:, par, kbi * 256:(kbi + 1) * 256].rearrange("p (c q) -> p c q", c=2),
                           kbi == 0, kbi == 1, [t_vv[slot], tp], [pb[bo]], signal=False, notick=True)
                    for kbi, slot in enumerate((sc - 1, sc)):
                        mm(od[rows, 1, :, :], ones[:, 0:64],
                           PT[p][:, par, kbi * 256:(kbi + 1) * 256].rearrange("p (c q) -> p c q", c=2),
                           kbi == 0, kbi == 1, [t_const, tp], [pb[bo]], signal=(par == 1 and kbi == 1), notick=True)
                u["r"] = nxt("rec", 2)

            def swa_C(u):
                if u["two"]:
                    return
                s, g, r = u["s"], u["g"], u["r"]
                bo = 4 * s.si + 2
                od = ps[:, bo, :].rearrange("p (a c q) -> p a c q", a=2, c=2)
                os3 = osb[r][:].rearrange("p (c q) -> p c q", c=2)
                act(os3, od[:, 0, :, :], AF.Copy, [pb[bo]], [T(f"osb{r}")])

            def swa_A(u):
                s, g, r = u["s"], u["g"], u["r"]
                bo = 4 * s.si + 2
                od = ps[:, bo, :].rearrange("p (a c q) -> p a c q", a=2, c=2)
                rc3 = rec[r][:].rearrange("p (c q) -> p c q", c=2)
                tt("dve", rc3, od[:, 1, :, :], es_bc[:, g, :, :], ALU.add, [pb[bo], t_const], [T(f"rec{r}")])

            def swa_L(u):
                r = u["r"]
                trc = T(f"rec{r}")
                rc3 = rec[r][:].rearrange("p (c q) -> p c q", c=2)
                act(rc3, rc3, AF.Ln, [trc], [trc])
                act(rc3, rc3, AF.Exp, [trc], [trc], scale=-1.0)

            def swa_N(u):
                s, bb, g, r = u["s"], u["bb"], u["g"], u["r"]
                si = s.si
                rc3 = rec[r][:].rearrange("p (c q) -> p c q", c=2)
                if u["two"]:
                    bo = 4 * si + 2
                    od = ps[:, bo, :].rearrange("p (a c q) -> p a c q", a=2, c=2)
                    tt("dve", attnT[si][:, 2 * g:2 * g + 2, bb * 128:(bb + 1) * 128], od[:, 0, :, :], rc3, ALU.mult,
                       [pb[bo], T(f"rec{r}")], [t_at[si][bb]])
                    return
                os3 = osb[r][:].rearrange("p (c q) -> p c q", c=2)
                tt("dve", attnT[si][:, 2 * g:2 * g + 2, bb * 128:(bb + 1) * 128], os3, rc3, ALU.mult,
                   [T(f"osb{r}"), T(f"rec{r}")], [t_at[si][bb]])

            def gmlp(s, bb):
                si = s.si
                bm = 4 * si + 3
                mx = ps[:, bm, :].rearrange("p (c t) -> p c t", c=4)
                for c in range(4):
                    for par in range(2):
                        rows = slice(par * 64, (par + 1) * 64)
                        hd = 2 * c + par
                        mm(mx[rows, c, :], gvn[si][:, bb, hd * 64:(hd + 1) * 64], wtb[:, hd, :], True, True,
                           [t_gvn[si][bb], t_const], [pb[bm]], signal=(c == 3 and par == 1), notick=True)
                t_mxt = T("mxt")
                tt("dve", mxt[:], mx, bias_bc[:], ALU.add, [pb[bm], T("bias_bc")], [t_mxt])
                tt(GM_ENG, gmT[si][:, :, bb * 128:(bb + 1) * 128], mxt[:], guT[si][:, :, bb * 128:(bb + 1) * 128],
                   ALU.mult, [t_mxt] + t_gu[si], [t_gm[si][bb]])

            def mixer_core(subs):
                units = []
                maxnb = max(s.nb for s in subs)
                for bb in range(maxnb):
                    for g in range(2):
                        for s in subs:
                            if bb < s.nb:
                                units.append((s, bb, g))
                U = [{"s": s, "bb": bb, "g": g, "two": len(subs) == 2} for (s, bb, g) in units]
                n = len(U)

                def at(k):
                    return U[k] if 0 <= k < n else None

                for it in range(n + 3):
                    uS, uE, uP, uL = at(it), at(it - 1), at(it - 2), at(it - 3)
                    if uP is not None:
                        swa_PV(uP)
                    if uE is not None:
                        swa_E(uE)
                    if uS is not None:
                        swa_S(uS)
                    if uL is not None:
                        swa_L(uL)
                    if uP is not None:
                        swa_C(uP)
                    if uE is not None:
                        swa_M(uE)
                    if uP is not None:
                        swa_A(uP)
                    if uL is not None:
                        swa_N(uL)
                    if uL is not None and uL["g"] == 1:
                        gmlp(uL["s"], uL["bb"])
                last = subs[-1]
                ls = last.slot0 + last.nb - 1
                cp("pool", KD[:, :, 0:128], KD[:, :, ls * 128:(ls + 1) * 128], [t_kd[ls]], [t_kd[0]])
                cp("pool", VV[:, 0, :], VV[:, ls, :], [t_vv[ls]], [t_vv[0]])
                for s in subs:
                    si, nt = s.si, s.nt
                    act(sq[si][:, 0:4, 0:nt], attnT[si][:, :, 0:nt], AF.Square, t_at[si][:s.nb], t_sq[si][0:4])
                    act(sq[si][:, 4:8, 0:nt], gmT[si][:, :, 0:nt], AF.Square, t_gm[si][:s.nb], t_sq[si][4:8])
                    ba = aux_bank()
                    for c in range(4):
                        mm(ps[:, ba, 0:nt], ones[:], sq[si][:, c, 0:nt], c == 0, c == 3, [t_const, t_sq[si][c]], [pb[ba]])
                    bg = aux_bank()
                    for c in range(4):
                        mm(ps[:, bg, 0:nt], ones[:], sq[si][:, 4 + c, 0:nt], c == 0, c == 3, [t_const, t_sq[si][4 + c]],
                           [pb[bg]])
                    act(lnt[si][:, 0:nt], ps[:, ba, 0:nt], AF.Ln, [pb[ba], t_csta], [t_ln[si]], scale=1.0 / 512,
                        bias=ca("eps"))
                    act(rstd[si][:, 0:nt], lnt[si][:, 0:nt], AF.Exp, [t_ln[si]], [t_rs[si]], scale=-0.5)
                    act(lnt[si][:, 0:nt], ps[:, bg, 0:nt], AF.Ln, [pb[bg], t_csta, t_rs[si]], [t_ln[si]], scale=1.0 / 512,
                        bias=ca("eps"))
                    act(rstd2[si][:, 0:nt], lnt[si][:, 0:nt], AF.Exp, [t_ln[si]], [t_rs2[si]], scale=-0.5)
                    for c in range(4):
                        stt("dve", yT[si][:, c, 0:nt], attnT[si][:, c, 0:nt], ca("ga", c), rstd[si][:, 0:nt], ALU.mult,
                            ALU.mult, t_at[si][:s.nb] + [t_rs[si], t_csta], [t_y[si][c]])
                        stt("dve", yT[si][:, 4 + c, 0:nt], gmT[si][:, c, 0:nt], ca("gmo", c), rstd2[si][:, 0:nt],
                            ALU.mult, ALU.mult, t_gm[si][:s.nb] + [t_rs2[si], t_csta], [t_y[si][4 + c]])

            def proj_resid(subs, pbase, post_norm):
                held = []
                for pi in range(4):
                    held.append(w_acquire(pbase + pi))
                for idx, s in enumerate(subs):
                    si, nt = s.si, s.nt
                    for pi in range(4):
                        j, wp, wt = held[pi]
                        w3 = wp.rearrange("p (k c) -> p k c", k=KC)
                        for jj in range(2):
                            oc = 2 * pi + jj
                            b = acc_bank()
                            for kc in range(KC):
                                mm(ps[:, b, 0:nt], w3[:, kc, jj * 128:(jj + 1) * 128], yT[si][:, kc, 0:nt], kc == 0,
                                   kc == KC - 1, [wt, t_y[si][kc]], [pb[b]])
                            tt("dve", xc(si, oc)[:, 0:nt], ps[:, b, 0:nt], xc(si, oc)[:, 0:nt], ALU.add,
                               [pb[b], t_x[si][oc]], [t_x[si][oc]])
                        if idx == len(subs) - 1:
                            w_release(j)
                    last = idx == len(subs) - 1
                    norm_full(si, nt, post_norm, lag=(4 if not last else (3 if (post_norm == "g2" and len(subs) == 2) else 0)))
                if not (post_norm == "g2" and len(subs) == 2):
                    flush_def()

            def xattn_q(subs):
                held = [w_acquire(P_WQ + pi) for pi in range(4)]
                for idx, s in enumerate(subs):
                    si, nt = s.si, s.nt
                    for hx in range(4):
                        j, wp, wt = held[hx]
                        w3 = wp.rearrange("p (k c) -> p k c", k=KC)
                        bks = []
                        for dc in range(2):
                            b = nxt("xq", 6)
                            bks.append(b)
                            for kc in range(KC):
                                mm(ps[:, b, 0:nt], w3[:, kc, dc * 128:(dc + 1) * 128], hT[si][:, kc, 2:2 + nt], kc == 0,
                                   kc == KC - 1, [wt, t_h[si][kc]], [pb[b]])
                            act(sq[si][:, 2 * hx + dc, 0:nt], ps[:, b, 0:nt], AF.Square, [pb[b]], [t_sq[si][2 * hx + dc]])
                        if idx == len(subs) - 1:
                            w_release(j)
                        defer(2, (lambda s=s, hx=hx, bks=bks: xq_tail(s, hx, bks)))
                flush_def()

            def xq_tail(s, hx, bks):
                si, nt = s.si, s.nt
                bs_ = 6 + nxt("xqs", 2)
                for dc in range(2):
                    mm(ps[:, bs_, 0:nt], ones[:], sq[si][:, 2 * hx + dc, 0:nt], dc == 0, dc == 1,
                       [t_const, t_sq[si][2 * hx + dc]], [pb[bs_]], notick=True)
                r = nxt("r2", NR)
                act(lnc[r][:, 0:nt], ps[:, bs_, 0:nt], AF.Ln, [pb[bs_], t_csta], [T(f"lnc{r}")], scale=1.0 / 256,
                    bias=ca("eps"))
                act(rsc[r][:, 0:nt], lnc[r][:, 0:nt], AF.Exp, [T(f"lnc{r}")], [T(f"rsc{r}")], scale=-0.5)
                for dc in range(2):
                    stt("dve", yT[si][:, 2 * hx + dc, 0:nt], ps[:, bks[dc], 0:nt], ca("gxq", dc), rsc[r][:, 0:nt],
                        ALU.mult, ALU.mult, [pb[bks[dc]], T(f"rsc{r}"), t_csta], [t_y[si][2 * hx + dc]])

            def xattn_scores(s, hx, k):
                si, nt = s.si, s.nt
                bk = (2 * k, 2 * k + 1)
                for mb in range(2):
                    for dc in range(2):
                        mm(ps[:, bk[mb], 0:nt], kmT[:, 2 * hx + dc, mb * 128:(mb + 1) * 128], yT[si][:, 2 * hx + dc, 0:nt],
                           dc == 0, dc == 1, [t_km, t_y[si][2 * hx + dc]], [pb[bk[mb]]])
                p = nxt("ptx", 3)
                tp = T(f"PTx{p}")
                act(PTx[p][:, :, 0:nt], ps[:, bk[0]:bk[0] + 2, 0:nt], AF.Exp, [pb[bk[0]], pb[bk[1]]], [tp], scale=1.0 / 16)
                return p

            def xattn_pv(s, hx, p):
                si, nt = s.si, s.nt
                tp = T(f"PTx{p}")
                for mb in range(2):
                    mm(ps[:, 6, 0:nt], ones[:], PTx[p][:, mb, 0:nt], mb == 0, mb == 1, [t_const, tp], [pb[6]])
                trx = T(f"recx{p}")
                act(recx[p][:, 0:nt], ps[:, 6, 0:nt], AF.Ln, [pb[6]], [trx])
                act(recx[p][:, 0:nt], recx[p][:, 0:nt], AF.Exp, [trx], [trx], scale=-1.0)
                for dc in range(2):
                    bo = 4 + dc
                    for mb in range(2):
                        mm(ps[:, bo, 0:nt], vm[:, mb, hx * 256 + dc * 128:hx * 256 + (dc + 1) * 128], PTx[p][:, mb, 0:nt],
                           mb == 0, mb == 1, [t_vm, tp], [pb[bo]])
                    tt("dve", yT[si][:, 2 * hx + dc, 0:nt], ps[:, bo, 0:nt], recx[p][:, 0:nt], ALU.mult, [pb[bo], trx],
                       [t_y[si][2 * hx + dc]])

            def xattn_core(subs):
                units = []
                for hx in range(4):
                    for s in subs:
                        units.append((s, hx))
                pend = []
                for k, (s, hx) in enumerate(units):
                    p = xattn_scores(s, hx, k % 2)
                    pend.append((s, hx, p))
                    if len(pend) > ATT_LAG:
                        xattn_pv(*pend.pop(0))
                for u in pend:
                    xattn_pv(*u)

            def ffn(subs, next_subs):
                cwl, _ = CA["cw"]
                cbl, _ = CA["cb"]
                assert len(subs) == 2 and subs[0].nt == subs[1].nt
                nt = subs[0].nt
                for jp in range(NJ):
                    P.tag = f"ffn_up{jp:02d}"
                    j, wp, wt = w_acquire(P_UP + jp)
                    w3 = wp.rearrange("p (k c) -> p k c", k=KC)
                    v = nxt("cv", 2)
                    tcvs = [T(f"cvb{v}_0"), T(f"cvb{v}_1")]
                    prs = [nxt("fp", 4), nxt("fp", 4)]
                    order = [(wh, pos) for wh in range(2) for pos in range(2)]
                    if jp == 0:
                        order = [(wh, pos) for pos in range(2) for wh in range(2)]
                    for (wh, pos) in order:
                        s = subs[pos]
                        b = 2 * prs[wh] + pos
                        for kc in range(KC):
                            mm(ps[:, b, 0:nt + 2], w3[:, kc, wh * 128:(wh + 1) * 128], hT[s.si][:, kc, 0:nt + 2],
                               kc == 0, kc == KC - 1, [wt, t_h[s.si][kc]], [pb[b]])
                    w_release(j)
                    for wh in range(2):
                        m = 2 * jp + wh
                        pr = prs[wh]
                        pbs = [pb[2 * pr], pb[2 * pr + 1]]
                        w0 = csta[:, cwl + 3 * m:cwl + 3 * m + 1]
                        w1 = csta[:, cwl + 3 * m + 1:cwl + 3 * m + 2]
                        w2 = csta[:, cwl + 3 * m + 2:cwl + 3 * m + 3]
                        bb_ = csta[:, cbl + m:cbl + m + 1]
                        cv = cvb[v][:, wh, :, 0:nt]
                        tcv = tcvs[wh]
                        act(cv, ps[:, 2 * pr:2 * pr + 2, 2:nt + 2], AF.Identity, pbs + [t_csta], [tcv], scale=w2, bias=bb_)
                        stt("dve", cv, ps[:, 2 * pr:2 * pr + 2, 1:nt + 1], w1, cv, ALU.mult, ALU.add,
                            pbs + [tcv, t_csta], [tcv])
                        stt("dve", cv, ps[:, 2 * pr:2 * pr + 2, 0:nt], w0, cv, ALU.mult, ALU.add, pbs + [tcv, t_csta],
                            [tcv])
                    act(cvb[v][:, 0, :, 0:nt], cvb[v][:, 0, :, 0:nt], AF.Gelu_apprx_tanh, [tcvs[0]], [tcvs[0]])
                    tt("dve", gT_all[:, :, jp, 0:nt], cvb[v][:, 0, :, 0:nt], cvb[v][:, 1, :, 0:nt], ALU.mult, tcvs,
                       [t_gT[0][jp], t_gT[1][jp]])
                def dn_mm(oc, pos, s, b, helds, kcs):
                    for kc in kcs:
                        j, wp, wt = helds[kc // 11]
                        w3 = wp[:, 0:11 * 128].rearrange("p (k c) -> p k c", k=11)
                        mm(ps[:, b, 0:s.nt], w3[:, kc % 11, :], gT_all[:, pos, kc, 0:s.nt], kc == 0, kc == NJ - 1,
                           [wt, t_gT[pos][kc]], [pb[b]])

                def dn_evac(oc, pos, s, b):
                    si, nt = s.si, s.nt
                    tt("dve", xc(si, oc)[:, 0:nt], ps[:, b, 0:nt], xc(si, oc)[:, 0:nt], ALU.add, [pb[b], t_x[si][oc]],
                       [t_x[si][oc]])
                    c0 = KC * s.tok0 + oc * nt
                    dma("sp", dr["om"][:, c0:c0 + nt], xc(si, oc)[:, 0:nt], [t_x[si][oc]], [T(f"om{s.idx}_{oc}")],
                        f"xs{si}_{oc}")
                    if next_subs is not None:
                        ns_ = next_subs[pos]
                        assert ns_.si == si
                        n0 = KC * ns_.tok0 + oc * nt
                        slot = (oc + XR[si] - 1) % (KC + 1)
                        dma("pool", xT[si][:, slot, 0:nt], dr["xm"][:, n0:n0 + nt], [], [t_xp[si][slot]],
                            f"xp{si}_{oc}")
                        defer(2, (lambda ns_=ns_, oc=oc, pos=pos, slot=slot: norm1_chunk(ns_, oc, 6 + pos, slot)))

                P.tag = "ffn_dn0"
                h01 = {oc: [w_acquire(P_DOWN + 2 * oc), w_acquire(P_DOWN + 2 * oc + 1)] for oc in (0, 1)}
                bk01 = {}
                for oc in (0, 1):
                    for pos, s in enumerate(subs):
                        bk01[(oc, pos)] = acc_bank()
                        dn_mm(oc, pos, s, bk01[(oc, pos)], h01[oc], range(0, NJ - 1))
                for oc in (0, 1):
                    for pos, s in enumerate(subs):
                        dn_mm(oc, pos, s, bk01[(oc, pos)], h01[oc], [NJ - 1])
                        dn_evac(oc, pos, s, bk01[(oc, pos)])
                    w_release(h01[oc][0][0])
                    w_release(h01[oc][1][0])
                for oc in range(2, KC):
                    P.tag = f"ffn_dn{oc}"
                    helds = [w_acquire(P_DOWN + 2 * oc), w_acquire(P_DOWN + 2 * oc + 1)]
                    for pos, s in enumerate(subs):
                        b = acc_bank()
                        dn_mm(oc, pos, s, b, helds, range(NJ))
                        dn_evac(oc, pos, s, b)
                    w_release(helds[0][0])
                    w_release(helds[1][0])
                flush_def()

            t_hist = T("hist")

            def hist_save(last, use_flag):
                src = hT[last.si][:, :, last.nt:last.nt + 2]
                if use_flag:
                    ts("pool", hist[:], src, ca("flag"), None, ALU.mult, None, t_h[last.si] + [t_csta], [t_hist])
                else:
                    cp("pool", hist[:], src, t_h[last.si], [t_hist])

            def hist_load(cur, prev):
                dst = hT[cur.si][:, :, 0:2]
                if prev is None:
                    cp("pool", dst, hist[:], [t_hist], t_h[cur.si])
                else:
                    cp("pool", dst, hT[prev.si][:, :, prev.nt:prev.nt + 2], t_h[prev.si], t_h[cur.si])

            halo = Sub(0, 2, True, False, 0, 0, 1, 0)
            subs_all = []
            for k in range(n_main_sub):
                si = (k + 1) % 2
                subs_all.append(Sub(si, NBM, False, k == 0, k * NTM, HALO + k * NTM, 0, k + 1))
            tiles = [[halo]]
            for k in range(0, n_main_sub, 2):
                tiles.append(subs_all[k:k + 2])

            for ti, subs in enumerate(tiles):
                slot = 1
                for s in subs:
                    s.slot0 = slot
                    slot += s.nb
                prefetched = ti >= 2
                P.tag = "rope_norm1"
                if not prefetched:
                    for s in subs:
                        load_sub(s)
                    ckpt(f"t{ti}_load")
                    for s in subs:
                        rope_tables(s)
                for pos, s in enumerate(subs):
                    if prefetched:
                        norm_tail_ops(s.si, s.nt, "g1", 6 + pos)
                    else:
                        norm_full(s.si, s.nt, "g1")
                ckpt(f"t{ti}_norm")
                P.tag = "mixer_proj"
                mixer_proj(subs)
                ckpt(f"t{ti}_proj")
                P.tag = "mixer_core"
                mixer_core(subs)
                ckpt(f"t{ti}_core")
                P.tag = "proj_resid"
                proj_resid(subs, P_WOUT, "g2")
                ckpt(f"t{ti}_wout")
                if ti == 0:
                    P.tag = "mem_kv"
                    mem_kv()
                P.tag = "xattn_q"
                xattn_q(subs)
                ckpt(f"t{ti}_xq")
                P.tag = "xattn_core"
                xattn_core(subs)
                ckpt(f"t{ti}_xc")
                P.tag = "proj_resid_wo"
                if not subs[0].halo:
                    hist_load(subs[0], None)
                proj_resid(subs, P_WO, "g3")
                ckpt(f"t{ti}_wo")
                if subs[0].halo:
                    hist_save(subs[0], True)
                    continue
                prev = subs[0]
                for s in subs[1:]:
                    hist_load(s, prev)
                    prev = s
                hist_save(subs[-1], False)
                next_subs = tiles[ti + 1] if (ti + 1 < len(tiles) and ti >= 1) else None
                if next_subs is not None:
                    P.tag = "rope_norm1"
                    for s in next_subs:
                        pos_load(s)
                        rope_tables(s)
                P.tag = "ffn"
                ffn(subs, next_subs)
                if next_subs is not None:
                    for s in subs:
                        XR[s.si] = (XR[s.si] - 1) % (KC + 1)
                ckpt(f"t{ti}_ffn")

        stopped = False
        try:
            emit_all()
        except _Stop:
            stopped = True
        if not stopped:
            assert W["next_acq"] == len(seq), (W["next_acq"], len(seq))
        final_waits = sorted(P.dmacnt.items())

        sems = {}
        for k in ["pe", "act", "dve", "pool"] + sorted(P.dmacnt.keys()):
            sems[k] = es.enter_context(nc.semaphore(k))
        _CACHE["sbuf_left"] = nc.sbuf_bytes_remaining
        block = es.enter_context(nc.Block())

        def replay(en, h, final=False):
            for waits, fn, inc in P.eng[en].ops:
                for k, v in waits:
                    h.wait_ge(sems[k], v)
                ins = fn(h)
                if inc is not None:
                    ins.then_inc(sems[inc[0]], inc[1])
            if final:
                for k, v in final_waits:
                    h.wait_ge(sems[k], v)

        @block.tensor
        def _(h):
            replay("pe", h)

        @block.scalar
        def _(h):
            replay("act", h)

        @block.vector
        def _(h):
            replay("dve", h)

        @block.gpsimd
        def _(h):
            replay("pool", h)

        @block.sync
        def _(h):
            replay("sp", h, final=True)

    counts = {k: len(v.ops) for k, v in P.eng.items()}
    _CACHE["tags"] = P.tags
    return nc, counts


def _piece_k1024(Wm, cols):
    K, N = Wm.shape
    w = Wm.reshape(KC, 128, N)[:, :, cols]
    return np.ascontiguousarray(w.transpose(1, 0, 2)).reshape(128, -1)


def _build_wall(inp):
    wall = np.zeros((NPIECE, 128, PW), np.float32)
    wkv = inp["xa_wkv"][0]
    for hx in range(4):
        wall[P_WK + hx] = _piece_k1024(wkv, np.arange(hx * 256, (hx + 1) * 256))
        wall[P_WV + hx] = _piece_k1024(wkv, 1024 + np.arange(hx * 256, (hx + 1) * 256))
    w_in = inp["w_in"][0]
    order = np.concatenate([np.arange(0, 512), np.arange(512, 640), np.arange(768, 1280), np.arange(640, 768),
                            np.arange(1280, 1792)])
    for pi in range(7):
        wall[P_WIN + pi] = _piece_k1024(w_in, order[pi * 256:(pi + 1) * 256])
    for pi in range(4):
        cols = np.arange(pi * 256, (pi + 1) * 256)
        wall[P_WOUT + pi] = _piece_k1024(inp["w_out"][0], cols)
        wall[P_WQ + pi] = _piece_k1024(inp["xa_wq"][0], cols)
        wall[P_WO + pi] = _piece_k1024(inp["xa_wo"][0], cols)
    up = inp["ffn_up"][0]
    for j in range(NJ):
        cols = np.concatenate([np.arange(j * 128, (j + 1) * 128), DFF + np.arange(j * 128, (j + 1) * 128)])
        wall[P_UP + j] = _piece_k1024(up, cols)
    dn = inp["ffn_down"][0].reshape(NJ, 128, D)
    for oc in range(KC):
        for half in range(2):
            w = dn[half * 11:(half + 1) * 11, :, oc * 128:(oc + 1) * 128]
            wall[P_DOWN + 2 * oc + half, :, 0:11 * 128] = w.transpose(1, 0, 2).reshape(128, -1)
    return wall


def _cols(v, n):
    return np.ascontiguousarray(np.asarray(v, np.float32).reshape(n, 128).T)


def _build_csta(inp, flag):
    c = np.zeros((128, NCA), np.float32)

    def put(name, arr):
        lo, hi = CA[name]
        c[:, lo:hi] = arr

    put("g1", _cols(inp["mix_norm"][0], 8))
    put("g2", _cols(inp["xa_norm"][0], 8))
    put("g3", _cols(inp["ffn_norm"][0], 8))
    put("gmem", _cols(inp["mem_norm"][0], 8))
    p = np.arange(128)
    put("gq", inp["q_norm"][0][p % 64][:, None])
    put("gk", inp["k_norm"][0][p % 64][:, None])
    put("gqp", inp["q_norm"][0][(p % 64 + 32) % 64][:, None])
    put("gkp", inp["k_norm"][0][(p % 64 + 32) % 64][:, None])
    inv_freq = (1.0 / (10000.0 ** (np.arange(32, dtype=np.float32) * np.float32(2.0 / 64)))).astype(np.float32)
    put("invf", inv_freq[(p % 64) % 32][:, None])
    put("flag", np.full((128, 1), flag, np.float32))
    put("sel0", (p == 0).astype(np.float32)[:, None])
    put("sel1", (p == 1).astype(np.float32)[:, None])
    put("ga", _cols(inp["attn_out_norm"][0], 4))
    put("gmo", _cols(inp["gmlp_out_norm"][0], 4))
    put("gxq", _cols(inp["xa_q_norm"][0], 2))
    put("gxk", _cols(inp["xa_k_norm"][0], 2))
    conv = inp["ffn_conv"][0]
    cbv = inp["ffn_conv_b"][0]
    cw = np.zeros((128, 44, 3), np.float32)
    cbm = np.zeros((128, 44), np.float32)
    for j in range(NJ):
        for wh in range(2):
            ch = wh * DFF + j * 128 + p
            cw[:, 2 * j + wh, :] = conv[:, ch].T
            cbm[:, 2 * j + wh] = cbv[ch]
    put("cw", cw.reshape(128, -1))
    put("cb", cbm)
    put("halfpi", np.full((128, 1), math.pi / 2, np.float32))
    put("eps", np.full((128, 1), EPS, np.float32))
    put("gvn", np.broadcast_to(inp["gmlp_v_norm"][0][None, :], (128, 512)))
    return c


def _build_cstb(inp):
    c = np.zeros((128, NCB), np.float32)

    def put(name, arr):
        lo, hi = CB[name]
        c[:, lo:hi] = arr

    put("sinkr", np.broadcast_to(inp["attn_sinks"][0].reshape(1, -1), (128, 8)))
    s = np.arange(128)[:, None]
    t = np.arange(128)[None, :]
    put("tril", (s <= t).astype(np.float32))
    rm = np.zeros((128, 128), np.float32)
    for hh in range(2):
        for mm_ in range(64):
            if mm_ < 32:
                rm[hh * 64 + mm_ + 32, hh * 64 + mm_] = -1.0
            else:
                rm[hh * 64 + mm_ - 32, hh * 64 + mm_] = 1.0
    put("rm", rm)
    bo = np.zeros((128, 128), np.float32)
    bo[0:64, 0:64] = 1.0
    bo[64:128, 64:128] = 1.0
    put("bones", bo)
    mprev = (s > t).astype(np.float32)
    mcur = (s <= t).astype(np.float32)
    m = np.zeros((128, 2, 2, 2, 128), np.float32)
    m[:, :, 0, :, :] = mprev[:, None, None, :]
    m[:, :, 1, :, :] = mcur[:, None, None, :]
    ws = inp["gmlp_ws"][0]
    wtf = np.ascontiguousarray(ws.transpose(2, 0, 1)).reshape(128, -1)
    bs = inp["gmlp_bs"][0]
    bsr = np.zeros((128, 4, 128), np.float32)
    for c_ in range(4):
        bsr[0:64, c_, :] = bs[2 * c_][None, :]
        bsr[64:128, c_, :] = bs[2 * c_ + 1][None, :]
    bsr = bsr.reshape(128, 512)
    return c, m.reshape(128, -1), wtf, bsr


def _xT_blocks(xrows, nt):
    Tn = xrows.shape[0]
    a = xrows.reshape(Tn // nt, nt, KC, 128)
    a = a.transpose(3, 0, 2, 1)
    return np.ascontiguousarray(a).reshape(128, -1)


def _prepare_inputs(inp, n_main_sub=NSUB_MAIN):
    inp = {k: np.asarray(v) for k, v in inp.items()}
    wall = _build_wall(inp)
    cstb, c_mask, c_wtf, c_bsr = _build_cstb(inp)
    x = inp["x"]
    pos = inp["positions"]
    maps = []
    for c in range(NCORES):
        b, half = c // 2, c % 2
        s0 = half * TOK_CORE
        if half == 0:
            xh_rows = np.zeros((HALO, D), np.float32)
            ph = np.zeros((HALO,), np.int32)
        else:
            xh_rows = x[b, s0 - HALO:s0]
            ph = pos[b, s0 - HALO:s0]
        xm_rows = x[b, s0:s0 + TOK_CORE]
        posr = np.concatenate([ph, pos[b, s0:s0 + TOK_CORE]]).astype(np.int32)
        maps.append({
            "xh": _xT_blocks(xh_rows, HALO),
            "xm": _xT_blocks(xm_rows, NTM),
            "posr": np.ascontiguousarray(np.broadcast_to(posr[None, :], (128, HALO + TOK_CORE))),
            "memT": _xT_blocks(inp["mem"][b], 256),
            "wall": wall,
            "csta": _build_csta(inp, 1.0 if half == 1 else 0.0),
            "cstb": cstb,
            "c_mask": c_mask,
            "c_wtf": c_wtf,
            "c_bsr": c_bsr,
        })
    return maps


def _assemble(results):
    out = np.zeros((BATCH, SEQ, D), np.float32)
    for c in range(NCORES):
        b, half = c // 2, c % 2
        om = np.asarray(results[c]["om"]).reshape(128, NSUB_MAIN, KC, NTM)
        rows = om.transpose(1, 3, 2, 0).reshape(TOK_CORE, D)
        out[b, half * TOK_CORE:(half + 1) * TOK_CORE] = rows
    return out


def kernel(**inputs):
    if "nc" not in _CACHE:
        _CACHE["nc"] = build_program()[0]
    nc = _CACHE["nc"]
    maps = _prepare_inputs(inputs)
    res = run_bass_kernel_spmd(nc, maps, core_ids=list(range(NCORES)))
    return _assemble(res.results)
```
